# Optimizing a Trainium2 kernel written in Bass

```python
import math
import jax
import jax.numpy as jnp
from jax import lax
import numpy as np

D_MODEL = 2048
BATCH = 1
SEQ = 8192
DEPTH = 4

GRID_W = 64
CTX_LEN = 256

F32 = jnp.float32
N_MOD = 6
RMS_EPS = 1e-6

DIFF_HEADS = D_MODEL // 256
DIFF_QK_DIM = 64
DIFF_V_DIM = 2 * DIFF_QK_DIM
DIFF_QK_W = 2 * DIFF_HEADS * DIFF_QK_DIM
DIFF_V_W = DIFF_HEADS * DIFF_V_DIM
Q_BLOCK = 128
ROPE_BASE = 10000.0
ROPE_AXIS_DIM = DIFF_QK_DIM // 2

S5_WIDTH = D_MODEL // 2
S5_P = 16
S5_GROUPS = S5_WIDTH // S5_P
S5_STATE = 64

EVEN_IN_W = 2 * DIFF_QK_W + DIFF_V_W + S5_WIDTH
EVEN_OUT_W = DIFF_V_W + S5_WIDTH

SSD_INNER = 2 * D_MODEL
SSD_HEAD_DIM = 64
SSD_HEADS = SSD_INNER // SSD_HEAD_DIM
SSD_GROUPS = 8
SSD_HPG = SSD_HEADS // SSD_GROUPS
SSD_STATE = 128
SSD_CONV = 5
SSD_CHUNK = 128
SSD_CONV_CH = SSD_INNER + 2 * SSD_GROUPS * SSD_STATE
ODD_IN_W = SSD_INNER + SSD_CONV_CH + 2 * SSD_HEADS

MLP_HIDDEN = 4 * D_MODEL

kernel_name = 'hybrid_diffattn_s5_ssd_prefix_dit'


def _rms(x, g):
    xf = x.astype(F32)
    y = xf * lax.rsqrt(jnp.mean(xf * xf, axis=-1, keepdims=True) + RMS_EPS) * g.astype(F32)
    return y.astype(x.dtype)


def _modulate(x, shift, scale):
    return x * (1 + scale) + shift


def _flip(t, on):
    return jnp.flip(t, axis=1) if on else t


def _axial_rope_tables(seq_len):
    rows = seq_len // GRID_W
    row = jnp.repeat(jnp.arange(rows, dtype=F32), GRID_W)
    col = jnp.tile(jnp.arange(GRID_W, dtype=F32), rows)
    inv = ROPE_BASE ** (-jnp.arange(0, ROPE_AXIS_DIM, 2, dtype=F32) / ROPE_AXIS_DIM)
    ang_r = row[:, None] * inv
    ang_c = col[:, None] * inv
    ang = jnp.concatenate([ang_r, ang_r, ang_c, ang_c], axis=-1)
    return jnp.cos(ang), jnp.sin(ang)


def _rope(x, cos, sin):
    xr1, xr2, xc1, xc2 = jnp.split(x, 4, axis=-1)
    rot = jnp.concatenate([-xr2, xr1, -xc2, xc1], axis=-1)
    cb = cos[None, :, None, None, :]
    sb = sin[None, :, None, None, :]
    return (x * cb + rot * sb).astype(x.dtype)


def _diff_attn(q, k, v, lam):
    s = jnp.einsum('bqmhd,bkmhd->bmhqk', q, k).astype(F32) * (DIFF_QK_DIM ** -0.5)
    p = jax.nn.softmax(s, axis=-1)
    w = p[:, 0] - lam * p[:, 1]
    return jnp.einsum('bhqk,bkhe->bqhe', w.astype(v.dtype), v)


def _diff_head_out(o, subln_g, lam_init):
    bsz, l = o.shape[:2]
    return (_rms(o, subln_g) * (1.0 - lam_init)).reshape(bsz, l, DIFF_V_W)


def _split_even(p):
    bsz, l = p.shape[:2]
    q = p[..., :DIFF_QK_W].reshape(bsz, l, 2, DIFF_HEADS, DIFF_QK_DIM)
    k = p[..., DIFF_QK_W:2 * DIFF_QK_W].reshape(bsz, l, 2, DIFF_HEADS, DIFF_QK_DIM)
    v = p[..., 2 * DIFF_QK_W:2 * DIFF_QK_W + DIFF_V_W].reshape(bsz, l, DIFF_HEADS, DIFF_V_DIM)
    s = p[..., 2 * DIFF_QK_W + DIFF_V_W:]
    return q, k, v, s


def _s5_discretize(lam_re, lam_im, log_dt, b_re, b_im):
    lam = lax.complex(lam_re.astype(F32), lam_im.astype(F32))
    dt = jnp.exp(log_dt.astype(F32))[:, None]
    lam_bar = jnp.exp(lam * dt)
    b = lax.complex(b_re.astype(F32), b_im.astype(F32))
    b_bar = ((lam_bar - 1) / lam)[..., None] * b
    return lam_bar, b_bar


def _s5_scan(bu, lam_bar, h0, reverse):
    if h0 is not None:
        bu = bu.at[:, -1 if reverse else 0].add(lam_bar * h0)
    a = jnp.broadcast_to(lam_bar, bu.shape)

    def combine(e1, e2):
        a1, b1 = e1
        a2, b2 = e2
        return a1 * a2, a2 * b1 + b2

    _, h = lax.associative_scan(combine, (a, bu), reverse=reverse, axis=1)
    return h


def _s5(us, usc, lam_re, lam_im, log_dt, b_re, b_im, c_re, c_im, d_skip, w_glu, b_glu, ctx_out):
    bsz, l = us.shape[:2]
    lc = usc.shape[1]
    u = us.astype(F32).reshape(bsz, l, S5_GROUPS, S5_P)
    uc = usc.astype(F32).reshape(bsz, lc, S5_GROUPS, S5_P)
    dsk = d_skip.astype(F32)
    y = u * dsk
    yc = uc * dsk
    for d, rev in enumerate((False, True)):
        lam_bar, b_bar = _s5_discretize(lam_re[d], lam_im[d], log_dt[d], b_re[d], b_im[d])
        c_mat = lax.complex(c_re[d].astype(F32), c_im[d].astype(F32))
        h_ctx = _s5_scan(jnp.einsum('bsgp,gnp->bsgn', uc.astype(jnp.complex64), b_bar), lam_bar, None, rev)
        h0 = h_ctx[:, 0] if rev else h_ctx[:, -1]
        h = _s5_scan(jnp.einsum('bsgp,gnp->bsgn', u.astype(jnp.complex64), b_bar), lam_bar, h0, rev)
        y = y + jnp.real(jnp.einsum('bsgn,gpn->bsgp', h, c_mat))
        if ctx_out:
            yc = yc + jnp.real(jnp.einsum('bsgn,gpn->bsgp', h_ctx, c_mat))

    def glu(yv, length):
        g = jax.nn.gelu(yv.reshape(bsz, length, S5_WIDTH))
        return (g * jax.nn.sigmoid(g @ w_glu.astype(F32) + b_glu.astype(F32))).astype(us.dtype)

    return glu(y, l), (glu(yc, lc) if ctx_out else None)


def _even_mixer(u, uc, w_in, w_out, lam_p, subln_g, s5_lam_re, s5_lam_im, s5_log_dt, s5_b_re, s5_b_im,
                s5_c_re, s5_c_im, s5_d, s5_w_glu, s5_b_glu, cos, sin, lam_init, ctx_out):
    bsz, l = u.shape[:2]
    q, k, v, us = _split_even(u @ w_in)
    qc, kc, vc, usc = _split_even(uc @ w_in)
    q = _rope(q, cos, sin)
    k = _rope(k, cos, sin)
    lp = lam_p.astype(F32)
    lam = jnp.exp(jnp.sum(lp[0] * lp[1])) - jnp.exp(jnp.sum(lp[2] * lp[3])) + lam_init
    k_all = jnp.concatenate([kc, k], axis=1)
    v_all = jnp.concatenate([vc, v], axis=1)
    nb = l // Q_BLOCK
    qb = jnp.moveaxis(q.reshape(bsz, nb, Q_BLOCK, 2, DIFF_HEADS, DIFF_QK_DIM), 1, 0)
    o = lax.map(lambda blk: _diff_attn(blk, k_all, v_all, lam), qb)
    o = jnp.moveaxis(o, 0, 1).reshape(bsz, l, DIFF_HEADS, DIFF_V_DIM)
    s, sc = _s5(us, usc, s5_lam_re, s5_lam_im, s5_log_dt, s5_b_re, s5_b_im, s5_c_re, s5_c_im,
                s5_d, s5_w_glu, s5_b_glu, ctx_out)
    y = jnp.concatenate([_diff_head_out(o, subln_g, lam_init), s], axis=-1) @ w_out
    yc = None
    if ctx_out:
        oc = _diff_attn(qc, kc, vc, lam)
        yc = jnp.concatenate([_diff_head_out(oc, subln_g, lam_init), sc], axis=-1) @ w_out
    return y, yc


def _dwconv(x, w, b):
    ch = x.shape[-1]
    y = lax.conv_general_dilated(x, w[:, None, :], window_strides=(1,),
                                 padding=[(SSD_CONV // 2, SSD_CONV // 2)],
                                 dimension_numbers=('NWC', 'WIO', 'NWC'),
                                 feature_group_count=ch)
    return y + b


def _segsum(a):
    t = a.shape[-1]
    cs = jnp.cumsum(a, axis=-1)
    diff = cs[..., :, None] - cs[..., None, :]
    return jnp.where(jnp.tril(jnp.ones((t, t), dtype=bool)), diff, -jnp.inf)


def _ssd(x, a, bm, cm, h0):
    b, l, g, r, p = x.shape
    n = bm.shape[-1]
    nc = l // SSD_CHUNK
    x = x.reshape(b, nc, SSD_CHUNK, g, r, p)
    bm = bm.reshape(b, nc, SSD_CHUNK, g, n)
    cm = cm.reshape(b, nc, SSD_CHUNK, g, n)
    a = jnp.moveaxis(a.reshape(b, nc, SSD_CHUNK, g, r), (3, 4), (1, 2))
    a_cs = jnp.cumsum(a, axis=-1)
    cb = jnp.einsum('bclgn,bcsgn->bgcls', cm, bm)
    m = cb[:, :, None] * jnp.exp(_segsum(a))
    y_diag = jnp.einsum('bgrcls,bcsgrp->bclgrp', m, x)
    decay_in = jnp.moveaxis(jnp.exp(a_cs[..., -1:] - a_cs), (1, 2), (3, 4))
    states = jnp.einsum('bclgn,bclgrp->bcgrpn', bm, x * decay_in[..., None])
    if h0 is None:
        h0 = jnp.zeros_like(states[:, 0])
    states = jnp.concatenate([h0[:, None], states], axis=1)
    chunk_decay = jnp.exp(_segsum(jnp.pad(a_cs[..., -1], ((0, 0), (0, 0), (0, 0), (1, 0)))))
    states = jnp.einsum('bgrzc,bcgrpn->bzgrpn', chunk_decay, states)
    decay_out = jnp.moveaxis(jnp.exp(a_cs), (1, 2), (3, 4))
    y_off = jnp.einsum('bclgn,bcgrpn->bclgrp', cm, states[:, :-1]) * decay_out[..., None]
    return (y_diag + y_off).reshape(b, l, g, r, p), states[:, -1]


def _ssd_mixer(u, uc, w_in, conv_w, conv_b, dt_bias, a_log, d_skip, norm_w, w_out, ctx_out):
    def prep(v):
        bsz, l = v.shape[:2]
        p = v @ w_in
        z = p[..., :SSD_INNER]
        xbc = jax.nn.silu(_dwconv(p[..., SSD_INNER:SSD_INNER + SSD_CONV_CH], conv_w, conv_b)).astype(F32)
        xs = xbc[..., :SSD_INNER].reshape(bsz, l, SSD_GROUPS, SSD_HPG, SSD_HEAD_DIM)
        bm = xbc[..., SSD_INNER:SSD_INNER + SSD_GROUPS * SSD_STATE].reshape(bsz, l, SSD_GROUPS, SSD_STATE)
        cm = xbc[..., SSD_INNER + SSD_GROUPS * SSD_STATE:].reshape(bsz, l, SSD_GROUPS, SSD_STATE)
        dt_raw = p[..., SSD_INNER + SSD_CONV_CH:].astype(F32).reshape(bsz, l, 2, SSD_GROUPS, SSD_HPG)
        return z, xs, bm, cm, dt_raw

    z, xs, bm, cm, dt_raw = prep(u)
    zc, xsc, bmc, cmc, dtc_raw = prep(uc)
    d_h = d_skip.astype(F32).reshape(SSD_GROUPS, SSD_HPG)[..., None]
    y = xs * d_h
    yc = xsc * d_h
    for d, rev in enumerate((False, True)):
        a_h = -jnp.exp(a_log[d].astype(F32)).reshape(SSD_GROUPS, SSD_HPG)
        bias = dt_bias[d].astype(F32).reshape(SSD_GROUPS, SSD_HPG)
        dt = jax.nn.softplus(dt_raw[:, :, d] + bias)
        dtc = jax.nn.softplus(dtc_raw[:, :, d] + bias)
        yc_d, h_ctx = _ssd(_flip(xsc * dtc[..., None], rev), _flip(dtc * a_h, rev),
                           _flip(bmc, rev), _flip(cmc, rev), None)
        y_d, _ = _ssd(_flip(xs * dt[..., None], rev), _flip(dt * a_h, rev),
                      _flip(bm, rev), _flip(cm, rev), h_ctx)
        y = y + _flip(y_d, rev)
        if ctx_out:
            yc = yc + _flip(yc_d, rev)

    def finish(yv, zv):
        bsz, l = yv.shape[:2]
        g = yv.reshape(bsz, l, SSD_INNER) * jax.nn.silu(zv.astype(F32))
        return _rms(g, norm_w).astype(zv.dtype) @ w_out

    return finish(y, z), (finish(yc, zc) if ctx_out else None)


def _mlp(v, w_up, w_down):
    return jnp.square(jax.nn.relu(v @ w_up)) @ w_down


def setup_inputs(seed: int = 0) -> dict:
    key = jax.random.key(seed)
    ks = iter(jax.random.split(key, 48))
    n_even = (DEPTH + 1) // 2
    n_odd = DEPTH // 2

    def nrm(shape, scale):
        return jax.random.normal(next(ks), shape, F32) * scale

    def unif(shape, lo, hi):
        return jax.random.uniform(next(ks), shape, F32, minval=lo, maxval=hi)

    d = D_MODEL
    ssd_dt = jnp.exp(unif((n_odd, 2, SSD_HEADS), math.log(1e-3), math.log(1e-1)))
    return {
        'x': nrm((BATCH, SEQ, d), 1.0),
        'c': nrm((BATCH, d), 1.0),
        'ctx': nrm((BATCH, CTX_LEN, d), 1.0),
        'c_ctx': nrm((d,), 1.0),
        'w_mod': nrm((DEPTH, d, N_MOD * d), 0.5 * d ** -0.5),
        'b_mod': nrm((DEPTH, N_MOD * d), 0.01),
        'norm_g': 1.0 + nrm((DEPTH, 4, d), 0.05),
        'w_in_even': nrm((n_even, d, EVEN_IN_W), d ** -0.5),
        'w_out_even': nrm((n_even, EVEN_OUT_W, d), EVEN_OUT_W ** -0.5),
        'diff_lam': nrm((n_even, 4, DIFF_QK_DIM), 0.1),
        'diff_subln': 1.0 + nrm((n_even, DIFF_V_DIM), 0.05),
        's5_lam_re': -0.5 + nrm((n_even, 2, S5_GROUPS, S5_STATE), 0.01),
        's5_lam_im': math.pi * jnp.arange(S5_STATE, dtype=F32) + nrm((n_even, 2, S5_GROUPS, S5_STATE), 0.01),
        's5_log_dt': unif((n_even, 2, S5_GROUPS), math.log(1e-3), math.log(1e-1)),
        's5_b_re': nrm((n_even, 2, S5_GROUPS, S5_STATE, S5_P), (2 * S5_P) ** -0.5),
        's5_b_im': nrm((n_even, 2, S5_GROUPS, S5_STATE, S5_P), (2 * S5_P) ** -0.5),
        's5_c_re': nrm((n_even, 2, S5_GROUPS, S5_P, S5_STATE), S5_STATE ** -0.5),
        's5_c_im': nrm((n_even, 2, S5_GROUPS, S5_P, S5_STATE), S5_STATE ** -0.5),
        's5_d': nrm((n_even, S5_GROUPS, S5_P), 0.5),
        's5_w_glu': nrm((n_even, S5_WIDTH, S5_WIDTH), S5_WIDTH ** -0.5),
        's5_b_glu': nrm((n_even, S5_WIDTH), 0.01),
        'w_in_odd': nrm((n_odd, d, ODD_IN_W), d ** -0.5),
        'conv_w': nrm((n_odd, SSD_CONV, SSD_CONV_CH), SSD_CONV ** -0.5),
        'conv_b': nrm((n_odd, SSD_CONV_CH), 0.01),
        'ssd_dt_bias': ssd_dt + jnp.log(-jnp.expm1(-ssd_dt)),
        'ssd_a_log': jnp.log(unif((n_odd, 2, SSD_HEADS), 1.0, 16.0)),
        'ssd_d': 1.0 + nrm((n_odd, SSD_HEADS), 0.1),
        'ssd_norm_w': 1.0 + nrm((n_odd, SSD_INNER), 0.05),
        'w_out_odd': nrm((n_odd, SSD_INNER, d), SSD_INNER ** -0.5),
        'w_up': nrm((DEPTH, d, MLP_HIDDEN), d ** -0.5),
        'w_down': nrm((DEPTH, MLP_HIDDEN, d), MLP_HIDDEN ** -0.5),
    }


def reference(x, c, ctx, c_ctx, w_mod, b_mod, norm_g, w_in_even, w_out_even, diff_lam, diff_subln,
              s5_lam_re, s5_lam_im, s5_log_dt, s5_b_re, s5_b_im, s5_c_re, s5_c_im, s5_d, s5_w_glu,
              s5_b_glu, w_in_odd, conv_w, conv_b, ssd_dt_bias, ssd_a_log, ssd_d, ssd_norm_w, w_out_odd,
              w_up, w_down):
    cos, sin = _axial_rope_tables(x.shape[1])
    h, hc = x, ctx
    for i in range(DEPTH):
        last = i == DEPTH - 1
        mod = [m[:, None, :] for m in jnp.split(jax.nn.silu(c) @ w_mod[i] + b_mod[i], N_MOD, axis=-1)]
        mod_c = jnp.split(jax.nn.silu(c_ctx) @ w_mod[i] + b_mod[i], N_MOD, axis=-1)
        g = norm_g[i]
        u = _modulate(_rms(h, g[0]), mod[0], mod[1])
        uc = _modulate(_rms(hc, g[0]), mod_c[0], mod_c[1])
        j = i // 2
        if i % 2 == 0:
            lam_init = 0.8 - 0.6 * math.exp(-0.3 * i)
            y, yc = _even_mixer(u, uc, w_in_even[j], w_out_even[j], diff_lam[j], diff_subln[j],
                                s5_lam_re[j], s5_lam_im[j], s5_log_dt[j], s5_b_re[j], s5_b_im[j],
                                s5_c_re[j], s5_c_im[j], s5_d[j], s5_w_glu[j], s5_b_glu[j],
                                cos, sin, lam_init, not last)
        else:
            y, yc = _ssd_mixer(u, uc, w_in_odd[j], conv_w[j], conv_b[j], ssd_dt_bias[j], ssd_a_log[j],
                               ssd_d[j], ssd_norm_w[j], w_out_odd[j], not last)
        h = h + mod[2] * _rms(y, g[1])
        h = h + mod[5] * _rms(_mlp(_modulate(_rms(h, g[2]), mod[3], mod[4]), w_up[i], w_down[i]), g[3])
        if not last:
            hc = hc + mod_c[2] * _rms(yc, g[1])
            hc = hc + mod_c[5] * _rms(_mlp(_modulate(_rms(hc, g[2]), mod_c[3], mod_c[4]), w_up[i], w_down[i]), g[3])
    return h
```

```python
import math
import numpy as np
import concourse.bass as bass
import concourse.mybir as mybir
from concourse.bass_utils import run_bass_kernel_spmd

F32 = mybir.dt.float32
BF16 = mybir.dt.bfloat16
AF = mybir.ActivationFunctionType
ALU = mybir.AluOpType
AX = mybir.AxisListType

NCORES = 8
D = 2048
SEQ = 8192
CTX = 256
TOK = SEQ + CTX
TPC = 1056
LAT = 1024
TT = 352
NTT = 3
EPS = 1e-6


class _Op:
    __slots__ = ("eng", "fn", "deps", "needs_inc", "count", "sem", "dma", "idx")

    def __init__(self, eng, fn, dma):
        self.eng = eng
        self.fn = fn
        self.deps = []
        self.needs_inc = False
        self.count = 0
        self.sem = None
        self.dma = dma


ENGS = ("pe", "act", "dve", "pool", "sp")
N_DMA_SEMS = 24


class Prog:
    def __init__(self):
        self.nc = bass.Bass("TRN2", target_bir_lowering=False)
        self.ops = {e: [] for e in ENGS}
        self.lastw = {}
        self.readers = {}
        self.sems = {e: self.nc.alloc_semaphore("s_" + e) for e in ENGS if e != "sp"}
        self.dsems = [self.nc.alloc_semaphore("d%d" % i) for i in range(N_DMA_SEMS)]
        self.dcount = [0] * N_DMA_SEMS
        self.dlast = [None] * N_DMA_SEMS
        self.dnext = 0
        self.out_dmas = []
        self.uid = 0

    def din(self, name, shape, dt=F32):
        return self.nc.dram_tensor(name, list(shape), dt, kind="ExternalInput").ap()

    def dout(self, name, shape, dt=F32):
        return self.nc.dram_tensor(name, list(shape), dt, kind="ExternalOutput").ap()

    def sb(self, name, shape, dt=F32):
        return self.nc.alloc_sbuf_tensor(name, list(shape), dt)

    def ps(self, name, shape, dt=F32):
        return self.nc.alloc_psum_tensor(name, list(shape), dt)

    def add(self, eng, fn, r=(), w=(), dma=False, out=False):
        op = _Op(eng, fn, dma)
        deps = []
        seen = set()

        def push(d):
            if d is None or id(d) in seen or d is op:
                return
            seen.add(id(d))
            deps.append(d)

        for b in r:
            push(self.lastw.get(b))
        for b in w:
            push(self.lastw.get(b))
            for rd in self.readers.get(b, {}).values():
                push(rd)
        if dma:
            k = self.dnext
            self.dnext = (k + 1) % N_DMA_SEMS
            push(self.dlast[k])
            self.dlast[k] = op
            self.dcount[k] += 16
            op.sem = self.dsems[k]
            op.count = self.dcount[k]
            op.needs_inc = True
        final = []
        for d in deps:
            if (not d.dma) and d.eng == "pe" and eng == "pe" and not dma:
                continue
            d.needs_inc = True
            final.append(d)
        op.deps = final
        for b in r:
            key = ("dma", id(op)) if dma else eng
            self.readers.setdefault(b, {})[key] = op
        for b in w:
            self.lastw[b] = op
            self.readers[b] = {}
        self.ops[eng].append(op)
        if out:
            self.out_dmas.append(op)
        return op

    def finish(self):
        fin = _Op("sp", None, False)
        fin.deps = list(self.out_dmas)
        self.ops["sp"].append(fin)
        for e in ENGS:
            c = 0
            for op in self.ops[e]:
                if op.dma or op.fn is None:
                    continue
                if op.needs_inc:
                    c += 1
                    op.count = c
                    op.sem = self.sems[e]
        nc = self.nc
        progs = self.ops

        def emit(e, ops):
            waited = {}
            for op in ops:
                for d in op.deps:
                    key = id(d.sem)
                    if waited.get(key, 0) < d.count:
                        e.wait_ge(d.sem, d.count)
                        waited[key] = d.count
                if op.fn is None:
                    continue
                ins = op.fn(e)
                if op.dma:
                    ins.then_inc(op.sem, 16)
                elif op.needs_inc:
                    ins.then_inc(op.sem, 1)

        with nc.Block() as block:
            @block.tensor
            def _(e):
                emit(e, progs["pe"])

            @block.scalar
            def _(e):
                emit(e, progs["act"])

            @block.vector
            def _(e):
                emit(e, progs["dve"])

            @block.gpsimd
            def _(e):
                emit(e, progs["pool"])

            @block.sync
            def _(e):
                emit(e, progs["sp"])
        return nc

    def dma(self, q, out, in_, r=(), w=(), final=False):
        return self.add(q, lambda e, o=out, i=in_: e.dma_start(out=o, in_=i), r=r, w=w, dma=True, out=final)

    def mm(self, out, lhsT, rhs, start, stop, r=(), w=()):
        return self.add("pe", lambda e, o=out, l=lhsT, rr=rhs, s=start, t=stop: e.matmul(o, l, rr, start=s, stop=t), r=r, w=w)

    def act(self, out, in_, func, r=(), w=(), bias=None, scale=None, eng="act"):
        def fn(e, o=out, i=in_, f=func, b=bias, sc=scale):
            kw = {}
            if b is not None:
                kw["bias"] = b
            if sc is not None:
                kw["scale"] = sc
            return e.activation(out=o, in_=i, func=f, **kw)
        return self.add("act", fn, r=r, w=w)

    def tt(self, eng, out, in0, in1, op, r=(), w=()):
        return self.add(eng, lambda e, o=out, a=in0, b=in1, p=op: e.tensor_tensor(out=o, in0=a, in1=b, op=p), r=r, w=w)

    def ts(self, eng, out, in0, s1, op0, s2=None, op1=None, r=(), w=()):
        def fn(e, o=out, a=in0, x1=s1, x2=s2, p0=op0, p1=op1):
            if p1 is None:
                return e.tensor_scalar(out=o, in0=a, scalar1=x1, scalar2=None, op0=p0)
            return e.tensor_scalar(out=o, in0=a, scalar1=x1, scalar2=x2, op0=p0, op1=p1)
        return self.add(eng, fn, r=r, w=w)

    def stt(self, out, in0, scalar, in1, op0, op1, r=(), w=()):
        return self.add("dve", lambda e, o=out, a=in0, s=scalar, b=in1, p0=op0, p1=op1:
                        e.scalar_tensor_tensor(out=o, in0=a, scalar=s, in1=b, op0=p0, op1=p1), r=r, w=w)

    def copy(self, eng, out, in_, r=(), w=()):
        if eng == "act":
            return self.add("act", lambda e, o=out, i=in_: e.copy(out=o, in_=i), r=r, w=w)
        return self.add(eng, lambda e, o=out, i=in_: e.tensor_copy(out=o, in_=i), r=r, w=w)

    def memset(self, eng, ap, val, w=()):
        return self.add(eng, lambda e, a=ap, v=val: e.memset(a, v), w=w)

    def recip(self, out, in_, r=(), w=()):
        return self.add("dve", lambda e, o=out, i=in_: e.reciprocal(out=o, in_=i), r=r, w=w)


def _run(pg, in_maps):
    nc = pg.finish()
    res = run_bass_kernel_spmd(nc, in_maps, core_ids=list(range(NCORES)))
    return res.results


def build_k0():
    pg = Prog()
    NJ = 12
    cc = pg.din("cc", [128, 32])
    wm = pg.din("wm", [4, 2048, 1536])
    bm = pg.din("bm", [128, 4 * NJ])
    o = pg.dout("modT", [128, 4 * NJ * 2])
    cct = pg.sb("cct", [128, 32])
    sc = pg.sb("sc", [128, 32])
    bmt = pg.sb("bmt", [128, 4 * NJ])
    ot = pg.sb("ot", [128, 4 * NJ * 2])
    wt = [pg.sb("wt%d" % i, [128, 16, 768]) for i in range(2)]
    pst = [pg.ps("ps%d" % i, [128, 512]) for i in range(2)]
    pg.dma("sp", cct[:], cc[:, :], w=["cct"])
    pg.dma("sp", bmt[:], bm[:, :], w=["bmt"])
    pg.act(sc[:], cct[:], AF.Silu, r=["cct"], w=["sc"])
    n = 0
    for i in range(4):
        for hf in range(2):
            b = n % 2
            wv = wm[i, :, hf * 768:(hf + 1) * 768].rearrange("(dc p) n -> p dc n", p=128)
            for q4 in range(4):
                pg.dma("sp", wt[b][:, 4 * q4:4 * q4 + 4, :], wv[:, 4 * q4:4 * q4 + 4, :], w=["wt%d_%d" % (b, q4)])
            for jj in range(6):
                j = hf * 6 + jj
                pb = (i * NJ + j) % 2
                for dc in range(16):
                    pg.mm(pst[pb][:, 0:2], wt[b][:, dc, jj * 128:(jj + 1) * 128], sc[:, 2 * dc:2 * dc + 2],
                          dc == 0, dc == 15, r=["wt%d_%d" % (b, dc // 4), "sc"], w=["ps%d" % pb])
                col = (i * NJ + j)
                pg.ts("dve", ot[:, 2 * col:2 * col + 2], pst[pb][:, 0:2], bmt[:, col:col + 1], ALU.add,
                      r=["ps%d" % pb, "bmt"], w=["ot"])
            n += 1
    pg.dma("sp", o[:, :], ot[:], r=["ot"], final=True)
    return pg


def run_k0(c, c_ctx, w_mod, b_mod):
    cc = np.stack([c.reshape(16, 128), c_ctx.reshape(16, 128)], axis=-1)
    cc = np.ascontiguousarray(cc.transpose(1, 0, 2).reshape(128, 32))
    in_maps = []
    for k in range(NCORES):
        wm = np.ascontiguousarray(w_mod[:, :, 1536 * k:1536 * (k + 1)])
        bm = b_mod[:, 1536 * k:1536 * (k + 1)].reshape(4, 12, 128).transpose(2, 0, 1).reshape(128, 48)
        in_maps.append({"cc": cc, "wm": wm, "bm": np.ascontiguousarray(bm)})
    res = _run(build_k0(), in_maps)
    mod = np.zeros((4, 2, 12288), np.float32)
    for k in range(NCORES):
        r = res[k]["modT"].reshape(128, 4, 12, 2)
        mod[:, :, 1536 * k:1536 * (k + 1)] = r.transpose(1, 3, 2, 0).reshape(4, 2, 1536)
    return mod


class WStream:
    def __init__(self, pg, name, KC, gw=512, nbuf=3):
        self.pg, self.name, self.KC, self.gw, self.nbuf = pg, name, KC, gw, nbuf
        self.bufs = [pg.sb("%s_w%d" % (name, i), [128, KC, gw], BF16) for i in range(nbuf)]
        self.n = 0

    def load(self, wap, c0, cols):
        b = self.n % self.nbuf
        self.n += 1
        wv = wap[:, c0:c0 + cols].rearrange("(kc p) n -> p kc n", p=128)
        step = 4
        for q in range(0, self.KC, step):
            self.pg.dma("pool", self.bufs[b][:, q:q + step, 0:cols], wv[:, q:q + step, :],
                        w=["%s_w%d_%d" % (self.name, b, q // step)])
        return b

    def rid(self, b, kc):
        return "%s_w%d_%d" % (self.name, b, kc // 4)


def linear_fm(pg, ws, wap, N, X, xid, KC, pss, evac, col_ranges=None):
    ng = (N + ws.gw - 1) // ws.gw
    cnt = 0
    for g in range(ng):
        cols = min(ws.gw, N - g * ws.gw)
        b = ws.load(wap, g * ws.gw, cols)
        for ocl in range(cols // 128):
            oc = g * (ws.gw // 128) + ocl
            for tt in range(NTT):
                pb = cnt % len(pss)
                cnt += 1
                ps, psid = pss[pb]
                for kc in range(KC):
                    pg.mm(ps[:, 0:TT], ws.bufs[b][:, kc, ocl * 128:(ocl + 1) * 128], X[:, kc, tt * TT:(tt + 1) * TT],
                          kc == 0, kc == KC - 1, r=[ws.rid(b, kc), xid(kc)], w=[psid])
                evac(oc, tt, ps[:, 0:TT], psid)


def sumsq_rstd(pg, name, src_fn, src_id, KC, dim, ones, sq_bufs, ss_ps, rstd, eng_sq="act"):
    for c in range(KC):
        sb_, sid = sq_bufs[c % len(sq_bufs)]
        pg.act(sb_[:, :], src_fn(c), AF.Square, r=[src_id(c)], w=[sid])
        for tt in range(NTT):
            ps, psid = ss_ps[tt]
            pg.mm(ps[:, 0:TT], ones[:, :], sb_[:, tt * TT:(tt + 1) * TT], c == 0, c == KC - 1, r=[sid, "ones"], w=[psid])
    for tt in range(NTT):
        ps, psid = ss_ps[tt]
        pg.act(rstd[:, tt * TT:(tt + 1) * TT], ps[:, 0:TT], AF.Sqrt, r=[psid], w=[name + "_rs%d" % tt],
               bias=pg.eps_ap, scale=1.0 / dim)
        pg.recip(rstd[:, tt * TT:(tt + 1) * TT], rstd[:, tt * TT:(tt + 1) * TT], r=[name + "_rs%d" % tt], w=[name + "_rs%d" % tt])
    return [name + "_rs%d" % tt for tt in range(NTT)]


def setup_consts(pg):
    ones = pg.sb("ones", [128, 128], BF16)
    pg.memset("dve", ones[:, :], 1.0, w=["ones"])
    epst = pg.sb("epst", [128, 1])
    pg.memset("dve", epst[:, :], EPS, w=["eps"])
    pg.eps_ap = epst[:, 0:1]
    return ones


def build_k1(N):
    pg = Prog()
    hT = pg.din("hT", [D, TPC])
    mp = pg.din("mp", [128, 16 * 5])
    w = pg.din("w", [D, N])
    pT = pg.dout("pT", [N, TPC])
    ones = setup_consts(pg)
    H = pg.sb("H", [128, 16, TPC])
    U = pg.sb("U", [128, 16, TPC], BF16)
    mpt = pg.sb("mpt", [128, 16, 5])
    AB = pg.sb("AB", [128, 16, 2])
    rstd = pg.sb("rstd", [128, TPC])
    sqb = [(pg.sb("sq%d" % i, [128, TPC], BF16), "sq%d" % i) for i in range(2)]
    tmpb = [(pg.sb("tmp%d" % i, [128, TPC]), "tmp%d" % i) for i in range(2)]
    Ob = [(pg.sb("O%d" % i, [128, TPC]), "O%d" % i) for i in range(3)]
    ssp = [(pg.ps("ss%d" % i, [128, 512]), "ss%d" % i) for i in range(3)]
    mmp = [(pg.ps("mm%d" % i, [128, 512]), "mm%d" % i) for i in range(4)]
    ws = WStream(pg, "win", 16)

    hv = hT.rearrange("(c p) t -> p c t", p=128)
    for q in range(4):
        pg.dma("sp", H[:, 4 * q:4 * q + 4, :], hv[:, 4 * q:4 * q + 4, :], w=["H%d" % q])
    pg.dma("sp", mpt[:, :, :], mp.rearrange("p (c f) -> p c f", f=5), w=["mpt"])
    rs_ids = sumsq_rstd(pg, "n0", lambda c: H[:, c, :], lambda c: "H%d" % (c // 4), 16, D, ones, sqb, ssp, rstd)
    pg.stt(AB[:, :, 0], mpt[:, :, 2], 1.0, mpt[:, :, 0], ALU.add, ALU.mult, r=["mpt"], w=["AB"])
    pg.stt(AB[:, :, 1], mpt[:, :, 4], 1.0, mpt[:, :, 0], ALU.add, ALU.mult, r=["mpt"], w=["AB"])
    for c in range(16):
        tb, tid = tmpb[c % 2]
        pg.tt("dve", tb[:, :], H[:, c, :], rstd[:, :], ALU.mult, r=["H%d" % (c // 4)] + rs_ids, w=[tid])
        pg.act(U[:, c, 0:LAT], tb[:, 0:LAT], AF.Identity, r=[tid, "AB", "mpt"], w=["U%d" % c],
               bias=mpt[:, c, 1:2], scale=AB[:, c, 0:1])
        pg.act(U[:, c, LAT:TPC], tb[:, LAT:TPC], AF.Identity, r=[tid, "AB", "mpt"], w=["U%d" % c],
               bias=mpt[:, c, 3:4], scale=AB[:, c, 1:2])
    state = {"n": 0}

    def evac(oc, tt, ps, psid):
        ob, oid = Ob[oc % 3]
        eng = "act" if (state["n"] % 2 == 0) else "dve"
        state["n"] += 1
        pg.copy(eng, ob[:, tt * TT:(tt + 1) * TT], ps, r=[psid], w=[oid + "_%d" % tt])
        if tt == NTT - 1:
            pg.dma("sp", pT[oc * 128:(oc + 1) * 128, :], ob[:, :], r=[oid + "_%d" % t for t in range(NTT)], final=True)

    linear_fm(pg, ws, w, N, U, lambda kc: "U%d" % kc, 16, mmp, evac)
    return pg


def tok_cols(k):
    return np.concatenate([CTX + np.arange(LAT * k, LAT * (k + 1)), np.arange(32 * k, 32 * (k + 1))])


def shard_T(aT):
    return [np.ascontiguousarray(aT[:, tok_cols(k)]) for k in range(NCORES)]


def unshard_T(parts):
    F = parts[0].shape[0]
    out = np.empty((F, TOK), parts[0].dtype)
    for k in range(NCORES):
        out[:, tok_cols(k)] = parts[k]
    return out


def pack_cols(vecs):
    a = np.stack([v.reshape(16, 128) for v in vecs], axis=-1)
    return np.ascontiguousarray(a.transpose(1, 0, 2).reshape(128, -1)).astype(np.float32)


def run_k1(hT, g0, mod_l, mod_c, w_in):
    N = w_in.shape[1]
    m = lambda v, i: v[i * D:(i + 1) * D]
    mp = pack_cols([g0, m(mod_l, 0), m(mod_l, 1), m(mod_c, 0), m(mod_c, 1)])
    hs = shard_T(hT)
    in_maps = [{"hT": hs[k], "mp": mp, "w": w_in} for k in range(NCORES)]
    res = _run(build_k1(N), in_maps)
    return unshard_T([res[k]["pT"] for k in range(NCORES)])


def residual_tail(pg, name, Y, yid, rstd, rs_ids, coef, coef_id, h_dram, out_dram, scr, inplace=False):
    hb = scr[0:2]
    tb = scr[2:4]
    ob = scr[4:6]
    for c in range(16):
        h_, hid = hb[c % 2]
        t_, tid = tb[c % 2]
        if inplace:
            oo, oid = Y[:, c, :], yid(c)
        else:
            o_, oid = ob[c % 2]
            oo = o_[:, :]
        pg.dma("sp", h_[:, :], h_dram[c * 128:(c + 1) * 128, :], w=[hid])
        pg.tt("dve", t_[:, :], Y[:, c, :], rstd[:, :], ALU.mult, r=[yid(c)] + rs_ids, w=[tid])
        pg.stt(oo[:, 0:LAT], t_[:, 0:LAT], coef[:, c, 0:1], h_[:, 0:LAT], ALU.mult, ALU.add, r=[tid, hid, coef_id], w=[oid])
        pg.stt(oo[:, LAT:TPC], t_[:, LAT:TPC], coef[:, c, 1:2], h_[:, LAT:TPC], ALU.mult, ALU.add, r=[tid, hid, coef_id], w=[oid])
        pg.dma("sp", out_dram[c * 128:(c + 1) * 128, :], oo, r=[oid], final=True)


def build_k3b():
    pg = Prog()
    uT = pg.din("uT", [D, TPC])
    hT = pg.din("hT", [D, TPC])
    mp = pg.din("mp", [128, 16 * 3])
    wu = pg.din("wu", [D, 4 * D])
    wd = pg.din("wd", [4 * D, D])
    oT = pg.dout("oT", [D, TPC])
    ones = setup_consts(pg)
    U = pg.sb("U", [128, 16, TPC], BF16)
    HQ = pg.sb("HQ", [128, 8, TPC], BF16)
    Y = pg.sb("Y", [128, 16, TPC])
    mpt = pg.sb("mpt", [128, 16, 3])
    coef = pg.sb("coef", [128, 16, 2])
    rstd = pg.sb("rstd", [128, TPC])
    rt = [(pg.sb("rt%d" % i, [128, TT]), "rt%d" % i) for i in range(3)]
    sqb = [(pg.sb("sq%d" % i, [128, TPC], BF16), "sq%d" % i) for i in range(2)]
    ssp = [(pg.ps("ss%d" % i, [128, 512]), "ss%d" % i) for i in range(3)]
    mmp = [(pg.ps("mm%d" % i, [128, 512]), "mm%d" % i) for i in range(4)]
    wsu = WStream(pg, "wu", 16, gw=256)
    wsd = WStream(pg, "wd", 8, gw=512)
    uv = uT.rearrange("(c p) t -> p c t", p=128)
    for q in range(4):
        pg.dma("pool", U[:, 4 * q:4 * q + 4, :], uv[:, 4 * q:4 * q + 4, :], w=["U%d" % c for c in range(4 * q, 4 * q + 4)])
    pg.dma("sp", mpt[:, :, :], mp.rearrange("p (c f) -> p c f", f=3), w=["mpt"])
    pg.tt("dve", coef[:, :, 0], mpt[:, :, 0], mpt[:, :, 1], ALU.mult, r=["mpt"], w=["coef"])
    pg.tt("dve", coef[:, :, 1], mpt[:, :, 0], mpt[:, :, 2], ALU.mult, r=["mpt"], w=["coef"])
    st = {"n": 0}
    for e in range(8):
        def evac_up(oc, tt, ps, psid):
            r_, rid = rt[st["n"] % 3]
            st["n"] += 1
            pg.act(r_[:, :], ps, AF.Relu, r=[psid], w=[rid])
            pg.tt("pool", HQ[:, oc, tt * TT:(tt + 1) * TT], r_[:, :], r_[:, :], ALU.mult, r=[rid], w=["HQ%d" % oc])

        linear_fm(pg, wsu, wu[:, e * 1024:(e + 1) * 1024], 1024, U, lambda kc: "U%d" % kc, 16, mmp, evac_up)

        def evac_dn(oc, tt, ps, psid, e=e):
            dst = Y[:, oc, tt * TT:(tt + 1) * TT]
            if e == 0:
                pg.copy("act", dst, ps, r=[psid], w=["Y%d" % oc])
            else:
                pg.tt("dve", dst, ps, dst, ALU.add, r=[psid, "Y%d" % oc], w=["Y%d" % oc])

        linear_fm(pg, wsd, wd[e * 1024:(e + 1) * 1024, :], D, HQ, lambda kc: "HQ%d" % kc, 8, mmp, evac_dn)
    rs_ids = sumsq_rstd(pg, "n3", lambda c: Y[:, c, :], lambda c: "Y%d" % c, 16, D, ones, sqb, ssp, rstd)
    scr = [(pg.sb("scr%d" % i, [128, TPC]), "scr%d" % i) for i in range(6)]
    residual_tail(pg, "rt", Y, lambda c: "Y%d" % c, rstd, rs_ids, coef, "coef", hT, oT, scr)
    return pg


def run_k3b(u2T, h1T, g3, mod_l, mod_c, w_up, w_down):
    m = lambda v, i: v[i * D:(i + 1) * D]
    mp = pack_cols([g3, m(mod_l, 5), m(mod_c, 5)])
    us, hs = shard_T(u2T), shard_T(h1T)
    in_maps = [{"uT": us[k], "hT": hs[k], "mp": mp, "wu": w_up, "wd": w_down} for k in range(NCORES)]
    res = _run(build_k3b(), in_maps)
    return unshard_T([res[k]["oT"] for k in range(NCORES)])


GELU_C = 0.044715
GELU_S = 2.0 * math.sqrt(2.0 / math.pi)


def build_k3a(even):
    pg = Prog()
    hT = pg.din("hT", [D, TPC])
    mp = pg.din("mp", [128, 16 * 8])
    KC = 16 if even else 32
    wout = pg.din("wout", [KC * 128, D])
    h1T = pg.dout("h1T", [D, TPC])
    u2T = pg.dout("u2T", [D, TPC])
    ones = setup_consts(pg)
    M = pg.sb("M", [128, KC, TPC], BF16)
    Y = pg.sb("Y", [128, 16, TPC])
    mpt = pg.sb("mpt", [128, 16, 8])
    coef = pg.sb("coef", [128, 16, 2])
    AB = pg.sb("AB", [128, 16, 2])
    rstd = pg.sb("rstd", [128, TPC])
    rstd2 = pg.sb("rstd2", [128, TPC])
    sqb = [(pg.sb("sq%d" % i, [128, TPC], BF16), "sq%d" % i) for i in range(2)]
    ssp = [(pg.ps("ss%d" % i, [128, 512]), "ss%d" % i) for i in range(3)]
    mmp = [(pg.ps("mm%d" % i, [128, 512]), "mm%d" % i) for i in range(4)]
    scr = [(pg.sb("scr%d" % i, [128, TPC]), "scr%d" % i) for i in range(4)]
    pg.dma("sp", mpt[:, :, :], mp.rearrange("p (c f) -> p c f", f=8), w=["mpt"])
    pg.tt("dve", coef[:, :, 0], mpt[:, :, 0], mpt[:, :, 1], ALU.mult, r=["mpt"], w=["coef"])
    pg.tt("dve", coef[:, :, 1], mpt[:, :, 0], mpt[:, :, 2], ALU.mult, r=["mpt"], w=["coef"])
    pg.stt(AB[:, :, 0], mpt[:, :, 5], 1.0, mpt[:, :, 3], ALU.add, ALU.mult, r=["mpt"], w=["AB"])
    pg.stt(AB[:, :, 1], mpt[:, :, 7], 1.0, mpt[:, :, 3], ALU.add, ALU.mult, r=["mpt"], w=["AB"])
    if even:
        ws = WStream(pg, "w", 16, gw=512, nbuf=3)
        oT = pg.din("oT", [1024, TPC])
        sfT = pg.din("sfT", [1024, TPC])
        srT = pg.din("srT", [1024, TPC])
        wglu = pg.din("wglu", [1024, 1024])
        bglu = pg.din("bglu", [128, 8])
        bgt = pg.sb("bgt", [128, 8])
        Gb = pg.sb("Gb", [128, 8, TPC], BF16)
        pg.dma("sp", bgt[:, :], bglu[:, :], w=["bgt"])
        ov = oT.rearrange("(c p) t -> p c t", p=128)
        for q in range(2):
            pg.dma("pool", M[:, 4 * q:4 * q + 4, :], ov[:, 4 * q:4 * q + 4, :], w=["M%d" % c for c in range(4 * q, 4 * q + 4)])
        fa = scr[0:2]
        fb = scr[2:4]
        for c in range(8):
            a_, aid = fa[c % 2]
            b_, bid = fb[c % 2]
            G = Y[:, 8 + c, :]
            gid = "Y%d" % (8 + c)
            pg.dma("sp", a_[:, :], sfT[c * 128:(c + 1) * 128, :], w=[aid])
            pg.dma("sp", b_[:, :], srT[c * 128:(c + 1) * 128, :], w=[bid])
            pg.tt("dve", a_[:, :], a_[:, :], b_[:, :], ALU.add, r=[aid, bid], w=[aid])
            pg.tt("pool", b_[:, :], a_[:, :], a_[:, :], ALU.mult, r=[aid], w=[bid])
            pg.ts("dve", b_[:, :], b_[:, :], GELU_C, ALU.mult, 1.0, ALU.add, r=[bid], w=[bid])
            pg.tt("pool", b_[:, :], b_[:, :], a_[:, :], ALU.mult, r=[aid, bid], w=[bid])
            pg.act(b_[:, :], b_[:, :], AF.Sigmoid, r=[bid], w=[bid], scale=GELU_S)
            pg.tt("dve", G, a_[:, :], b_[:, :], ALU.mult, r=[aid, bid], w=[gid])
            pg.copy("pool", Gb[:, c, :], G, r=[gid], w=["Gb%d" % c])
        st = {"n": 0}
        zt = [(pg.sb("zt%d" % i, [128, TT]), "zt%d" % i) for i in range(3)]

        def evac_glu(oc, tt, ps, psid):
            z_, zid = zt[st["n"] % 3]
            st["n"] += 1
            pg.act(z_[:, :], ps, AF.Sigmoid, r=[psid, "bgt"], w=[zid], bias=bgt[:, oc:oc + 1])
            pg.tt("dve", M[:, 8 + oc, tt * TT:(tt + 1) * TT], z_[:, :], Y[:, 8 + oc, tt * TT:(tt + 1) * TT], ALU.mult,
                  r=[zid, "Y%d" % (8 + oc)], w=["M%d" % (8 + oc)])

        ws.KC = 8
        linear_fm(pg, ws, wglu, 1024, Gb, lambda kc: "Gb%d" % kc, 8, mmp, evac_glu)
        ws.KC = 16

        def evac_out(oc, tt, ps, psid):
            eng = "act" if (tt % 2 == 0) else "dve"
            pg.copy(eng, Y[:, oc, tt * TT:(tt + 1) * TT], ps, r=[psid], w=["Y%d" % oc])
    else:
        ws = WStream(pg, "w", 32, gw=256, nbuf=2)
        yfT = pg.din("yfT", [4096, TPC])
        yrT = pg.din("yrT", [4096, TPC])
        xT = pg.din("xT", [4096, TPC])
        zT = pg.din("zT", [4096, TPC])
        nw = pg.din("nw", [128, 64])
        nwt = pg.sb("nwt", [128, 64])
        rstdg = pg.sb("rstdg", [128, TPC])
        pg.dma("sp", nwt[:, :], nw[:, :], w=["nwt"])
        for c in range(32):
            a_, aid = scr[2 * (c % 2)]
            b_, bid = scr[2 * (c % 2) + 1]
            sb_, sid = sqb[c % 2]
            rows = slice(c * 128, (c + 1) * 128)
            pg.dma("sp", a_[:, :], yfT[rows, :], w=[aid])
            pg.dma("sp", b_[:, :], yrT[rows, :], w=[bid])
            pg.tt("dve", a_[:, :], a_[:, :], b_[:, :], ALU.add, r=[aid, bid], w=[aid])
            pg.dma("sp", b_[:, :], xT[rows, :], w=[bid])
            pg.stt(a_[:, :], b_[:, :], nwt[:, 32 + c:33 + c], a_[:, :], ALU.mult, ALU.add, r=[aid, bid, "nwt"], w=[aid])
            pg.dma("sp", b_[:, :], zT[rows, :], w=[bid])
            pg.act(b_[:, :], b_[:, :], AF.Silu, r=[bid], w=[bid])
            pg.tt("dve", a_[:, :], a_[:, :], b_[:, :], ALU.mult, r=[aid, bid], w=[aid])
            pg.act(sb_[:, :], a_[:, :], AF.Square, r=[aid], w=[sid])
            for tt in range(NTT):
                ps, psid = ssp[tt]
                pg.mm(ps[:, 0:TT], ones[:, :], sb_[:, tt * TT:(tt + 1) * TT], c == 0, c == 31, r=[sid, "ones"], w=[psid])
            pg.ts("pool", M[:, c, :], a_[:, :], nwt[:, c:c + 1], ALU.mult, r=[aid, "nwt"], w=["M%d" % c])
        rg_ids = []
        for tt in range(NTT):
            ps, psid = ssp[tt]
            sl = rstdg[:, tt * TT:(tt + 1) * TT]
            pg.act(sl, ps[:, 0:TT], AF.Sqrt, r=[psid], w=["rg%d" % tt], bias=pg.eps_ap, scale=1.0 / 4096)
            pg.recip(sl, sl, r=["rg%d" % tt], w=["rg%d" % tt])
            rg_ids.append("rg%d" % tt)

        def evac_out(oc, tt, ps, psid):
            pg.tt("dve", Y[:, oc, tt * TT:(tt + 1) * TT], ps, rstdg[:, tt * TT:(tt + 1) * TT], ALU.mult,
                  r=[psid, "rg%d" % tt], w=["Y%d" % oc])

    linear_fm(pg, ws, wout, D, M, lambda kc: "M%d" % kc, KC, mmp, evac_out)
    rs_ids = sumsq_rstd(pg, "n1", lambda c: Y[:, c, :], lambda c: "Y%d" % c, 16, D, ones, sqb, ssp, rstd)
    residual_tail(pg, "r1", Y, lambda c: "Y%d" % c, rstd, rs_ids, coef, "coef", hT, h1T, scr, inplace=True)
    rs2 = sumsq_rstd(pg, "n2", lambda c: Y[:, c, :], lambda c: "Y%d" % c, 16, D, ones, sqb, ssp, rstd2)
    tb = scr[0:2]
    ub = scr[2:4]
    for c in range(16):
        t_, tid = tb[c % 2]
        u_, uid = ub[c % 2]
        pg.tt("dve", t_[:, :], Y[:, c, :], rstd2[:, :], ALU.mult, r=["Y%d" % c] + rs2, w=[tid])
        pg.act(u_[:, 0:LAT], t_[:, 0:LAT], AF.Identity, r=[tid, "AB", "mpt"], w=[uid], bias=mpt[:, c, 4:5], scale=AB[:, c, 0:1])
        pg.act(u_[:, LAT:TPC], t_[:, LAT:TPC], AF.Identity, r=[tid, "AB", "mpt"], w=[uid], bias=mpt[:, c, 6:7], scale=AB[:, c, 1:2])
        pg.dma("sp", u2T[c * 128:(c + 1) * 128, :], u_[:, :], r=[uid], final=True)
    return pg


def run_k3a(even, hT, mix, g, mod_l, mod_c, w_out, extra):
    m = lambda v, i: v[i * D:(i + 1) * D]
    mp = pack_cols([g[1], m(mod_l, 2), m(mod_c, 2), g[2], m(mod_l, 3), m(mod_l, 4), m(mod_c, 3), m(mod_c, 4)])
    hs = shard_T(hT)
    sh = {k: shard_T(v) for k, v in mix.items()}
    in_maps = []
    for k in range(NCORES):
        d = {"hT": hs[k], "mp": mp, "wout": w_out}
        for kk in sh:
            d[kk] = sh[kk][k]
        d.update(extra)
        in_maps.append(d)
    res = _run(build_k3a(even), in_maps)
    return unshard_T([res[k]["h1T"] for k in range(NCORES)]), unshard_T([res[k]["u2T"] for k in range(NCORES)])


NCH = TOK // 128


def build_k2ob(nch=NCH):
    pg = Prog()
    T = nch * 128
    xa = pg.din("xa", [2, T, 512])
    dtr = pg.din("dtr", [2, T, 8])
    Bm = pg.din("Bm", [2, T, 128])
    BT = pg.din("BT", [2, 128, T])
    CT = pg.din("CT", [2, 128, T])
    hp = pg.din("hp", [128, 32])
    msk = pg.din("msk", [128, 256])
    y = pg.dout("y", [2, T, 512])
    hpt = pg.sb("hpt", [128, 2, 2, 8])
    mk = pg.sb("mk", [128, 256])
    onesf = pg.sb("onesf", [128, 128])
    Aneg = pg.sb("Aneg", [128, 2, 8])
    pg.dma("sp", hpt[:, :, :, :], hp.rearrange("p (d f h) -> p d f h", d=2, f=2), w=["hpt"])
    pg.dma("sp", mk[:, :], msk[:, :], w=["mk"])
    pg.memset("dve", onesf[:, :], 1.0, w=["onesf"])
    UT = mk[:, 0:128]
    TRI = mk[:, 128:256]
    for d in range(2):
        pg.act(Aneg[:, d, :], hpt[:, d, 1, :], AF.Exp, r=["hpt"], w=["Aneg"])
    pg.ts("dve", Aneg[:, :, :], Aneg[:, :, :], -1.0, ALU.mult, r=["Aneg"], w=["Aneg"])
    NB = 2

    def tiles(nm, shape, dt=F32):
        return [[(pg.sb("%s_%d_%d" % (nm, d, i), shape, dt), "%s_%d_%d" % (nm, d, i)) for i in range(NB)] for d in range(2)]

    Xt = tiles("X", [128, 8, 64])
    Dt = tiles("dt", [128, 8])
    At = tiles("a", [128, 8])
    Bb = tiles("Bb", [128, 128], BF16)
    BTb = tiles("BTb", [128, 128], BF16)
    CTb = tiles("CTb", [128, 128], BF16)
    Rt = tiles("R", [128, 8, 128])
    Et = tiles("E", [128, 8, 128])
    CBm = tiles("CBm", [128, 128])
    MT = tiles("MT", [128, 8, 128], BF16)
    xdt = tiles("xdt", [128, 8, 64], BF16)
    xdec = tiles("xdec", [128, 8, 64], BF16)
    ev = tiles("ev", [128, 3, 8])
    yt = tiles("yt", [128, 8, 64])
    S = [(pg.sb("S%d" % d, [128, 8, 64]), "S%d" % d) for d in range(2)]
    Sb = [(pg.sb("Sb%d" % d, [128, 8, 64], BF16), "Sb%d" % d) for d in range(2)]
    p_seg = [(pg.ps("pseg%d" % i, [128, 512]), "pseg%d" % i) for i in range(2)]
    p_cbt = (pg.ps("pcbt", [128, 512]), "pcbt")
    p_yd = (pg.ps("pyd", [128, 512]), "pyd")
    p_yo = (pg.ps("pyo", [128, 512]), "pyo")
    p_ns = (pg.ps("pns", [128, 512]), "pns")
    p_vec = (pg.ps("pvec", [128, 512]), "pvec")
    for d in range(2):
        pg.memset("dve", S[d][0][:, :, :], 0.0, w=[S[d][1]])
        pg.memset("dve", Sb[d][0][:, :, :], 0.0, w=[Sb[d][1]])

    def bc_h(ap8, n):
        return ap8.unsqueeze(2).to_broadcast([128, 8, n])

    for ci in range(nch):
        for d in range(2):
            i = ci % NB
            t0 = ci * 128
            X, Xi = Xt[d][i]
            dtt, dti = Dt[d][i]
            a_, ai = At[d][i]
            B_, Bi = Bb[d][i]
            BT_, BTi = BTb[d][i]
            CT_, CTi = CTb[d][i]
            R_, Ri = Rt[d][i]
            E_, Ei = Et[d][i]
            CB_, CBi = CBm[d][i]
            M_, Mi = MT[d][i]
            xd_, xdi = xdt[d][i]
            xc_, xci = xdec[d][i]
            ev_, evi = ev[d][i]
            y_, yi = yt[d][i]
            S_, Si = S[d]
            Sb_, Sbi = Sb[d]
            pg.dma("sp", X[:, :, :], xa[d, t0:t0 + 128, :].rearrange("t (h p) -> t h p", h=8), w=[Xi])
            pg.dma("sp", dtt[:, :], dtr[d, t0:t0 + 128, :], w=[dti])
            pg.dma("pool", B_[:, :], Bm[d, t0:t0 + 128, :], w=[Bi])
            pg.dma("pool", BT_[:, :], BT[d, :, t0:t0 + 128], w=[BTi])
            pg.dma("pool", CT_[:, :], CT[d, :, t0:t0 + 128], w=[CTi])
            pg.tt("dve", dtt[:, :], dtt[:, :], hpt[:, d, 0, :], ALU.add, r=[dti, "hpt"], w=[dti])
            pg.act(dtt[:, :], dtt[:, :], AF.Exp, r=[dti], w=[dti])
            pg.act(dtt[:, :], dtt[:, :], AF.Ln, r=[dti], w=[dti], bias=1.0)
            pg.tt("dve", a_[:, :], dtt[:, :], Aneg[:, d, :], ALU.mult, r=[dti, "Aneg"], w=[ai])
            pg.tt("dve", R_[:, :, :], TRI.unsqueeze(1).to_broadcast([128, 8, 128]), bc_h(a_[:, :], 128), ALU.mult,
                  r=[ai, "mk"], w=[Ri])
            for hf in range(2):
                ps, pid = p_seg[hf]
                pg.mm(ps[:, :], UT, R_[:, 4 * hf:4 * hf + 4, :].rearrange("p h l -> p (h l)"), True, True, r=["mk", Ri], w=[pid])
                pg.act(E_[:, 4 * hf:4 * hf + 4, :].rearrange("p h l -> p (h l)"), ps[:, :], AF.Exp, r=[pid], w=[Ei + "_%d" % hf])
            ps, pid = p_cbt
            pg.mm(ps[:, 0:128], BT_[:, :], CT_[:, :], True, True, r=[BTi, CTi], w=[pid])
            pg.tt("dve", CB_[:, :], ps[:, 0:128], TRI, ALU.mult, r=[pid, "mk"], w=[CBi])
            pg.tt("dve", M_[:, :, :], E_[:, :, :], CB_[:, :].unsqueeze(1).to_broadcast([128, 8, 128]), ALU.mult,
                  r=[Ei + "_0", Ei + "_1", CBi], w=[Mi])
            pg.tt("dve", xd_[:, :, :], X[:, :, :], bc_h(dtt[:, :], 64), ALU.mult, r=[Xi, dti], w=[xdi])
            pv, pvi = p_vec
            pg.mm(pv[:, 0:8], TRI, a_[:, :], True, True, r=["mk", ai], w=[pvi])
            pg.mm(pv[:, 8:16], UT, a_[:, :], True, True, r=["mk", ai], w=[pvi])
            pg.mm(pv[:, 16:24], onesf[:, :], a_[:, :], True, True, r=["onesf", ai], w=[pvi])
            pg.act(ev_[:, :, :].rearrange("p a h -> p (a h)"), pv[:, 0:24], AF.Exp, r=[pvi], w=[evi])
            pg.tt("dve", xc_[:, :, :], xd_[:, :, :], bc_h(ev_[:, 1, :], 64), ALU.mult, r=[xdi, evi], w=[xci])
            pyd, pydi = p_yd
            for h in range(8):
                pg.mm(pyd[:, 64 * h:64 * h + 64], M_[:, h, :], xd_[:, h, :], True, True, r=[Mi, xdi], w=[pydi])
            pyo, pyoi = p_yo
            pg.mm(pyo[:, :], CT_[:, :], Sb_[:, :, :].rearrange("p h q -> p (h q)"), True, True, r=[CTi, Sbi], w=[pyoi])
            pg.tt("dve", y_[:, :, :], pyo[:, :].rearrange("p (h q) -> p h q", h=8), bc_h(ev_[:, 0, :], 64), ALU.mult,
                  r=[pyoi, evi], w=[yi])
            pg.tt("dve", y_[:, :, :], pyd[:, :].rearrange("p (h q) -> p h q", h=8), y_[:, :, :], ALU.add, r=[pydi, yi], w=[yi])
            pg.dma("sp", y[d, t0:t0 + 128, :], y_[:, :, :].rearrange("p h q -> p (h q)"), r=[yi], final=True)
            pns, pnsi = p_ns
            pg.mm(pns[:, :], B_[:, :], xc_[:, :, :].rearrange("p h q -> p (h q)"), True, True, r=[Bi, xci], w=[pnsi])
            pg.tt("dve", S_[:, :, :], S_[:, :, :], bc_h(ev_[:, 2, :], 64), ALU.mult, r=[Si, evi], w=[Si])
            pg.tt("dve", S_[:, :, :], pns[:, :].rearrange("p (h q) -> p h q", h=8), S_[:, :, :], ALU.add, r=[pnsi, Si], w=[Si])
            pg.copy("act", Sb_[:, :, :], S_[:, :, :], r=[Si], w=[Sbi])
    return pg


def ssd_masks():
    k = np.arange(128)
    UT = (k[:, None] > k[None, :]).astype(np.float32)
    TRI = (k[:, None] <= k[None, :]).astype(np.float32)
    return np.ascontiguousarray(np.concatenate([UT, TRI], axis=1))


def build_k2oa():
    pg = Prog()
    xin = pg.din("xin", [768, TOK])
    cw = pg.din("cw", [128, 36])
    xo = pg.dout("xo", [768, TOK])
    cwt = pg.sb("cwt", [128, 6, 6])
    pg.dma("sp", cwt[:, :, :], cw.rearrange("p (c f) -> p c f", f=6), w=["cwt"])
    Xb = [(pg.sb("X%d" % i, [128, TOK]), "X%d" % i) for i in range(2)]
    Ab = [(pg.sb("A%d" % i, [128, TOK]), "A%d" % i) for i in range(2)]
    segs = [(0, CTX), (CTX, TOK)]
    for c in range(6):
        X, Xi = Xb[c % 2]
        A, Ai = Ab[c % 2]
        for q in range(4):
            pg.dma("sp", X[:, q * 2112:(q + 1) * 2112], xin[c * 128:(c + 1) * 128, q * 2112:(q + 1) * 2112], w=[Xi])
        for (s, e) in segs:
            pg.ts("dve", A[:, s:e], X[:, s:e], cwt[:, c, 2:3], ALU.mult, r=[Xi, "cwt"], w=[Ai])
            for k in (0, 1, 3, 4):
                o = k - 2
                lo = s + max(0, -o)
                hi = e - max(0, o)
                pg.stt(A[:, lo:hi], X[:, lo + o:hi + o], cwt[:, c, k:k + 1], A[:, lo:hi], ALU.mult, ALU.add,
                       r=[Xi, Ai, "cwt"], w=[Ai])
        for q in range(4):
            sl = slice(q * 2112, (q + 1) * 2112)
            pg.act(A[:, sl], A[:, sl], AF.Silu, r=[Ai], w=[Ai], bias=cwt[:, c, 5:6])
        pg.dma("sp", xo[c * 128:(c + 1) * 128, :], A[:, :], r=[Ai], final=True)
    return pg


def run_k2oa(xbcT, conv_w, conv_b):
    in_maps = []
    for k in range(NCORES):
        ch = slice(768 * k, 768 * (k + 1))
        f = np.concatenate([conv_w[:, ch], conv_b[None, ch]], axis=0)
        cwp = f.reshape(6, 6, 128).transpose(2, 1, 0).reshape(128, 36)
        in_maps.append({"xin": np.ascontiguousarray(xbcT[ch]), "cw": np.ascontiguousarray(cwp)})
    res = _run(build_k2oa(), in_maps)
    return np.concatenate([res[k]["xo"] for k in range(NCORES)], axis=0)


def flipseg(a, axis):
    a = np.moveaxis(a, axis, 0)
    out = np.concatenate([a[:CTX][::-1], a[CTX:][::-1]], axis=0)
    return np.moveaxis(out, 0, axis)


def run_k2ob(xbcaT, dtrT, dt_bias, a_log):
    in_maps = []
    msk = ssd_masks()
    for g in range(NCORES):
        xs = xbcaT[512 * g:512 * (g + 1)]
        Bs = xbcaT[4096 + 128 * g:4096 + 128 * (g + 1)]
        Cs = xbcaT[5120 + 128 * g:5120 + 128 * (g + 1)]
        xa, dtr, Bm, BTt, CTt = [], [], [], [], []
        for d in range(2):
            f = (lambda a: flipseg(a, 1)) if d == 1 else (lambda a: a)
            xa.append(f(xs).T)
            dtr.append(f(dtrT[64 * d + 8 * g:64 * d + 8 * g + 8]).T)
            Bm.append(f(Bs).T)
            BTt.append(f(Bs))
            CTt.append(f(Cs))
        hp = np.stack([dt_bias[:, 8 * g:8 * g + 8], a_log[:, 8 * g:8 * g + 8]], axis=1).reshape(1, 32).repeat(128, 0)
        c = np.ascontiguousarray
        in_maps.append({"xa": c(np.stack(xa)), "dtr": c(np.stack(dtr)), "Bm": c(np.stack(Bm)), "BT": c(np.stack(BTt)),
                        "CT": c(np.stack(CTt)), "hp": c(hp.astype(np.float32)), "msk": msk})
    res = _run(build_k2ob(), in_maps)
    yf = np.concatenate([res[g]["y"][0].T for g in range(NCORES)], axis=0)
    yr = np.concatenate([flipseg(res[g]["y"][1], 0).T for g in range(NCORES)], axis=0)
    return yf, yr


S5T = 256
S5N = TOK // S5T
PI = math.pi


def build_k2e(lam_init):
    pg = Prog()
    qk = pg.din("qk", [2, 2, 64, TOK])
    qkp = pg.din("qkp", [2, 2, 64, TOK])
    cs = pg.din("cs", [2, 64, TOK])
    v = pg.din("v", [TOK, 128])
    lamb = pg.din("lamb", [128, 256])
    sg = pg.din("sg", [128, 1])
    su = pg.din("su", [2, 128, TOK])
    Bblk = pg.din("Bblk", [128, 2 * 4 * 2 * 128])
    Cblk = pg.din("Cblk", [128, 2 * 4 * 2 * 128])
    s5p = pg.din("s5p", [128, 2 * 4 * 3])
    dsk = pg.din("dsk", [128, 1])
    oT = pg.dout("oT", [128, TOK])
    sf = pg.dout("sf", [128, TOK])
    sr = pg.dout("sr", [128, TOK])
    ones = setup_consts(pg)

    lt = pg.sb("lt", [128, 4, 64])
    sgt = pg.sb("sgt", [128, 1])
    dskt = pg.sb("dskt", [128, 1])
    lam2 = pg.sb("lam2", [128, 2, 64])
    lame = pg.sb("lame", [128, 2])
    nlam = pg.sb("nlam", [128, 1])
    pg.dma("sp", lt[:, :, :], lamb.rearrange("p (a b) -> p a b", a=4), w=["lt"])
    pg.dma("sp", sgt[:, :], sg[:, :], w=["sgt"])
    pg.dma("sp", dskt[:, :], dsk[:, :], w=["dskt"])
    pg.tt("dve", lam2[:, 0, :], lt[:, 0, :], lt[:, 1, :], ALU.mult, r=["lt"], w=["lam2"])
    pg.tt("dve", lam2[:, 1, :], lt[:, 2, :], lt[:, 3, :], ALU.mult, r=["lt"], w=["lam2"])
    pg.add("dve", lambda e: e.tensor_reduce(out=lame[:, :], in_=lam2[:, :, :], axis=AX.X, op=ALU.add), r=["lam2"], w=["lame"])
    pg.act(lame[:, :], lame[:, :], AF.Exp, r=["lame"], w=["lame"])
    pg.tt("dve", nlam[:, :], lame[:, 1:2], lame[:, 0:1], ALU.subtract, r=["lame"], w=["nlam"])
    pg.ts("dve", nlam[:, :], nlam[:, :], -lam_init, ALU.add, r=["nlam"], w=["nlam"])
    pg.ts("dve", sgt[:, :], sgt[:, :], 1.0 - lam_init, ALU.mult, r=["sgt"], w=["sgt"])

    Bb = pg.sb("Bb", [128, 16, 128], BF16)
    Cb = pg.sb("Cb", [128, 16, 128], BF16)
    pg.dma("pool", Bb[:, :, :], Bblk.rearrange("p (a n) -> p a n", n=128), w=["Bb"])
    pg.dma("pool", Cb[:, :, :], Cblk.rearrange("p (a n) -> p a n", n=128), w=["Cb"])
    subt = [(pg.sb("sub%d" % i, [128, S5T], BF16), "sub%d" % i) for i in range(3)]
    pt = pg.sb("pt", [128, 2, 4, 3])
    pg.dma("sp", pt[:, :, :, :], s5p.rearrange("p (d g f) -> p d g f", d=2, g=4), w=["pt"])
    W8 = [128, 2, 4]
    names = ["dt", "th", "rho", "m", "sn", "cn", "thc", "nre", "nim", "den", "kre", "kim", "t1", "t2"]
    sm = {n: pg.sb("s5_" + n, W8) for n in names}
    a3 = lambda n: sm[n][:, :, :]
    lre, lim, ldt = pt[:, :, :, 0], pt[:, :, :, 1], pt[:, :, :, 2]
    pg.act(a3("dt"), ldt, AF.Exp, r=["pt"], w=["s_dt"])
    pg.tt("dve", a3("th"), lim, a3("dt"), ALU.mult, r=["pt", "s_dt"], w=["s_th"])
    pg.tt("dve", a3("rho"), lre, a3("dt"), ALU.mult, r=["pt", "s_dt"], w=["s_rho"])
    pg.act(a3("rho"), a3("rho"), AF.Exp, r=["s_rho"], w=["s_rho"])
    for _ in range(4):
        pg.ts("dve", a3("m"), a3("th"), PI, ALU.is_gt, r=["s_th"], w=["s_m"])
        pg.stt(a3("th"), a3("m"), -2.0 * PI, a3("th"), ALU.mult, ALU.add, r=["s_m", "s_th"], w=["s_th"])
    pg.act(a3("sn"), a3("th"), AF.Sin, r=["s_th"], w=["s_sn"])
    pg.ts("dve", a3("thc"), a3("th"), PI / 2, ALU.add, r=["s_th"], w=["s_thc"])
    pg.ts("dve", a3("m"), a3("thc"), PI, ALU.is_gt, r=["s_thc"], w=["s_m"])
    pg.stt(a3("thc"), a3("m"), -2.0 * PI, a3("thc"), ALU.mult, ALU.add, r=["s_m", "s_thc"], w=["s_thc"])
    pg.act(a3("cn"), a3("thc"), AF.Sin, r=["s_thc"], w=["s_cn"])
    pg.tt("dve", a3("nre"), a3("rho"), a3("cn"), ALU.mult, r=["s_rho", "s_cn"], w=["s_nre"])
    pg.ts("dve", a3("nre"), a3("nre"), -1.0, ALU.add, r=["s_nre"], w=["s_nre"])
    pg.tt("dve", a3("nim"), a3("rho"), a3("sn"), ALU.mult, r=["s_rho", "s_sn"], w=["s_nim"])
    pg.tt("dve", a3("den"), lre, lre, ALU.mult, r=["pt"], w=["s_den"])
    pg.tt("dve", a3("t1"), lim, lim, ALU.mult, r=["pt"], w=["s_t1"])
    pg.tt("dve", a3("den"), a3("den"), a3("t1"), ALU.add, r=["s_den", "s_t1"], w=["s_den"])
    pg.recip(a3("den"), a3("den"), r=["s_den"], w=["s_den"])
    pg.tt("dve", a3("t1"), a3("nre"), lre, ALU.mult, r=["s_nre", "pt"], w=["s_t1"])
    pg.tt("dve", a3("t2"), a3("nim"), lim, ALU.mult, r=["s_nim", "pt"], w=["s_t2"])
    pg.tt("dve", a3("kre"), a3("t1"), a3("t2"), ALU.add, r=["s_t1", "s_t2"], w=["s_kre"])
    pg.tt("dve", a3("kre"), a3("kre"), a3("den"), ALU.mult, r=["s_kre", "s_den"], w=["s_kre"])
    pg.tt("dve", a3("t1"), a3("nim"), lre, ALU.mult, r=["s_nim", "pt"], w=["s_t1"])
    pg.tt("dve", a3("t2"), a3("nre"), lim, ALU.mult, r=["s_nre", "pt"], w=["s_t2"])
    pg.tt("dve", a3("kim"), a3("t1"), a3("t2"), ALU.subtract, r=["s_t1", "s_t2"], w=["s_kim"])
    pg.tt("dve", a3("kim"), a3("kim"), a3("den"), ALU.mult, r=["s_kim", "s_den"], w=["s_kim"])
    TS = [128, 2, 4, S5T]
    Ec, Es, Fre, Fim, Rho, Tmp, Tmp2 = [pg.sb("tab%d" % i, TS) for i in range(7)]
    pg.copy("dve", Ec[:, :, :, 0], a3("cn"), r=["s_cn"], w=["Ec"])
    pg.copy("dve", Es[:, :, :, 0], a3("sn"), r=["s_sn"], w=["Es"])
    mlen = 1
    while mlen < S5T:
        bc = lambda T_: T_[:, :, :, mlen - 1:mlen].to_broadcast([128, 2, 4, mlen])
        lo = lambda T_: T_[:, :, :, 0:mlen]
        hi = lambda T_: T_[:, :, :, mlen:2 * mlen]
        pg.tt("dve", lo(Tmp), lo(Ec), bc(Ec), ALU.mult, r=["Ec"], w=["Tmp"])
        pg.tt("dve", lo(Tmp2), lo(Es), bc(Es), ALU.mult, r=["Es"], w=["Tmp2"])
        pg.tt("dve", hi(Tmp), lo(Ec), bc(Es), ALU.mult, r=["Ec", "Es"], w=["Tmp"])
        pg.tt("dve", hi(Tmp2), lo(Es), bc(Ec), ALU.mult, r=["Ec", "Es"], w=["Tmp2"])
        pg.tt("dve", hi(Ec), lo(Tmp), lo(Tmp2), ALU.subtract, r=["Tmp", "Tmp2"], w=["Ec"])
        pg.tt("dve", hi(Es), hi(Tmp), hi(Tmp2), ALU.add, r=["Tmp", "Tmp2"], w=["Es"])
        mlen *= 2
    bk = lambda n: sm[n][:, :, :].unsqueeze(3).to_broadcast(TS)
    A4 = lambda T_: T_[:, :, :, :]
    pg.tt("dve", A4(Tmp), A4(Ec), bk("kre"), ALU.mult, r=["Ec", "s_kre"], w=["Tmp"])
    pg.tt("dve", A4(Tmp2), A4(Es), bk("kim"), ALU.mult, r=["Es", "s_kim"], w=["Tmp2"])
    pg.tt("dve", A4(Fre), A4(Tmp), A4(Tmp2), ALU.add, r=["Tmp", "Tmp2"], w=["Fre"])
    pg.tt("dve", A4(Tmp), A4(Es), bk("kre"), ALU.mult, r=["Es", "s_kre"], w=["Tmp"])
    pg.tt("dve", A4(Tmp2), A4(Ec), bk("kim"), ALU.mult, r=["Ec", "s_kim"], w=["Tmp2"])
    pg.tt("dve", A4(Fim), A4(Tmp), A4(Tmp2), ALU.subtract, r=["Tmp", "Tmp2"], w=["Fim"])
    pg.copy("dve", A4(Rho), bk("rho"), r=["s_rho"], w=["Rho"])

    carry = pg.sb("carry", [128, 2, 4, 2])
    pg.memset("dve", carry[:, :, :, :], 0.0, w=["carry%d%d" % (d, g) for d in range(2) for g in range(4)])
    NB = 2
    s5t = [[(pg.sb("s5w%d_%d" % (j, i), [128, S5T]), "s5w%d_%d" % (j, i)) for j in range(10)] for i in range(NB)]
    s5c = [[(pg.sb("s5c%d_%d" % (j, i), [128, S5T], BF16), "s5c%d_%d" % (j, i)) for j in range(2)] for i in range(NB)]
    s5u = [(pg.sb("s5u%d" % i, [128, S5T]), "s5u%d" % i) for i in range(2)]
    s5o = [(pg.sb("s5o%d" % i, [128, S5T]), "s5o%d" % i) for i in range(2)]
    p_raw = (pg.ps("praw", [128, 512]), "praw")
    p_y = (pg.ps("pys5", [128, 512]), "pys5")
    st5 = {"n": 0, "u": 0}

    def s5_step(d, ci):
        t0 = ci * S5T
        praw, prid = p_raw
        py, pyid = p_y
        sb_, sbid = subt[st5["u"] % 3]
        st5["u"] += 1
        pg.dma("pool", sb_[:, :], su[d, :, t0:t0 + S5T], w=[sbid])
        for g in range(4):
            i = st5["n"] % NB
            st5["n"] += 1
            W = s5t[i]
            cid = "carry%d%d" % (d, g)
            for comp in range(2):
                pg.mm(praw[:, comp * S5T:(comp + 1) * S5T], Bb[:, (d * 4 + g) * 2 + comp, :], sb_[:, :], True, True,
                      r=["Bb", sbid], w=[prid])
            rre, rim = praw[:, 0:S5T], praw[:, S5T:2 * S5T]
            fre, fim = Fre[:, d, g, :], Fim[:, d, g, :]
            ec, es = Ec[:, d, g, :], Es[:, d, g, :]
            (t1, i1), (t2, i2), (bre, ib), (bim, ibm), (gre, igr), (gim, igi), (u1, j1), (u2, j2), (cre, icr), (cim, ici) = W
            pg.tt("dve", t1[:, :], rre, fre, ALU.mult, r=[prid, "Fre"], w=[i1])
            pg.tt("dve", t2[:, :], rim, fim, ALU.mult, r=[prid, "Fim"], w=[i2])
            pg.tt("dve", bre[:, :], t1[:, :], t2[:, :], ALU.add, r=[i1, i2], w=[ib])
            pg.tt("dve", t1[:, :], rre, fim, ALU.mult, r=[prid, "Fim"], w=[i1])
            pg.tt("dve", t2[:, :], rim, fre, ALU.mult, r=[prid, "Fre"], w=[i2])
            pg.tt("dve", bim[:, :], t1[:, :], t2[:, :], ALU.subtract, r=[i1, i2], w=[ibm])
            init_re = 0.0 if ci == 0 else carry[:, d, g, 0:1]
            init_im = 0.0 if ci == 0 else carry[:, d, g, 1:2]
            pg.add("dve", lambda e, o=gre[:, :], a=Rho[:, d, g, :], b=bre[:, :], ini=init_re:
                   e.tensor_tensor_scan(out=o, data0=a, data1=b, initial=ini, op0=ALU.mult, op1=ALU.add), r=["Rho", ib, cid], w=[igr])
            pg.add("dve", lambda e, o=gim[:, :], a=Rho[:, d, g, :], b=bim[:, :], ini=init_im:
                   e.tensor_tensor_scan(out=o, data0=a, data1=b, initial=ini, op0=ALU.mult, op1=ALU.add), r=["Rho", ibm, cid], w=[igi])
            pg.tt("pool", u1[:, :], gre[:, :], ec, ALU.mult, r=[igr, "Ec"], w=[j1])
            pg.tt("pool", u2[:, :], gim[:, :], es, ALU.mult, r=[igi, "Es"], w=[j2])
            pg.tt("pool", cre[:, :], u1[:, :], u2[:, :], ALU.add, r=[j1, j2], w=[icr])
            pg.tt("pool", u1[:, :], gim[:, :], ec, ALU.mult, r=[igi, "Ec"], w=[j1])
            pg.tt("pool", u2[:, :], gre[:, :], es, ALU.mult, r=[igr, "Es"], w=[j2])
            pg.tt("pool", cim[:, :], u1[:, :], u2[:, :], ALU.subtract, r=[j1, j2], w=[ici])
            pg.copy("pool", carry[:, d, g, 0:1], cre[:, S5T - 1:S5T], r=[icr], w=[cid])
            pg.copy("pool", carry[:, d, g, 1:2], cim[:, S5T - 1:S5T], r=[ici], w=[cid])
            (cbr, icbr), (cbi, icbi) = s5c[i]
            pg.copy("act", cbr[:, :], cre[:, :], r=[icr], w=[icbr])
            pg.copy("act", cbi[:, :], cim[:, :], r=[ici], w=[icbi])
            pg.mm(py[:, 0:S5T], Cb[:, (d * 4 + g) * 2 + 0, :], cbr[:, :], g == 0, False, r=["Cb", icbr], w=[pyid])
            pg.mm(py[:, 0:S5T], Cb[:, (d * 4 + g) * 2 + 1, :], cbi[:, :], False, g == 3, r=["Cb", icbi], w=[pyid])
        o_, oid = s5o[d]
        if d == 0:
            u_, uid = s5u[ci % 2]
            pg.dma("sp", u_[:, :], su[0, :, t0:t0 + S5T], w=[uid])
            pg.stt(o_[:, :], u_[:, :], dskt[:, 0:1], py[:, 0:S5T], ALU.mult, ALU.add, r=[uid, "dskt", pyid], w=[oid])
            pg.dma("sp", sf[:, t0:t0 + S5T], o_[:, :], r=[oid], final=True)
        else:
            pg.copy("dve", o_[:, :], py[:, 0:S5T], r=[pyid], w=[oid])
            pg.dma("sp", sr[:, t0:t0 + S5T], o_[:, :], r=[oid], final=True)

    s5_sched = [(d, ci) for ci in range(S5N) for d in range(2)]

    KT = pg.sb("KT", [64, 2, TOK], BF16)
    V = pg.sb("V", [128, NCH, 128], BF16)
    vv = v.rearrange("(c p) e -> p c e", p=128)
    for q in range(3):
        pg.dma("pool", V[:, 22 * q:22 * q + 22, :], vv[:, 22 * q:22 * q + 22, :], w=["V"])
    RW = 528
    rt_ = [[(pg.sb("rp%d_%d" % (j, i), [64, RW]), "rp%d_%d" % (j, i)) for j in range(4)] for i in range(2)]
    rc_ = [[(pg.sb("rc%d_%d" % (j, i), [64, RW]), "rc%d_%d" % (j, i)) for j in range(2)] for i in range(2)]
    rn = {"n": 0, "c": 0}

    def rope(which, t0, n, dst_fn, dst_id):
        ci = rn["c"] % 2
        rn["c"] += 1
        (cc, cci), (ss, ssi) = rc_[ci]
        pg.dma("sp", cc[:, 0:n], cs[0, :, t0:t0 + n], w=[cci])
        pg.dma("sp", ss[:, 0:n], cs[1, :, t0:t0 + n], w=[ssi])
        for m in range(2):
            i = rn["n"] % 2
            rn["n"] += 1
            (x, xi), (xp, xpi), (a, ai), (b, bi) = rt_[i]
            pg.dma("sp", x[:, 0:n], qk[which, m, :, t0:t0 + n], w=[xi])
            pg.dma("sp", xp[:, 0:n], qkp[which, m, :, t0:t0 + n], w=[xpi])
            pg.tt("dve", a[:, 0:n], x[:, 0:n], cc[:, 0:n], ALU.mult, r=[xi, cci], w=[ai])
            pg.tt("dve", b[:, 0:n], xp[:, 0:n], ss[:, 0:n], ALU.mult, r=[xpi, ssi], w=[bi])
            pg.tt("dve", dst_fn(m), a[:, 0:n], b[:, 0:n], ALU.add, r=[ai, bi], w=[dst_id])

    for t in range(TOK // RW):
        rope(1, t * RW, RW, lambda m, t=t: KT[:, m, t * RW:(t + 1) * RW], "KT")

    QT = [(pg.sb("QT%d" % i, [64, 2, 512], BF16), "QT%d" % i) for i in range(2)]
    Pt = [(pg.sb("P%d" % i, [128, 512], BF16), "P%d" % i) for i in range(3)]
    p_s = [(pg.ps("pS%d" % i, [128, 512]), "pS%d" % i) for i in range(2)]
    p_o = [(pg.ps("pO%d" % i, [128, 512]), "pO%d" % i) for i in range(2)]
    p_z = [(pg.ps("pZ%d" % i, [128, 512]), "pZ%d" % i) for i in range(2)]
    rz = [(pg.sb("rz%d" % i, [128, 512]), "rz%d" % i) for i in range(2)]
    ot = [(pg.sb("ot%d" % i, [128, 512]), "ot%d" % i) for i in range(2)]
    o2 = (pg.sb("o2", [128, 512]), "o2")
    osq = (pg.sb("osq", [128, 512], BF16), "osq")
    ors = (pg.sb("ors", [128, 512]), "ors")
    qtiles = [(0, CTX, 0, 2)] + [(CTX + 512 * i, 512, 0, NCH) for i in range(16)]

    def qrope(qi):
        q0, nq, _, _ = qtiles[qi]
        Q, Qid = QT[qi % 2]
        rope(0, q0, nq, lambda m: Q[:, m, 0:nq], Qid)

    qrope(0)
    cnt = {"s": 0, "p": 0}
    s5i = 0
    for qi, (q0, nq, k0, k1) in enumerate(qtiles):
        if qi + 1 < len(qtiles):
            qrope(qi + 1)
        Q, Qid = QT[qi % 2]
        for m in range(2):
            pO, pOid = p_o[m]
            pZ, pZid = p_z[m]
            for kc in range(k0, k1):
                pS, pSid = p_s[cnt["s"] % 2]
                cnt["s"] += 1
                P, Pid = Pt[cnt["p"] % 3]
                cnt["p"] += 1
                pg.mm(pS[:, 0:nq], KT[:, m, kc * 128:(kc + 1) * 128], Q[:, m, 0:nq], True, True, r=["KT", Qid], w=[pSid])
                pg.act(P[:, 0:nq], pS[:, 0:nq], AF.Exp, r=[pSid], w=[Pid], scale=0.125)
                pg.mm(pO[:, 0:nq], V[:, kc, :], P[:, 0:nq], kc == k0, kc == k1 - 1, r=["V", Pid], w=[pOid])
                pg.mm(pZ[:, 0:nq], ones[:, :], P[:, 0:nq], kc == k0, kc == k1 - 1, r=["ones", Pid], w=[pZid])
        o_, oid = ot[qi % 2]
        for m in range(2):
            pg.recip(rz[m][0][:, 0:nq], p_z[m][0][:, 0:nq], r=[p_z[m][1]], w=[rz[m][1]])
        pg.tt("dve", o_[:, 0:nq], p_o[0][0][:, 0:nq], rz[0][0][:, 0:nq], ALU.mult, r=[p_o[0][1], rz[0][1]], w=[oid])
        pg.tt("dve", o2[0][:, 0:nq], p_o[1][0][:, 0:nq], rz[1][0][:, 0:nq], ALU.mult, r=[p_o[1][1], rz[1][1]], w=[o2[1]])
        pg.stt(o_[:, 0:nq], o2[0][:, 0:nq], nlam[:, 0:1], o_[:, 0:nq], ALU.mult, ALU.add, r=[o2[1], oid, "nlam"], w=[oid])
        pg.act(osq[0][:, 0:nq], o_[:, 0:nq], AF.Square, r=[oid], w=[osq[1]])
        pS, pSid = p_s[cnt["s"] % 2]
        cnt["s"] += 1
        pg.mm(pS[:, 0:nq], ones[:, :], osq[0][:, 0:nq], True, True, r=["ones", osq[1]], w=[pSid])
        pg.act(ors[0][:, 0:nq], pS[:, 0:nq], AF.Sqrt, r=[pSid], w=[ors[1]], bias=pg.eps_ap, scale=1.0 / 128)
        pg.recip(ors[0][:, 0:nq], ors[0][:, 0:nq], r=[ors[1]], w=[ors[1]])
        pg.tt("dve", o_[:, 0:nq], o_[:, 0:nq], ors[0][:, 0:nq], ALU.mult, r=[oid, ors[1]], w=[oid])
        pg.ts("dve", o_[:, 0:nq], o_[:, 0:nq], sgt[:, 0:1], ALU.mult, r=[oid, "sgt"], w=[oid])
        pg.dma("sp", oT[:, q0:q0 + nq], o_[:, 0:nq], r=[oid], final=True)
        nstep = 4 if qi > 0 else 2
        for _ in range(nstep):
            if s5i < len(s5_sched):
                s5_step(*s5_sched[s5i])
                s5i += 1
    while s5i < len(s5_sched):
        s5_step(*s5_sched[s5i])
        s5i += 1
    return pg


def rope_tables():
    rows = SEQ // 64
    row = np.repeat(np.arange(rows, dtype=np.float32), 64)
    col = np.tile(np.arange(64, dtype=np.float32), rows)
    inv = (10000.0 ** (-np.arange(0, 32, 2, dtype=np.float32) / 32)).astype(np.float32)
    ang_r = row[:, None] * inv
    ang_c = col[:, None] * inv
    ang = np.concatenate([ang_r, ang_r, ang_c, ang_c], axis=-1)
    cos = np.cos(ang).astype(np.float32)
    sin = np.sin(ang).astype(np.float32)
    sgn = np.concatenate([-np.ones(16), np.ones(16), -np.ones(16), np.ones(16)]).astype(np.float32)
    cosT = np.concatenate([np.ones((64, CTX), np.float32), cos.T], axis=1)
    sinT = np.concatenate([np.zeros((64, CTX), np.float32), (sin * sgn[None, :]).T], axis=1)
    return np.ascontiguousarray(np.stack([cosT, sinT]))


ROPE_PERM = np.concatenate([np.arange(16, 32), np.arange(0, 16), np.arange(48, 64), np.arange(32, 48)])


def k2e_inputs(pT, j, P, cores=range(NCORES)):
    cs = rope_tables()
    c = np.ascontiguousarray
    maps = []
    for k in cores:
        q = np.stack([pT[m * 512 + k * 64:m * 512 + k * 64 + 64] for m in range(2)])
        kk = np.stack([pT[1024 + m * 512 + k * 64:1024 + m * 512 + k * 64 + 64] for m in range(2)])
        qk = np.stack([q, kk])
        qkp = qk[:, :, ROPE_PERM, :]
        v = pT[2048 + 128 * k:2048 + 128 * (k + 1)].T
        s = pT[3072 + 128 * k:3072 + 128 * (k + 1)]
        su = np.stack([s, flipseg(s, 1)])
        Bblk = np.zeros((128, 2, 4, 2, 128), np.float32)
        Cblk = np.zeros((128, 2, 4, 2, 128), np.float32)
        s5p = np.zeros((128, 2, 4, 3), np.float32)
        for d in range(2):
            for gp in range(4):
                for gi in range(2):
                    gl = 2 * gp + gi
                    g = 8 * k + gl
                    for comp, (bn, cn) in enumerate([("s5_b_re", "s5_c_re"), ("s5_b_im", "s5_c_im")]):
                        Bblk[gl * 16:(gl + 1) * 16, d, gp, comp, gi * 64:(gi + 1) * 64] = P[bn][j, d, g].T
                        Cblk[gi * 64:(gi + 1) * 64, d, gp, comp, gl * 16:(gl + 1) * 16] = P[cn][j, d, g].T
                    s5p[gi * 64:(gi + 1) * 64, d, gp, 0] = P["s5_lam_re"][j, d, g]
                    s5p[gi * 64:(gi + 1) * 64, d, gp, 1] = P["s5_lam_im"][j, d, g]
                    s5p[gi * 64:(gi + 1) * 64, d, gp, 2] = P["s5_log_dt"][j, d, g]
        dsk = P["s5_d"][j, 8 * k:8 * k + 8].reshape(128, 1)
        maps.append({"qk": c(qk), "qkp": c(qkp), "cs": cs, "v": c(v),
                     "lamb": c(P["diff_lam"][j].reshape(1, 256).repeat(128, 0)), "sg": c(P["diff_subln"][j].reshape(128, 1)),
                     "su": c(su), "Bblk": c(Bblk.reshape(128, -1)), "Cblk": c(Cblk.reshape(128, -1)),
                     "s5p": c(s5p.reshape(128, -1)), "dsk": c(dsk.astype(np.float32))})
    return maps


def run_k2e(pT, j, lam_init, P):
    res = _run(build_k2e(lam_init), k2e_inputs(pT, j, P))
    oT = np.concatenate([res[k]["oT"] for k in range(NCORES)], axis=0)
    sfT = np.concatenate([res[k]["sf"] for k in range(NCORES)], axis=0)
    srT = np.concatenate([flipseg(res[k]["sr"], 1) for k in range(NCORES)], axis=0)
    return oT, sfT, srT


def kernel(**inp):
    P = {k: np.asarray(v, dtype=np.float32) for k, v in inp.items()}
    mod = run_k0(P["c"][0], P["c_ctx"], P["w_mod"], P["b_mod"])
    hT = np.ascontiguousarray(np.concatenate([P["ctx"][0], P["x"][0]], axis=0).T)
    for i in range(4):
        j = i // 2
        g = P["norm_g"][i]
        ml, mc = mod[i, 0], mod[i, 1]
        if i % 2 == 0:
            lam_init = 0.8 - 0.6 * math.exp(-0.3 * i)
            pT = run_k1(hT, g[0], ml, mc, P["w_in_even"][j])
            oT, sfT, srT = run_k2e(pT, j, lam_init, P)
            bg = np.ascontiguousarray(P["s5_b_glu"][j].reshape(8, 128).T)
            h1T, u2T = run_k3a(True, hT, {"oT": oT, "sfT": sfT, "srT": srT}, g, ml, mc, P["w_out_even"][j],
                               {"wglu": P["s5_w_glu"][j], "bglu": bg})
        else:
            pT = run_k1(hT, g[0], ml, mc, P["w_in_odd"][j])
            zT = pT[0:4096]
            xbcaT = run_k2oa(pT[4096:4096 + 6144], P["conv_w"][j], P["conv_b"][j])
            yfT, yrT = run_k2ob(xbcaT, pT[10240:10368], P["ssd_dt_bias"][j], P["ssd_a_log"][j])
            dexp = np.repeat(P["ssd_d"][j], 64)
            nw = np.concatenate([P["ssd_norm_w"][j].reshape(32, 128).T, dexp.reshape(32, 128).T], axis=1)
            h1T, u2T = run_k3a(False, hT, {"yfT": yfT, "yrT": yrT, "xT": xbcaT[0:4096], "zT": zT}, g, ml, mc,
                               P["w_out_odd"][j], {"nw": np.ascontiguousarray(nw.astype(np.float32))})
        hT = run_k3b(u2T, h1T, g[3], ml, mc, P["w_up"][i], P["w_down"][i])
    return np.ascontiguousarray(hT[:, CTX:].T)[None].astype(np.float32)
```

```python
import math
import numpy as np
import concourse.bass as bass
import concourse.mybir as mybir
from concourse.bass_utils import run_bass_kernel_spmd

F32 = mybir.dt.float32
BF16 = mybir.dt.bfloat16
AF = mybir.ActivationFunctionType
ALU = mybir.AluOpType
AX = mybir.AxisListType

NCORES = 8
D = 2048
SEQ = 8192
CTX = 256
TOK = SEQ + CTX
TPC = 1056
LAT = 1024
TT = 352
NTT = 3
EPS = 1e-6


class _Op:
    __slots__ = ("eng", "fn", "deps", "needs_inc", "count", "sem", "dma", "idx")

    def __init__(self, eng, fn, dma):
        self.eng = eng
        self.fn = fn
        self.deps = []
        self.needs_inc = False
        self.count = 0
        self.sem = None
        self.dma = dma


ENGS = ("pe", "act", "dve", "pool", "sp")
N_DMA_SEMS = 24


class Prog:
    def __init__(self):
        self.nc = bass.Bass("TRN2", target_bir_lowering=False)
        self.ops = {e: [] for e in ENGS}
        self.lastw = {}
        self.readers = {}
        self.sems = {e: self.nc.alloc_semaphore("s_" + e) for e in ENGS if e != "sp"}
        self.dsems = [self.nc.alloc_semaphore("d%d" % i) for i in range(N_DMA_SEMS)]
        self.dcount = [0] * N_DMA_SEMS
        self.dlast = [None] * N_DMA_SEMS
        self.dnext = 0
        self.out_dmas = []
        self.uid = 0

    def din(self, name, shape, dt=F32):
        return self.nc.dram_tensor(name, list(shape), dt, kind="ExternalInput").ap()

    def dout(self, name, shape, dt=F32):
        return self.nc.dram_tensor(name, list(shape), dt, kind="ExternalOutput").ap()

    def sb(self, name, shape, dt=F32):
        return self.nc.alloc_sbuf_tensor(name, list(shape), dt)

    def ps(self, name, shape, dt=F32):
        return self.nc.alloc_psum_tensor(name, list(shape), dt)

    def add(self, eng, fn, r=(), w=(), dma=False, out=False):
        op = _Op(eng, fn, dma)
        deps = []
        seen = set()

        def push(d):
            if d is None or id(d) in seen or d is op:
                return
            seen.add(id(d))
            deps.append(d)

        for b in r:
            push(self.lastw.get(b))
        for b in w:
            push(self.lastw.get(b))
            for rd in self.readers.get(b, {}).values():
                push(rd)
        if dma:
            k = self.dnext
            self.dnext = (k + 1) % N_DMA_SEMS
            push(self.dlast[k])
            self.dlast[k] = op
            self.dcount[k] += 16
            op.sem = self.dsems[k]
            op.count = self.dcount[k]
            op.needs_inc = True
        final = []
        for d in deps:
            if (not d.dma) and d.eng == "pe" and eng == "pe" and not dma:
                continue
            d.needs_inc = True
            final.append(d)
        op.deps = final
        for b in r:
            key = ("dma", id(op)) if dma else eng
            self.readers.setdefault(b, {})[key] = op
        for b in w:
            self.lastw[b] = op
            self.readers[b] = {}
        self.ops[eng].append(op)
        if out:
            self.out_dmas.append(op)
        return op

    def finish(self):
        fin = _Op("sp", None, False)
        fin.deps = list(self.out_dmas)
        self.ops["sp"].append(fin)
        for e in ENGS:
            c = 0
            for op in self.ops[e]:
                if op.dma or op.fn is None:
                    continue
                if op.needs_inc:
                    c += 1
                    op.count = c
                    op.sem = self.sems[e]
        nc = self.nc
        progs = self.ops

        def emit(e, ops):
            waited = {}
            for op in ops:
                for d in op.deps:
                    key = id(d.sem)
                    if waited.get(key, 0) < d.count:
                        e.wait_ge(d.sem, d.count)
                        waited[key] = d.count
                if op.fn is None:
                    continue
                ins = op.fn(e)
                if op.dma:
                    ins.then_inc(op.sem, 16)
                elif op.needs_inc:
                    ins.then_inc(op.sem, 1)

        with nc.Block() as block:
            @block.tensor
            def _(e):
                emit(e, progs["pe"])

            @block.scalar
            def _(e):
                emit(e, progs["act"])

            @block.vector
            def _(e):
                emit(e, progs["dve"])

            @block.gpsimd
            def _(e):
                emit(e, progs["pool"])

            @block.sync
            def _(e):
                emit(e, progs["sp"])
        return nc

    def dma(self, q, out, in_, r=(), w=(), final=False):
        return self.add(q, lambda e, o=out, i=in_: e.dma_start(out=o, in_=i), r=r, w=w, dma=True, out=final)

    def mm(self, out, lhsT, rhs, start, stop, r=(), w=()):
        return self.add("pe", lambda e, o=out, l=lhsT, rr=rhs, s=start, t=stop: e.matmul(o, l, rr, start=s, stop=t), r=r, w=w)

    def act(self, out, in_, func, r=(), w=(), bias=None, scale=None, eng="act"):
        def fn(e, o=out, i=in_, f=func, b=bias, sc=scale):
            kw = {}
            if b is not None:
                kw["bias"] = b
            if sc is not None:
                kw["scale"] = sc
            return e.activation(out=o, in_=i, func=f, **kw)
        return self.add("act", fn, r=r, w=w)

    def tt(self, eng, out, in0, in1, op, r=(), w=()):
        return self.add(eng, lambda e, o=out, a=in0, b=in1, p=op: e.tensor_tensor(out=o, in0=a, in1=b, op=p), r=r, w=w)

    def ts(self, eng, out, in0, s1, op0, s2=None, op1=None, r=(), w=()):
        def fn(e, o=out, a=in0, x1=s1, x2=s2, p0=op0, p1=op1):
            if p1 is None:
                return e.tensor_scalar(out=o, in0=a, scalar1=x1, scalar2=None, op0=p0)
            return e.tensor_scalar(out=o, in0=a, scalar1=x1, scalar2=x2, op0=p0, op1=p1)
        return self.add(eng, fn, r=r, w=w)

    def stt(self, out, in0, scalar, in1, op0, op1, r=(), w=()):
        return self.add("dve", lambda e, o=out, a=in0, s=scalar, b=in1, p0=op0, p1=op1:
                        e.scalar_tensor_tensor(out=o, in0=a, scalar=s, in1=b, op0=p0, op1=p1), r=r, w=w)

    def copy(self, eng, out, in_, r=(), w=()):
        if eng == "act":
            return self.add("act", lambda e, o=out, i=in_: e.copy(out=o, in_=i), r=r, w=w)
        return self.add(eng, lambda e, o=out, i=in_: e.tensor_copy(out=o, in_=i), r=r, w=w)

    def memset(self, eng, ap, val, w=()):
        return self.add(eng, lambda e, a=ap, v=val: e.memset(a, v), w=w)

    def recip(self, out, in_, r=(), w=()):
        return self.add("dve", lambda e, o=out, i=in_: e.reciprocal(out=o, in_=i), r=r, w=w)


_TRACE = {"on": False, "log": []}


def _run(pg, in_maps):
    nc = pg.finish()
    if _TRACE["on"]:
        res = run_bass_kernel_spmd(nc, in_maps, core_ids=list(range(NCORES)), trace=True)
        _TRACE["log"].append(res.exec_time_ns)
        print("EXEC_NS", res.exec_time_ns, {e: len(pg.ops[e]) for e in ENGS}, flush=True)
    else:
        res = run_bass_kernel_spmd(nc, in_maps, core_ids=list(range(NCORES)))
    return res.results


def build_k0():
    pg = Prog()
    NJ = 12
    cc = pg.din("cc", [128, 32])
    wm = pg.din("wm", [4, 2048, 1536])
    bm = pg.din("bm", [128, 4 * NJ])
    o = pg.dout("modT", [128, 4 * NJ * 2])
    cct = pg.sb("cct", [128, 32])
    sc = pg.sb("sc", [128, 32])
    bmt = pg.sb("bmt", [128, 4 * NJ])
    ot = pg.sb("ot", [128, 4 * NJ * 2])
    wt = [pg.sb("wt%d" % i, [128, 16, 768]) for i in range(2)]
    pst = [pg.ps("ps%d" % i, [128, 512]) for i in range(2)]
    pg.dma("sp", cct[:], cc[:, :], w=["cct"])
    pg.dma("sp", bmt[:], bm[:, :], w=["bmt"])
    pg.act(sc[:], cct[:], AF.Silu, r=["cct"], w=["sc"])
    n = 0
    for i in range(4):
        for hf in range(2):
            b = n % 2
            wv = wm[i, :, hf * 768:(hf + 1) * 768].rearrange("(dc p) n -> p dc n", p=128)
            for q4 in range(4):
                pg.dma("sp", wt[b][:, 4 * q4:4 * q4 + 4, :], wv[:, 4 * q4:4 * q4 + 4, :], w=["wt%d_%d" % (b, q4)])
            for jj in range(6):
                j = hf * 6 + jj
                pb = (i * NJ + j) % 2
                for dc in range(16):
                    pg.mm(pst[pb][:, 0:2], wt[b][:, dc, jj * 128:(jj + 1) * 128], sc[:, 2 * dc:2 * dc + 2],
                          dc == 0, dc == 15, r=["wt%d_%d" % (b, dc // 4), "sc"], w=["ps%d" % pb])
                col = (i * NJ + j)
                pg.ts("dve", ot[:, 2 * col:2 * col + 2], pst[pb][:, 0:2], bmt[:, col:col + 1], ALU.add,
                      r=["ps%d" % pb, "bmt"], w=["ot"])
            n += 1
    pg.dma("sp", o[:, :], ot[:], r=["ot"], final=True)
    return pg


def run_k0(c, c_ctx, w_mod, b_mod):
    cc = np.stack([c.reshape(16, 128), c_ctx.reshape(16, 128)], axis=-1)
    cc = np.ascontiguousarray(cc.transpose(1, 0, 2).reshape(128, 32))
    in_maps = []
    for k in range(NCORES):
        wm = np.ascontiguousarray(w_mod[:, :, 1536 * k:1536 * (k + 1)])
        bm = b_mod[:, 1536 * k:1536 * (k + 1)].reshape(4, 12, 128).transpose(2, 0, 1).reshape(128, 48)
        in_maps.append({"cc": cc, "wm": wm, "bm": np.ascontiguousarray(bm)})
    res = _run(build_k0(), in_maps)
    mod = np.zeros((4, 2, 12288), np.float32)
    for k in range(NCORES):
        r = res[k]["modT"].reshape(128, 4, 12, 2)
        mod[:, :, 1536 * k:1536 * (k + 1)] = r.transpose(1, 3, 2, 0).reshape(4, 2, 1536)
    return mod


class WStream:
    def __init__(self, pg, name, KC, gw=512, nbuf=3):
        self.pg, self.name, self.KC, self.gw, self.nbuf = pg, name, KC, gw, nbuf
        self.bufs = [pg.sb("%s_w%d" % (name, i), [128, KC, gw], BF16) for i in range(nbuf)]
        self.n = 0

    def load(self, wap, c0, cols):
        b = self.n % self.nbuf
        self.n += 1
        wv = wap[:, c0:c0 + cols].rearrange("(kc p) n -> p kc n", p=128)
        step = 4
        for q in range(0, self.KC, step):
            self.pg.dma("pool", self.bufs[b][:, q:q + step, 0:cols], wv[:, q:q + step, :],
                        w=["%s_w%d_%d" % (self.name, b, q // step)])
        return b

    def rid(self, b, kc):
        return "%s_w%d_%d" % (self.name, b, kc // 4)


def linear_fm(pg, ws, wap, N, X, xid, KC, pss, evac, col_ranges=None):
    ng = (N + ws.gw - 1) // ws.gw
    cnt = 0
    for g in range(ng):
        cols = min(ws.gw, N - g * ws.gw)
        b = ws.load(wap, g * ws.gw, cols)
        for ocl in range(cols // 128):
            oc = g * (ws.gw // 128) + ocl
            for tt in range(NTT):
                pb = cnt % len(pss)
                cnt += 1
                ps, psid = pss[pb]
                for kc in range(KC):
                    pg.mm(ps[:, 0:TT], ws.bufs[b][:, kc, ocl * 128:(ocl + 1) * 128], X[:, kc, tt * TT:(tt + 1) * TT],
                          kc == 0, kc == KC - 1, r=[ws.rid(b, kc), xid(kc)], w=[psid])
                evac(oc, tt, ps[:, 0:TT], psid)


def sumsq_rstd(pg, name, src_fn, src_id, KC, dim, ones, sq_bufs, ss_ps, rstd, eng_sq="act"):
    for c in range(KC):
        sb_, sid = sq_bufs[c % len(sq_bufs)]
        pg.act(sb_[:, :], src_fn(c), AF.Square, r=[src_id(c)], w=[sid])
        for tt in range(NTT):
            ps, psid = ss_ps[tt]
            pg.mm(ps[:, 0:TT], ones[:, :], sb_[:, tt * TT:(tt + 1) * TT], c == 0, c == KC - 1, r=[sid, "ones"], w=[psid])
    for tt in range(NTT):
        ps, psid = ss_ps[tt]
        pg.act(rstd[:, tt * TT:(tt + 1) * TT], ps[:, 0:TT], AF.Sqrt, r=[psid], w=[name + "_rs%d" % tt],
               bias=pg.eps_ap, scale=1.0 / dim)
        pg.recip(rstd[:, tt * TT:(tt + 1) * TT], rstd[:, tt * TT:(tt + 1) * TT], r=[name + "_rs%d" % tt], w=[name + "_rs%d" % tt])
    return [name + "_rs%d" % tt for tt in range(NTT)]


def setup_consts(pg):
    ones = pg.sb("ones", [128, 128], BF16)
    pg.memset("dve", ones[:, :], 1.0, w=["ones"])
    epst = pg.sb("epst", [128, 1])
    pg.memset("dve", epst[:, :], EPS, w=["eps"])
    pg.eps_ap = epst[:, 0:1]
    return ones


def build_k1(N):
    pg = Prog()
    hT = pg.din("hT", [D, TPC])
    mp = pg.din("mp", [128, 16 * 5])
    w = pg.din("w", [D, N])
    pT = pg.dout("pT", [N, TPC])
    ones = setup_consts(pg)
    H = pg.sb("H", [128, 16, TPC])
    U = pg.sb("U", [128, 16, TPC], BF16)
    mpt = pg.sb("mpt", [128, 16, 5])
    AB = pg.sb("AB", [128, 16, 2])
    rstd = pg.sb("rstd", [128, TPC])
    sqb = [(pg.sb("sq%d" % i, [128, TPC], BF16), "sq%d" % i) for i in range(2)]
    tmpb = [(pg.sb("tmp%d" % i, [128, TPC]), "tmp%d" % i) for i in range(2)]
    Ob = [(pg.sb("O%d" % i, [128, TPC]), "O%d" % i) for i in range(3)]
    ssp = [(pg.ps("ss%d" % i, [128, 512]), "ss%d" % i) for i in range(3)]
    mmp = [(pg.ps("mm%d" % i, [128, 512]), "mm%d" % i) for i in range(4)]
    ws = WStream(pg, "win", 16)

    hv = hT.rearrange("(c p) t -> p c t", p=128)
    for q in range(4):
        pg.dma("sp", H[:, 4 * q:4 * q + 4, :], hv[:, 4 * q:4 * q + 4, :], w=["H%d" % q])
    pg.dma("sp", mpt[:, :, :], mp.rearrange("p (c f) -> p c f", f=5), w=["mpt"])
    rs_ids = sumsq_rstd(pg, "n0", lambda c: H[:, c, :], lambda c: "H%d" % (c // 4), 16, D, ones, sqb, ssp, rstd)
    pg.stt(AB[:, :, 0], mpt[:, :, 2], 1.0, mpt[:, :, 0], ALU.add, ALU.mult, r=["mpt"], w=["AB"])
    pg.stt(AB[:, :, 1], mpt[:, :, 4], 1.0, mpt[:, :, 0], ALU.add, ALU.mult, r=["mpt"], w=["AB"])
    for c in range(16):
        tb, tid = tmpb[c % 2]
        pg.tt("dve", tb[:, :], H[:, c, :], rstd[:, :], ALU.mult, r=["H%d" % (c // 4)] + rs_ids, w=[tid])
        pg.act(U[:, c, 0:LAT], tb[:, 0:LAT], AF.Identity, r=[tid, "AB", "mpt"], w=["U%d" % c],
               bias=mpt[:, c, 1:2], scale=AB[:, c, 0:1])
        pg.act(U[:, c, LAT:TPC], tb[:, LAT:TPC], AF.Identity, r=[tid, "AB", "mpt"], w=["U%d" % c],
               bias=mpt[:, c, 3:4], scale=AB[:, c, 1:2])
    state = {"n": 0}

    def evac(oc, tt, ps, psid):
        ob, oid = Ob[oc % 3]
        eng = "act" if (state["n"] % 2 == 0) else "dve"
        state["n"] += 1
        pg.copy(eng, ob[:, tt * TT:(tt + 1) * TT], ps, r=[psid], w=[oid + "_%d" % tt])
        if tt == NTT - 1:
            pg.dma("sp", pT[oc * 128:(oc + 1) * 128, :], ob[:, :], r=[oid + "_%d" % t for t in range(NTT)], final=True)

    linear_fm(pg, ws, w, N, U, lambda kc: "U%d" % kc, 16, mmp, evac)
    return pg


def tok_cols(k):
    return np.concatenate([CTX + np.arange(LAT * k, LAT * (k + 1)), np.arange(32 * k, 32 * (k + 1))])


def shard_T(aT):
    return [np.ascontiguousarray(aT[:, tok_cols(k)]) for k in range(NCORES)]


def unshard_T(parts):
    F = parts[0].shape[0]
    out = np.empty((F, TOK), parts[0].dtype)
    for k in range(NCORES):
        out[:, tok_cols(k)] = parts[k]
    return out


def pack_cols(vecs):
    a = np.stack([v.reshape(16, 128) for v in vecs], axis=-1)
    return np.ascontiguousarray(a.transpose(1, 0, 2).reshape(128, -1)).astype(np.float32)


def run_k1(hT, g0, mod_l, mod_c, w_in):
    N = w_in.shape[1]
    m = lambda v, i: v[i * D:(i + 1) * D]
    mp = pack_cols([g0, m(mod_l, 0), m(mod_l, 1), m(mod_c, 0), m(mod_c, 1)])
    hs = shard_T(hT)
    in_maps = [{"hT": hs[k], "mp": mp, "w": w_in} for k in range(NCORES)]
    res = _run(build_k1(N), in_maps)
    return unshard_T([res[k]["pT"] for k in range(NCORES)])


def residual_tail(pg, name, Y, yid, rstd, rs_ids, coef, coef_id, h_dram, out_dram, scr, inplace=False):
    hb = scr[0:2]
    tb = scr[2:4]
    ob = scr[4:6]
    for c in range(16):
        h_, hid = hb[c % 2]
        t_, tid = tb[c % 2]
        if inplace:
            oo, oid = Y[:, c, :], yid(c)
        else:
            o_, oid = ob[c % 2]
            oo = o_[:, :]
        pg.dma("sp", h_[:, :], h_dram[c * 128:(c + 1) * 128, :], w=[hid])
        pg.tt("dve", t_[:, :], Y[:, c, :], rstd[:, :], ALU.mult, r=[yid(c)] + rs_ids, w=[tid])
        pg.stt(oo[:, 0:LAT], t_[:, 0:LAT], coef[:, c, 0:1], h_[:, 0:LAT], ALU.mult, ALU.add, r=[tid, hid, coef_id], w=[oid])
        pg.stt(oo[:, LAT:TPC], t_[:, LAT:TPC], coef[:, c, 1:2], h_[:, LAT:TPC], ALU.mult, ALU.add, r=[tid, hid, coef_id], w=[oid])
        pg.dma("sp", out_dram[c * 128:(c + 1) * 128, :], oo, r=[oid], final=True)


def build_k3b():
    pg = Prog()
    uT = pg.din("uT", [D, TPC])
    hT = pg.din("hT", [D, TPC])
    mp = pg.din("mp", [128, 16 * 3])
    wu = pg.din("wu", [D, 4 * D])
    wd = pg.din("wd", [4 * D, D])
    oT = pg.dout("oT", [D, TPC])
    ones = setup_consts(pg)
    U = pg.sb("U", [128, 16, TPC], BF16)
    HQ = pg.sb("HQ", [128, 8, TPC], BF16)
    Y = pg.sb("Y", [128, 16, TPC])
    mpt = pg.sb("mpt", [128, 16, 3])
    coef = pg.sb("coef", [128, 16, 2])
    rstd = pg.sb("rstd", [128, TPC])
    rt = [(pg.sb("rt%d" % i, [128, TT]), "rt%d" % i) for i in range(3)]
    sqb = [(pg.sb("sq%d" % i, [128, TPC], BF16), "sq%d" % i) for i in range(2)]
    ssp = [(pg.ps("ss%d" % i, [128, 512]), "ss%d" % i) for i in range(3)]
    mmp = [(pg.ps("mm%d" % i, [128, 512]), "mm%d" % i) for i in range(4)]
    wsu = WStream(pg, "wu", 16, gw=256)
    wsd = WStream(pg, "wd", 8, gw=512)
    uv = uT.rearrange("(c p) t -> p c t", p=128)
    for q in range(4):
        pg.dma("pool", U[:, 4 * q:4 * q + 4, :], uv[:, 4 * q:4 * q + 4, :], w=["U%d" % c for c in range(4 * q, 4 * q + 4)])
    pg.dma("sp", mpt[:, :, :], mp.rearrange("p (c f) -> p c f", f=3), w=["mpt"])
    pg.tt("dve", coef[:, :, 0], mpt[:, :, 0], mpt[:, :, 1], ALU.mult, r=["mpt"], w=["coef"])
    pg.tt("dve", coef[:, :, 1], mpt[:, :, 0], mpt[:, :, 2], ALU.mult, r=["mpt"], w=["coef"])
    st = {"n": 0}
    for e in range(8):
        def evac_up(oc, tt, ps, psid):
            r_, rid = rt[st["n"] % 3]
            st["n"] += 1
            pg.act(r_[:, :], ps, AF.Relu, r=[psid], w=[rid])
            pg.tt("dve", HQ[:, oc, tt * TT:(tt + 1) * TT], r_[:, :], r_[:, :], ALU.mult, r=[rid], w=["HQ%d" % oc])

        linear_fm(pg, wsu, wu[:, e * 1024:(e + 1) * 1024], 1024, U, lambda kc: "U%d" % kc, 16, mmp, evac_up)

        def evac_dn(oc, tt, ps, psid, e=e):
            dst = Y[:, oc, tt * TT:(tt + 1) * TT]
            if e == 0:
                pg.copy("act", dst, ps, r=[psid], w=["Y%d" % oc])
            else:
                pg.tt("dve", dst, ps, dst, ALU.add, r=[psid, "Y%d" % oc], w=["Y%d" % oc])

        linear_fm(pg, wsd, wd[e * 1024:(e + 1) * 1024, :], D, HQ, lambda kc: "HQ%d" % kc, 8, mmp, evac_dn)
    rs_ids = sumsq_rstd(pg, "n3", lambda c: Y[:, c, :], lambda c: "Y%d" % c, 16, D, ones, sqb, ssp, rstd)
    scr = [(pg.sb("scr%d" % i, [128, TPC]), "scr%d" % i) for i in range(6)]
    residual_tail(pg, "rt", Y, lambda c: "Y%d" % c, rstd, rs_ids, coef, "coef", hT, oT, scr)
    return pg


def run_k3b(u2T, h1T, g3, mod_l, mod_c, w_up, w_down):
    m = lambda v, i: v[i * D:(i + 1) * D]
    mp = pack_cols([g3, m(mod_l, 5), m(mod_c, 5)])
    us, hs = shard_T(u2T), shard_T(h1T)
    in_maps = [{"uT": us[k], "hT": hs[k], "mp": mp, "wu": w_up, "wd": w_down} for k in range(NCORES)]
    res = _run(build_k3b(), in_maps)
    return unshard_T([res[k]["oT"] for k in range(NCORES)])


GELU_C = 0.044715
GELU_S = 2.0 * math.sqrt(2.0 / math.pi)


def build_k3a(even):
    pg = Prog()
    hT = pg.din("hT", [D, TPC])
    mp = pg.din("mp", [128, 16 * 8])
    KC = 16 if even else 32
    wout = pg.din("wout", [KC * 128, D])
    h1T = pg.dout("h1T", [D, TPC])
    u2T = pg.dout("u2T", [D, TPC])
    ones = setup_consts(pg)
    M = pg.sb("M", [128, KC, TPC], BF16)
    Y = pg.sb("Y", [128, 16, TPC])
    mpt = pg.sb("mpt", [128, 16, 8])
    coef = pg.sb("coef", [128, 16, 2])
    AB = pg.sb("AB", [128, 16, 2])
    rstd = pg.sb("rstd", [128, TPC])
    rstd2 = pg.sb("rstd2", [128, TPC])
    sqb = [(pg.sb("sq%d" % i, [128, TPC], BF16), "sq%d" % i) for i in range(2)]
    ssp = [(pg.ps("ss%d" % i, [128, 512]), "ss%d" % i) for i in range(3)]
    mmp = [(pg.ps("mm%d" % i, [128, 512]), "mm%d" % i) for i in range(4)]
    scr = [(pg.sb("scr%d" % i, [128, TPC]), "scr%d" % i) for i in range(4)]
    pg.dma("sp", mpt[:, :, :], mp.rearrange("p (c f) -> p c f", f=8), w=["mpt"])
    pg.tt("dve", coef[:, :, 0], mpt[:, :, 0], mpt[:, :, 1], ALU.mult, r=["mpt"], w=["coef"])
    pg.tt("dve", coef[:, :, 1], mpt[:, :, 0], mpt[:, :, 2], ALU.mult, r=["mpt"], w=["coef"])
    pg.stt(AB[:, :, 0], mpt[:, :, 5], 1.0, mpt[:, :, 3], ALU.add, ALU.mult, r=["mpt"], w=["AB"])
    pg.stt(AB[:, :, 1], mpt[:, :, 7], 1.0, mpt[:, :, 3], ALU.add, ALU.mult, r=["mpt"], w=["AB"])
    if even:
        ws = WStream(pg, "w", 16, gw=512, nbuf=3)
        oT = pg.din("oT", [1024, TPC])
        sfT = pg.din("sfT", [1024, TPC])
        srT = pg.din("srT", [1024, TPC])
        wglu = pg.din("wglu", [1024, 1024])
        bglu = pg.din("bglu", [128, 8])
        bgt = pg.sb("bgt", [128, 8])
        Gb = pg.sb("Gb", [128, 8, TPC], BF16)
        pg.dma("sp", bgt[:, :], bglu[:, :], w=["bgt"])
        ov = oT.rearrange("(c p) t -> p c t", p=128)
        for q in range(2):
            pg.dma("pool", M[:, 4 * q:4 * q + 4, :], ov[:, 4 * q:4 * q + 4, :], w=["M%d" % c for c in range(4 * q, 4 * q + 4)])
        fa = scr[0:2]
        fb = scr[2:4]
        for c in range(8):
            a_, aid = fa[c % 2]
            b_, bid = fb[c % 2]
            G = Y[:, 8 + c, :]
            gid = "Y%d" % (8 + c)
            pg.dma("sp", a_[:, :], sfT[c * 128:(c + 1) * 128, :], w=[aid])
            pg.dma("sp", b_[:, :], srT[c * 128:(c + 1) * 128, :], w=[bid])
            pg.tt("dve", a_[:, :], a_[:, :], b_[:, :], ALU.add, r=[aid, bid], w=[aid])
            pg.tt("pool", b_[:, :], a_[:, :], a_[:, :], ALU.mult, r=[aid], w=[bid])
            pg.ts("dve", b_[:, :], b_[:, :], GELU_C, ALU.mult, 1.0, ALU.add, r=[bid], w=[bid])
            pg.tt("pool", b_[:, :], b_[:, :], a_[:, :], ALU.mult, r=[aid, bid], w=[bid])
            pg.act(b_[:, :], b_[:, :], AF.Sigmoid, r=[bid], w=[bid], scale=GELU_S)
            pg.tt("dve", G, a_[:, :], b_[:, :], ALU.mult, r=[aid, bid], w=[gid])
            pg.copy("pool", Gb[:, c, :], G, r=[gid], w=["Gb%d" % c])
        st = {"n": 0}
        zt = [(pg.sb("zt%d" % i, [128, TT]), "zt%d" % i) for i in range(3)]

        def evac_glu(oc, tt, ps, psid):
            z_, zid = zt[st["n"] % 3]
            st["n"] += 1
            pg.act(z_[:, :], ps, AF.Sigmoid, r=[psid, "bgt"], w=[zid], bias=bgt[:, oc:oc + 1])
            pg.tt("dve", M[:, 8 + oc, tt * TT:(tt + 1) * TT], z_[:, :], Y[:, 8 + oc, tt * TT:(tt + 1) * TT], ALU.mult,
                  r=[zid, "Y%d" % (8 + oc)], w=["M%d" % (8 + oc)])

        ws.KC = 8
        linear_fm(pg, ws, wglu, 1024, Gb, lambda kc: "Gb%d" % kc, 8, mmp, evac_glu)
        ws.KC = 16

        def evac_out(oc, tt, ps, psid):
            eng = "act" if (tt % 2 == 0) else "dve"
            pg.copy(eng, Y[:, oc, tt * TT:(tt + 1) * TT], ps, r=[psid], w=["Y%d" % oc])
    else:
        ws = WStream(pg, "w", 32, gw=256, nbuf=2)
        yfT = pg.din("yfT", [4096, TPC])
        yrT = pg.din("yrT", [4096, TPC])
        xT = pg.din("xT", [4096, TPC])
        zT = pg.din("zT", [4096, TPC])
        nw = pg.din("nw", [128, 64])
        nwt = pg.sb("nwt", [128, 64])
        rstdg = pg.sb("rstdg", [128, TPC])
        pg.dma("sp", nwt[:, :], nw[:, :], w=["nwt"])
        HW = TPC // 2
        half = [(scr[i // 2][0][:, (i % 2) * HW:(i % 2 + 1) * HW], scr[i // 2][1] + "_h%d" % (i % 2)) for i in range(8)]
        for c in range(32):
            sb_, sid = sqb[c % 2]
            rows = slice(c * 128, (c + 1) * 128)
            for hf in range(2):
                cols = slice(hf * HW, (hf + 1) * HW)
                base = 4 * ((2 * c + hf) % 2)
                (a_, aid), (b_, bid), (x_, xid), (z_, zid) = half[base:base + 4]
                pg.dma("sp", a_, yfT[rows, cols], w=[aid])
                pg.dma("sp", b_, yrT[rows, cols], w=[bid])
                pg.dma("sp", x_, xT[rows, cols], w=[xid])
                pg.dma("sp", z_, zT[rows, cols], w=[zid])
                pg.tt("dve", a_, a_, b_, ALU.add, r=[aid, bid], w=[aid])
                pg.stt(a_, x_, nwt[:, 32 + c:33 + c], a_, ALU.mult, ALU.add, r=[aid, xid, "nwt"], w=[aid])
                pg.act(z_, z_, AF.Silu, r=[zid], w=[zid])
                pg.tt("dve", a_, a_, z_, ALU.mult, r=[aid, zid], w=[aid])
                pg.act(sb_[:, cols], a_, AF.Square, r=[aid], w=[sid + "_h%d" % hf])
                pg.act(M[:, c, cols], a_, AF.Identity, r=[aid, "nwt"], w=["M%d" % c], scale=nwt[:, c:c + 1])
            for tt in range(NTT):
                ps, psid = ssp[tt]
                pg.mm(ps[:, 0:TT], ones[:, :], sb_[:, tt * TT:(tt + 1) * TT], c == 0, c == 31,
                      r=[sid + "_h0", sid + "_h1", "ones"], w=[psid])
        for i in range(4):
            t_, tid = scr[i]
            pg.copy("dve", t_[:, 0:1], t_[:, 0:1], w=[tid + "_h0", tid + "_h1", tid])
        for i in range(2):
            t_, tid = sqb[i]
            pg.copy("dve", t_[:, 0:1], t_[:, 0:1], w=[tid + "_h0", tid + "_h1", tid])
        rg_ids = []
        for tt in range(NTT):
            ps, psid = ssp[tt]
            sl = rstdg[:, tt * TT:(tt + 1) * TT]
            pg.act(sl, ps[:, 0:TT], AF.Sqrt, r=[psid], w=["rg%d" % tt], bias=pg.eps_ap, scale=1.0 / 4096)
            pg.recip(sl, sl, r=["rg%d" % tt], w=["rg%d" % tt])
            rg_ids.append("rg%d" % tt)

        def evac_out(oc, tt, ps, psid):
            pg.tt("dve", Y[:, oc, tt * TT:(tt + 1) * TT], ps, rstdg[:, tt * TT:(tt + 1) * TT], ALU.mult,
                  r=[psid, "rg%d" % tt], w=["Y%d" % oc])

    linear_fm(pg, ws, wout, D, M, lambda kc: "M%d" % kc, KC, mmp, evac_out)
    rs_ids = sumsq_rstd(pg, "n1", lambda c: Y[:, c, :], lambda c: "Y%d" % c, 16, D, ones, sqb, ssp, rstd)
    residual_tail(pg, "r1", Y, lambda c: "Y%d" % c, rstd, rs_ids, coef, "coef", hT, h1T, scr, inplace=True)
    rs2 = sumsq_rstd(pg, "n2", lambda c: Y[:, c, :], lambda c: "Y%d" % c, 16, D, ones, sqb, ssp, rstd2)
    tb = scr[0:2]
    ub = scr[2:4]
    for c in range(16):
        t_, tid = tb[c % 2]
        u_, uid = ub[c % 2]
        pg.tt("dve", t_[:, :], Y[:, c, :], rstd2[:, :], ALU.mult, r=["Y%d" % c] + rs2, w=[tid])
        pg.act(u_[:, 0:LAT], t_[:, 0:LAT], AF.Identity, r=[tid, "AB", "mpt"], w=[uid], bias=mpt[:, c, 4:5], scale=AB[:, c, 0:1])
        pg.act(u_[:, LAT:TPC], t_[:, LAT:TPC], AF.Identity, r=[tid, "AB", "mpt"], w=[uid], bias=mpt[:, c, 6:7], scale=AB[:, c, 1:2])
        pg.dma("sp", u2T[c * 128:(c + 1) * 128, :], u_[:, :], r=[uid], final=True)
    return pg


def run_k3a(even, hT, mix, g, mod_l, mod_c, w_out, extra):
    m = lambda v, i: v[i * D:(i + 1) * D]
    mp = pack_cols([g[1], m(mod_l, 2), m(mod_c, 2), g[2], m(mod_l, 3), m(mod_l, 4), m(mod_c, 3), m(mod_c, 4)])
    hs = shard_T(hT)
    sh = {k: shard_T(v) for k, v in mix.items()}
    in_maps = []
    for k in range(NCORES):
        d = {"hT": hs[k], "mp": mp, "wout": w_out}
        for kk in sh:
            d[kk] = sh[kk][k]
        d.update(extra)
        in_maps.append(d)
    res = _run(build_k3a(even), in_maps)
    return unshard_T([res[k]["h1T"] for k in range(NCORES)]), unshard_T([res[k]["u2T"] for k in range(NCORES)])


NCH = TOK // 128


def build_k2ob(nch=NCH):
    pg = Prog()
    T = nch * 128
    xa = pg.din("xa", [2, T, 512])
    dtr = pg.din("dtr", [2, T, 8])
    Bm = pg.din("Bm", [2, T, 128])
    BT = pg.din("BT", [2, 128, T])
    CT = pg.din("CT", [2, 128, T])
    hp = pg.din("hp", [128, 32])
    msk = pg.din("msk", [128, 256])
    y = pg.dout("y", [2, T, 512])
    hpt = pg.sb("hpt", [128, 2, 2, 8])
    mk = pg.sb("mk", [128, 256])
    onesf = pg.sb("onesf", [128, 128])
    Aneg = pg.sb("Aneg", [128, 2, 8])
    pg.dma("sp", hpt[:, :, :, :], hp.rearrange("p (d f h) -> p d f h", d=2, f=2), w=["hpt"])
    pg.dma("sp", mk[:, :], msk[:, :], w=["mk"])
    pg.memset("dve", onesf[:, :], 1.0, w=["onesf"])
    UT = mk[:, 0:128]
    TRI = mk[:, 128:256]
    for d in range(2):
        pg.act(Aneg[:, d, :], hpt[:, d, 1, :], AF.Exp, r=["hpt"], w=["Aneg"])
    pg.ts("dve", Aneg[:, :, :], Aneg[:, :, :], -1.0, ALU.mult, r=["Aneg"], w=["Aneg"])
    NB = 2

    def tiles(nm, shape, dt=F32):
        return [[(pg.sb("%s_%d_%d" % (nm, d, i), shape, dt), "%s_%d_%d" % (nm, d, i)) for i in range(NB)] for d in range(2)]

    Xt = tiles("X", [128, 8, 64])
    Dt = tiles("dt", [128, 8])
    At = tiles("a", [128, 8])
    Bb = tiles("Bb", [128, 128], BF16)
    BTb = tiles("BTb", [128, 128], BF16)
    CTb = tiles("CTb", [128, 128], BF16)
    Rt = tiles("R", [128, 8, 128])
    Et = tiles("E", [128, 8, 128])
    CBm = tiles("CBm", [128, 128])
    MT = tiles("MT", [128, 8, 128], BF16)
    xdt = tiles("xdt", [128, 8, 64], BF16)
    xdec = tiles("xdec", [128, 8, 64], BF16)
    ev = tiles("ev", [128, 3, 8])
    yt = tiles("yt", [128, 8, 64])
    S = [(pg.sb("S%d" % d, [128, 8, 64]), "S%d" % d) for d in range(2)]
    Sb = [(pg.sb("Sb%d" % d, [128, 8, 64], BF16), "Sb%d" % d) for d in range(2)]
    p_seg = [(pg.ps("pseg%d" % i, [128, 512]), "pseg%d" % i) for i in range(2)]
    p_cbt = (pg.ps("pcbt", [128, 512]), "pcbt")
    p_yd = (pg.ps("pyd", [128, 512]), "pyd")
    p_yo = (pg.ps("pyo", [128, 512]), "pyo")
    p_ns = (pg.ps("pns", [128, 512]), "pns")
    p_vec = (pg.ps("pvec", [128, 512]), "pvec")
    for d in range(2):
        pg.memset("dve", S[d][0][:, :, :], 0.0, w=[S[d][1]])
        pg.memset("dve", Sb[d][0][:, :, :], 0.0, w=[Sb[d][1]])

    def bc_h(ap8, n):
        return ap8.unsqueeze(2).to_broadcast([128, 8, n])

    for ci in range(nch):
        for d in range(2):
            i = ci % NB
            t0 = ci * 128
            X, Xi = Xt[d][i]
            dtt, dti = Dt[d][i]
            a_, ai = At[d][i]
            B_, Bi = Bb[d][i]
            BT_, BTi = BTb[d][i]
            CT_, CTi = CTb[d][i]
            R_, Ri = Rt[d][i]
            E_, Ei = Et[d][i]
            CB_, CBi = CBm[d][i]
            M_, Mi = MT[d][i]
            xd_, xdi = xdt[d][i]
            xc_, xci = xdec[d][i]
            ev_, evi = ev[d][i]
            y_, yi = yt[d][i]
            S_, Si = S[d]
            Sb_, Sbi = Sb[d]
            pg.dma("sp", X[:, :, :], xa[d, t0:t0 + 128, :].rearrange("t (h p) -> t h p", h=8), w=[Xi])
            pg.dma("sp", dtt[:, :], dtr[d, t0:t0 + 128, :], w=[dti])
            pg.dma("pool", B_[:, :], Bm[d, t0:t0 + 128, :], w=[Bi])
            pg.dma("pool", BT_[:, :], BT[d, :, t0:t0 + 128], w=[BTi])
            pg.dma("pool", CT_[:, :], CT[d, :, t0:t0 + 128], w=[CTi])
            pg.tt("dve", dtt[:, :], dtt[:, :], hpt[:, d, 0, :], ALU.add, r=[dti, "hpt"], w=[dti])
            pg.act(dtt[:, :], dtt[:, :], AF.Exp, r=[dti], w=[dti])
            pg.act(dtt[:, :], dtt[:, :], AF.Ln, r=[dti], w=[dti], bias=1.0)
            pg.tt("dve", a_[:, :], dtt[:, :], Aneg[:, d, :], ALU.mult, r=[dti, "Aneg"], w=[ai])
            pg.tt("dve", R_[:, :, :], TRI.unsqueeze(1).to_broadcast([128, 8, 128]), bc_h(a_[:, :], 128), ALU.mult,
                  r=[ai, "mk"], w=[Ri])
            for hf in range(2):
                ps, pid = p_seg[hf]
                pg.mm(ps[:, :], UT, R_[:, 4 * hf:4 * hf + 4, :].rearrange("p h l -> p (h l)"), True, True, r=["mk", Ri], w=[pid])
                pg.act(E_[:, 4 * hf:4 * hf + 4, :].rearrange("p h l -> p (h l)"), ps[:, :], AF.Exp, r=[pid], w=[Ei + "_%d" % hf])
            ps, pid = p_cbt
            pg.mm(ps[:, 0:128], BT_[:, :], CT_[:, :], True, True, r=[BTi, CTi], w=[pid])
            pg.tt("dve", CB_[:, :], ps[:, 0:128], TRI, ALU.mult, r=[pid, "mk"], w=[CBi])
            pg.tt("dve", M_[:, :, :], E_[:, :, :], CB_[:, :].unsqueeze(1).to_broadcast([128, 8, 128]), ALU.mult,
                  r=[Ei + "_0", Ei + "_1", CBi], w=[Mi])
            pg.tt("dve", xd_[:, :, :], X[:, :, :], bc_h(dtt[:, :], 64), ALU.mult, r=[Xi, dti], w=[xdi])
            pv, pvi = p_vec
            pg.mm(pv[:, 0:8], TRI, a_[:, :], True, True, r=["mk", ai], w=[pvi])
            pg.mm(pv[:, 8:16], UT, a_[:, :], True, True, r=["mk", ai], w=[pvi])
            pg.mm(pv[:, 16:24], onesf[:, :], a_[:, :], True, True, r=["onesf", ai], w=[pvi])
            pg.act(ev_[:, :, :].rearrange("p a h -> p (a h)"), pv[:, 0:24], AF.Exp, r=[pvi], w=[evi])
            pg.tt("dve", xc_[:, :, :], xd_[:, :, :], bc_h(ev_[:, 1, :], 64), ALU.mult, r=[xdi, evi], w=[xci])
            pyd, pydi = p_yd
            for h in range(8):
                pg.mm(pyd[:, 64 * h:64 * h + 64], M_[:, h, :], xd_[:, h, :], True, True, r=[Mi, xdi], w=[pydi])
            pyo, pyoi = p_yo
            pg.mm(pyo[:, :], CT_[:, :], Sb_[:, :, :].rearrange("p h q -> p (h q)"), True, True, r=[CTi, Sbi], w=[pyoi])
            pg.tt("dve", y_[:, :, :], pyo[:, :].rearrange("p (h q) -> p h q", h=8), bc_h(ev_[:, 0, :], 64), ALU.mult,
                  r=[pyoi, evi], w=[yi])
            pg.tt("dve", y_[:, :, :], pyd[:, :].rearrange("p (h q) -> p h q", h=8), y_[:, :, :], ALU.add, r=[pydi, yi], w=[yi])
            pg.dma("sp", y[d, t0:t0 + 128, :], y_[:, :, :].rearrange("p h q -> p (h q)"), r=[yi], final=True)
            pns, pnsi = p_ns
            pg.mm(pns[:, :], B_[:, :], xc_[:, :, :].rearrange("p h q -> p (h q)"), True, True, r=[Bi, xci], w=[pnsi])
            pg.tt("dve", S_[:, :, :], S_[:, :, :], bc_h(ev_[:, 2, :], 64), ALU.mult, r=[Si, evi], w=[Si])
            pg.tt("dve", S_[:, :, :], pns[:, :].rearrange("p (h q) -> p h q", h=8), S_[:, :, :], ALU.add, r=[pnsi, Si], w=[Si])
            pg.copy("act", Sb_[:, :, :], S_[:, :, :], r=[Si], w=[Sbi])
    return pg


def ssd_masks():
    k = np.arange(128)
    UT = (k[:, None] > k[None, :]).astype(np.float32)
    TRI = (k[:, None] <= k[None, :]).astype(np.float32)
    return np.ascontiguousarray(np.concatenate([UT, TRI], axis=1))


def build_k2oa():
    pg = Prog()
    xin = pg.din("xin", [768, TOK])
    cw = pg.din("cw", [128, 36])
    xo = pg.dout("xo", [768, TOK])
    cwt = pg.sb("cwt", [128, 6, 6])
    pg.dma("sp", cwt[:, :, :], cw.rearrange("p (c f) -> p c f", f=6), w=["cwt"])
    Xb = [(pg.sb("X%d" % i, [128, TOK]), "X%d" % i) for i in range(2)]
    Ab = [(pg.sb("A%d" % i, [128, TOK]), "A%d" % i) for i in range(2)]
    segs = [(0, CTX), (CTX, TOK)]
    for c in range(6):
        X, Xi = Xb[c % 2]
        A, Ai = Ab[c % 2]
        for q in range(4):
            pg.dma("sp", X[:, q * 2112:(q + 1) * 2112], xin[c * 128:(c + 1) * 128, q * 2112:(q + 1) * 2112], w=[Xi])
        for (s, e) in segs:
            pg.ts("dve", A[:, s:e], X[:, s:e], cwt[:, c, 2:3], ALU.mult, r=[Xi, "cwt"], w=[Ai])
            for k in (0, 1, 3, 4):
                o = k - 2
                lo = s + max(0, -o)
                hi = e - max(0, o)
                pg.stt(A[:, lo:hi], X[:, lo + o:hi + o], cwt[:, c, k:k + 1], A[:, lo:hi], ALU.mult, ALU.add,
                       r=[Xi, Ai, "cwt"], w=[Ai])
        for q in range(4):
            sl = slice(q * 2112, (q + 1) * 2112)
            pg.act(A[:, sl], A[:, sl], AF.Silu, r=[Ai], w=[Ai], bias=cwt[:, c, 5:6])
        pg.dma("sp", xo[c * 128:(c + 1) * 128, :], A[:, :], r=[Ai], final=True)
    return pg


def run_k2oa(xbcT, conv_w, conv_b):
    in_maps = []
    for k in range(NCORES):
        ch = slice(768 * k, 768 * (k + 1))
        f = np.concatenate([conv_w[:, ch], conv_b[None, ch]], axis=0)
        cwp = f.reshape(6, 6, 128).transpose(2, 1, 0).reshape(128, 36)
        in_maps.append({"xin": np.ascontiguousarray(xbcT[ch]), "cw": np.ascontiguousarray(cwp)})
    res = _run(build_k2oa(), in_maps)
    return np.concatenate([res[k]["xo"] for k in range(NCORES)], axis=0)


def flipseg(a, axis):
    a = np.moveaxis(a, axis, 0)
    out = np.concatenate([a[:CTX][::-1], a[CTX:][::-1]], axis=0)
    return np.moveaxis(out, 0, axis)


def run_k2ob(xbcaT, dtrT, dt_bias, a_log):
    in_maps = []
    msk = ssd_masks()
    for g in range(NCORES):
        xs = xbcaT[512 * g:512 * (g + 1)]
        Bs = xbcaT[4096 + 128 * g:4096 + 128 * (g + 1)]
        Cs = xbcaT[5120 + 128 * g:5120 + 128 * (g + 1)]
        xa, dtr, Bm, BTt, CTt = [], [], [], [], []
        for d in range(2):
            f = (lambda a: flipseg(a, 1)) if d == 1 else (lambda a: a)
            xa.append(f(xs).T)
            dtr.append(f(dtrT[64 * d + 8 * g:64 * d + 8 * g + 8]).T)
            Bm.append(f(Bs).T)
            BTt.append(f(Bs))
            CTt.append(f(Cs))
        hp = np.stack([dt_bias[:, 8 * g:8 * g + 8], a_log[:, 8 * g:8 * g + 8]], axis=1).reshape(1, 32).repeat(128, 0)
        c = np.ascontiguousarray
        in_maps.append({"xa": c(np.stack(xa)), "dtr": c(np.stack(dtr)), "Bm": c(np.stack(Bm)), "BT": c(np.stack(BTt)),
                        "CT": c(np.stack(CTt)), "hp": c(hp.astype(np.float32)), "msk": msk})
    res = _run(build_k2ob(), in_maps)
    yf = np.concatenate([res[g]["y"][0].T for g in range(NCORES)], axis=0)
    yr = np.concatenate([flipseg(res[g]["y"][1], 0).T for g in range(NCORES)], axis=0)
    return yf, yr


S5T = 256
S5N = TOK // S5T
PI = math.pi


def build_k2e(lam_init):
    pg = Prog()
    qk = pg.din("qk", [2, 2, 64, TOK])
    qkp = pg.din("qkp", [2, 2, 64, TOK])
    cs = pg.din("cs", [2, 64, TOK])
    v = pg.din("v", [TOK, 128])
    lamb = pg.din("lamb", [128, 256])
    sg = pg.din("sg", [128, 1])
    su = pg.din("su", [2, 128, TOK])
    Bblk = pg.din("Bblk", [128, 2 * 4 * 2 * 128])
    Cblk = pg.din("Cblk", [128, 2 * 4 * 2 * 128])
    s5p = pg.din("s5p", [128, 2 * 4 * 3])
    dsk = pg.din("dsk", [128, 1])
    oT = pg.dout("oT", [128, TOK])
    sf = pg.dout("sf", [128, TOK])
    sr = pg.dout("sr", [128, TOK])
    ones = setup_consts(pg)

    lt = pg.sb("lt", [128, 4, 64])
    sgt = pg.sb("sgt", [128, 1])
    dskt = pg.sb("dskt", [128, 1])
    lam2 = pg.sb("lam2", [128, 2, 64])
    lame = pg.sb("lame", [128, 2])
    nlam = pg.sb("nlam", [128, 1])
    pg.dma("sp", lt[:, :, :], lamb.rearrange("p (a b) -> p a b", a=4), w=["lt"])
    pg.dma("sp", sgt[:, :], sg[:, :], w=["sgt"])
    pg.dma("sp", dskt[:, :], dsk[:, :], w=["dskt"])
    pg.tt("dve", lam2[:, 0, :], lt[:, 0, :], lt[:, 1, :], ALU.mult, r=["lt"], w=["lam2"])
    pg.tt("dve", lam2[:, 1, :], lt[:, 2, :], lt[:, 3, :], ALU.mult, r=["lt"], w=["lam2"])
    pg.add("dve", lambda e: e.tensor_reduce(out=lame[:, :], in_=lam2[:, :, :], axis=AX.X, op=ALU.add), r=["lam2"], w=["lame"])
    pg.act(lame[:, :], lame[:, :], AF.Exp, r=["lame"], w=["lame"])
    pg.tt("dve", nlam[:, :], lame[:, 1:2], lame[:, 0:1], ALU.subtract, r=["lame"], w=["nlam"])
    pg.ts("dve", nlam[:, :], nlam[:, :], -lam_init, ALU.add, r=["nlam"], w=["nlam"])
    pg.ts("dve", sgt[:, :], sgt[:, :], 1.0 - lam_init, ALU.mult, r=["sgt"], w=["sgt"])

    Bb = pg.sb("Bb", [128, 16, 128], BF16)
    Cb = pg.sb("Cb", [128, 16, 128], BF16)
    pg.dma("pool", Bb[:, :, :], Bblk.rearrange("p (a n) -> p a n", n=128), w=["Bb"])
    pg.dma("pool", Cb[:, :, :], Cblk.rearrange("p (a n) -> p a n", n=128), w=["Cb"])
    subt = [(pg.sb("sub%d" % i, [128, S5T], BF16), "sub%d" % i) for i in range(3)]
    pt = pg.sb("pt", [128, 2, 4, 3])
    pg.dma("sp", pt[:, :, :, :], s5p.rearrange("p (d g f) -> p d g f", d=2, g=4), w=["pt"])
    W8 = [128, 2, 4]
    names = ["dt", "th", "rho", "m", "sn", "cn", "thc", "nre", "nim", "den", "kre", "kim", "t1", "t2"]
    sm = {n: pg.sb("s5_" + n, W8) for n in names}
    a3 = lambda n: sm[n][:, :, :]
    lre, lim, ldt = pt[:, :, :, 0], pt[:, :, :, 1], pt[:, :, :, 2]
    pg.act(a3("dt"), ldt, AF.Exp, r=["pt"], w=["s_dt"])
    pg.tt("dve", a3("th"), lim, a3("dt"), ALU.mult, r=["pt", "s_dt"], w=["s_th"])
    pg.tt("dve", a3("rho"), lre, a3("dt"), ALU.mult, r=["pt", "s_dt"], w=["s_rho"])
    pg.act(a3("rho"), a3("rho"), AF.Exp, r=["s_rho"], w=["s_rho"])
    for _ in range(4):
        pg.ts("dve", a3("m"), a3("th"), PI, ALU.is_gt, r=["s_th"], w=["s_m"])
        pg.stt(a3("th"), a3("m"), -2.0 * PI, a3("th"), ALU.mult, ALU.add, r=["s_m", "s_th"], w=["s_th"])
    pg.act(a3("sn"), a3("th"), AF.Sin, r=["s_th"], w=["s_sn"])
    pg.ts("dve", a3("thc"), a3("th"), PI / 2, ALU.add, r=["s_th"], w=["s_thc"])
    pg.ts("dve", a3("m"), a3("thc"), PI, ALU.is_gt, r=["s_thc"], w=["s_m"])
    pg.stt(a3("thc"), a3("m"), -2.0 * PI, a3("thc"), ALU.mult, ALU.add, r=["s_m", "s_thc"], w=["s_thc"])
    pg.act(a3("cn"), a3("thc"), AF.Sin, r=["s_thc"], w=["s_cn"])
    pg.tt("dve", a3("nre"), a3("rho"), a3("cn"), ALU.mult, r=["s_rho", "s_cn"], w=["s_nre"])
    pg.ts("dve", a3("nre"), a3("nre"), -1.0, ALU.add, r=["s_nre"], w=["s_nre"])
    pg.tt("dve", a3("nim"), a3("rho"), a3("sn"), ALU.mult, r=["s_rho", "s_sn"], w=["s_nim"])
    pg.tt("dve", a3("den"), lre, lre, ALU.mult, r=["pt"], w=["s_den"])
    pg.tt("dve", a3("t1"), lim, lim, ALU.mult, r=["pt"], w=["s_t1"])
    pg.tt("dve", a3("den"), a3("den"), a3("t1"), ALU.add, r=["s_den", "s_t1"], w=["s_den"])
    pg.recip(a3("den"), a3("den"), r=["s_den"], w=["s_den"])
    pg.tt("dve", a3("t1"), a3("nre"), lre, ALU.mult, r=["s_nre", "pt"], w=["s_t1"])
    pg.tt("dve", a3("t2"), a3("nim"), lim, ALU.mult, r=["s_nim", "pt"], w=["s_t2"])
    pg.tt("dve", a3("kre"), a3("t1"), a3("t2"), ALU.add, r=["s_t1", "s_t2"], w=["s_kre"])
    pg.tt("dve", a3("kre"), a3("kre"), a3("den"), ALU.mult, r=["s_kre", "s_den"], w=["s_kre"])
    pg.tt("dve", a3("t1"), a3("nim"), lre, ALU.mult, r=["s_nim", "pt"], w=["s_t1"])
    pg.tt("dve", a3("t2"), a3("nre"), lim, ALU.mult, r=["s_nre", "pt"], w=["s_t2"])
    pg.tt("dve", a3("kim"), a3("t1"), a3("t2"), ALU.subtract, r=["s_t1", "s_t2"], w=["s_kim"])
    pg.tt("dve", a3("kim"), a3("kim"), a3("den"), ALU.mult, r=["s_kim", "s_den"], w=["s_kim"])
    TS = [128, 2, 4, S5T]
    Ec, Es, Fre, Fim, Rho, Tmp, Tmp2 = [pg.sb("tab%d" % i, TS) for i in range(7)]
    pg.copy("dve", Ec[:, :, :, 0], a3("cn"), r=["s_cn"], w=["Ec"])
    pg.copy("dve", Es[:, :, :, 0], a3("sn"), r=["s_sn"], w=["Es"])
    mlen = 1
    while mlen < S5T:
        bc = lambda T_: T_[:, :, :, mlen - 1:mlen].to_broadcast([128, 2, 4, mlen])
        lo = lambda T_: T_[:, :, :, 0:mlen]
        hi = lambda T_: T_[:, :, :, mlen:2 * mlen]
        pg.tt("dve", lo(Tmp), lo(Ec), bc(Ec), ALU.mult, r=["Ec"], w=["Tmp"])
        pg.tt("dve", lo(Tmp2), lo(Es), bc(Es), ALU.mult, r=["Es"], w=["Tmp2"])
        pg.tt("dve", hi(Tmp), lo(Ec), bc(Es), ALU.mult, r=["Ec", "Es"], w=["Tmp"])
        pg.tt("dve", hi(Tmp2), lo(Es), bc(Ec), ALU.mult, r=["Ec", "Es"], w=["Tmp2"])
        pg.tt("dve", hi(Ec), lo(Tmp), lo(Tmp2), ALU.subtract, r=["Tmp", "Tmp2"], w=["Ec"])
        pg.tt("dve", hi(Es), hi(Tmp), hi(Tmp2), ALU.add, r=["Tmp", "Tmp2"], w=["Es"])
        mlen *= 2
    bk = lambda n: sm[n][:, :, :].unsqueeze(3).to_broadcast(TS)
    A4 = lambda T_: T_[:, :, :, :]
    pg.tt("dve", A4(Tmp), A4(Ec), bk("kre"), ALU.mult, r=["Ec", "s_kre"], w=["Tmp"])
    pg.tt("dve", A4(Tmp2), A4(Es), bk("kim"), ALU.mult, r=["Es", "s_kim"], w=["Tmp2"])
    pg.tt("dve", A4(Fre), A4(Tmp), A4(Tmp2), ALU.add, r=["Tmp", "Tmp2"], w=["Fre"])
    pg.tt("dve", A4(Tmp), A4(Es), bk("kre"), ALU.mult, r=["Es", "s_kre"], w=["Tmp"])
    pg.tt("dve", A4(Tmp2), A4(Ec), bk("kim"), ALU.mult, r=["Ec", "s_kim"], w=["Tmp2"])
    pg.tt("dve", A4(Fim), A4(Tmp), A4(Tmp2), ALU.subtract, r=["Tmp", "Tmp2"], w=["Fim"])
    pg.copy("dve", A4(Rho), bk("rho"), r=["s_rho"], w=["Rho"])

    carry = pg.sb("carry", [128, 2, 4, 2])
    pg.memset("dve", carry[:, :, :, :], 0.0, w=["carry%d%d" % (d, g) for d in range(2) for g in range(4)])
    NB = 3
    s5t = [[(pg.sb("s5w%d_%d" % (j, i), [128, S5T]), "s5w%d_%d" % (j, i)) for j in range(10)] for i in range(NB)]
    s5c = [[(pg.sb("s5c%d_%d" % (j, i), [128, S5T], BF16), "s5c%d_%d" % (j, i)) for j in range(2)] for i in range(NB)]
    s5u = [(pg.sb("s5u%d" % i, [128, S5T]), "s5u%d" % i) for i in range(2)]
    s5o = [(pg.sb("s5o%d" % i, [128, S5T]), "s5o%d" % i) for i in range(2)]
    p_raw = [(pg.ps("praw%d" % i, [128, 512]), "praw%d" % i) for i in range(2)]
    p_y = [(pg.ps("pys5%d" % i, [128, 512]), "pys5%d" % i) for i in range(2)]
    NU = S5N * 8

    def unit(n):
        ci, r = divmod(n, 8)
        d, g = divmod(r, 4)
        return d, ci, g, ci * 2 + d

    def s5_prefetch(grp):
        if grp >= S5N * 2:
            return
        ci, d = divmod(grp, 2)
        sb_, sbid = subt[grp % 3]
        pg.dma("pool", sb_[:, :], su[d, :, ci * S5T:(ci + 1) * S5T], w=[sbid])

    def stA(n):
        d, ci, g, grp = unit(n)
        if g == 0:
            s5_prefetch(grp + 1)
        sb_, sbid = subt[grp % 3]
        praw, prid = p_raw[n % 2]
        for comp in range(2):
            pg.mm(praw[:, comp * S5T:(comp + 1) * S5T], Bb[:, (d * 4 + g) * 2 + comp, :], sb_[:, :], True, True,
                  r=["Bb", sbid], w=[prid])

    def stB(n):
        d, ci, g, grp = unit(n)
        praw, prid = p_raw[n % 2]
        W = s5t[n % NB]
        cid = "carry%d%d" % (d, g)
        rre, rim = praw[:, 0:S5T], praw[:, S5T:2 * S5T]
        fre, fim = Fre[:, d, g, :], Fim[:, d, g, :]
        (t1, i1), (t2, i2), (bre, ib), (bim, ibm), (gre, igr), (gim, igi) = W[0:6]
        pg.tt("dve", t1[:, :], rre, fre, ALU.mult, r=[prid, "Fre"], w=[i1])
        pg.tt("dve", t2[:, :], rim, fim, ALU.mult, r=[prid, "Fim"], w=[i2])
        pg.tt("dve", bre[:, :], t1[:, :], t2[:, :], ALU.add, r=[i1, i2], w=[ib])
        pg.tt("dve", t1[:, :], rre, fim, ALU.mult, r=[prid, "Fim"], w=[i1])
        pg.tt("dve", t2[:, :], rim, fre, ALU.mult, r=[prid, "Fre"], w=[i2])
        pg.tt("dve", bim[:, :], t1[:, :], t2[:, :], ALU.subtract, r=[i1, i2], w=[ibm])
        init_re = 0.0 if ci == 0 else carry[:, d, g, 0:1]
        init_im = 0.0 if ci == 0 else carry[:, d, g, 1:2]
        pg.add("dve", lambda e, o=gre[:, :], a=Rho[:, d, g, :], b=bre[:, :], ini=init_re:
               e.tensor_tensor_scan(out=o, data0=a, data1=b, initial=ini, op0=ALU.mult, op1=ALU.add), r=["Rho", ib, cid], w=[igr])
        pg.add("dve", lambda e, o=gim[:, :], a=Rho[:, d, g, :], b=bim[:, :], ini=init_im:
               e.tensor_tensor_scan(out=o, data0=a, data1=b, initial=ini, op0=ALU.mult, op1=ALU.add), r=["Rho", ibm, cid], w=[igi])

    def stC(n):
        d, ci, g, grp = unit(n)
        W = s5t[n % NB]
        cid = "carry%d%d" % (d, g)
        ec, es = Ec[:, d, g, :], Es[:, d, g, :]
        (gre, igr), (gim, igi), (u1, j1), (u2, j2), (cre, icr), (cim, ici) = W[4:10]
        pg.tt("pool", u1[:, :], gre[:, :], ec, ALU.mult, r=[igr, "Ec"], w=[j1])
        pg.tt("pool", u2[:, :], gim[:, :], es, ALU.mult, r=[igi, "Es"], w=[j2])
        pg.tt("pool", cre[:, :], u1[:, :], u2[:, :], ALU.add, r=[j1, j2], w=[icr])
        pg.tt("pool", u1[:, :], gim[:, :], ec, ALU.mult, r=[igi, "Ec"], w=[j1])
        pg.tt("pool", u2[:, :], gre[:, :], es, ALU.mult, r=[igr, "Es"], w=[j2])
        pg.tt("pool", cim[:, :], u1[:, :], u2[:, :], ALU.subtract, r=[j1, j2], w=[ici])
        pg.copy("pool", carry[:, d, g, 0:1], cre[:, S5T - 1:S5T], r=[icr], w=[cid])
        pg.copy("pool", carry[:, d, g, 1:2], cim[:, S5T - 1:S5T], r=[ici], w=[cid])

    def stD(n):
        W = s5t[n % NB]
        (cre, icr), (cim, ici) = W[8:10]
        (cbr, icbr), (cbi, icbi) = s5c[n % NB]
        pg.copy("act", cbr[:, :], cre[:, :], r=[icr], w=[icbr])
        pg.copy("act", cbi[:, :], cim[:, :], r=[ici], w=[icbi])

    def stE(n):
        d, ci, g, grp = unit(n)
        (cbr, icbr), (cbi, icbi) = s5c[n % NB]
        py, pyid = p_y[grp % 2]
        pg.mm(py[:, 0:S5T], Cb[:, (d * 4 + g) * 2 + 0, :], cbr[:, :], g == 0, False, r=["Cb", icbr], w=[pyid])
        pg.mm(py[:, 0:S5T], Cb[:, (d * 4 + g) * 2 + 1, :], cbi[:, :], False, g == 3, r=["Cb", icbi], w=[pyid])

    def stF(n):
        d, ci, g, grp = unit(n)
        if g != 3:
            return
        t0 = ci * S5T
        py, pyid = p_y[grp % 2]
        o_, oid = s5o[d]
        if d == 0:
            u_, uid = s5u[ci % 2]
            pg.dma("sp", u_[:, :], su[0, :, t0:t0 + S5T], w=[uid])
            pg.stt(o_[:, :], u_[:, :], dskt[:, 0:1], py[:, 0:S5T], ALU.mult, ALU.add, r=[uid, "dskt", pyid], w=[oid])
            pg.dma("sp", sf[:, t0:t0 + S5T], o_[:, :], r=[oid], final=True)
        else:
            pg.copy("dve", o_[:, :], py[:, 0:S5T], r=[pyid], w=[oid])
            pg.dma("sp", sr[:, t0:t0 + S5T], o_[:, :], r=[oid], final=True)

    stages = [stA, stB, stC, stD, stE, stF]
    tick = {"k": 0}

    def s5_tick():
        k = tick["k"]
        tick["k"] += 1
        for si, fn in enumerate(stages):
            n = k - si
            if 0 <= n < NU:
                fn(n)

    s5_prefetch(0)

    KT = pg.sb("KT", [64, 2, TOK], BF16)
    V = pg.sb("V", [128, NCH, 128], BF16)
    vv = v.rearrange("(c p) e -> p c e", p=128)
    for q in range(3):
        pg.dma("pool", V[:, 22 * q:22 * q + 22, :], vv[:, 22 * q:22 * q + 22, :], w=["V"])
    RW = 528
    rt_ = [[(pg.sb("rp%d_%d" % (j, i), [64, RW]), "rp%d_%d" % (j, i)) for j in range(4)] for i in range(2)]
    rc_ = [[(pg.sb("rc%d_%d" % (j, i), [64, RW]), "rc%d_%d" % (j, i)) for j in range(2)] for i in range(2)]
    rn = {"n": 0, "c": 0}

    def rope(which, t0, n, dst_fn, dst_id):
        ci = rn["c"] % 2
        rn["c"] += 1
        (cc, cci), (ss, ssi) = rc_[ci]
        pg.dma("sp", cc[:, 0:n], cs[0, :, t0:t0 + n], w=[cci])
        pg.dma("sp", ss[:, 0:n], cs[1, :, t0:t0 + n], w=[ssi])
        for m in range(2):
            i = rn["n"] % 2
            rn["n"] += 1
            (x, xi), (xp, xpi), (a, ai), (b, bi) = rt_[i]
            pg.dma("sp", x[:, 0:n], qk[which, m, :, t0:t0 + n], w=[xi])
            pg.dma("sp", xp[:, 0:n], qkp[which, m, :, t0:t0 + n], w=[xpi])
            pg.tt("dve", a[:, 0:n], x[:, 0:n], cc[:, 0:n], ALU.mult, r=[xi, cci], w=[ai])
            pg.tt("dve", b[:, 0:n], xp[:, 0:n], ss[:, 0:n], ALU.mult, r=[xpi, ssi], w=[bi])
            pg.tt("dve", dst_fn(m), a[:, 0:n], b[:, 0:n], ALU.add, r=[ai, bi], w=[dst_id])

    for t in range(TOK // RW):
        rope(1, t * RW, RW, lambda m, t=t: KT[:, m, t * RW:(t + 1) * RW], "KT")

    QT = [(pg.sb("QT%d" % i, [64, 2, 512], BF16), "QT%d" % i) for i in range(2)]
    Pt = [(pg.sb("P%d" % i, [128, 512], BF16), "P%d" % i) for i in range(3)]
    p_s = [(pg.ps("pS%d" % i, [128, 512]), "pS%d" % i) for i in range(2)]
    pO, pOid = (pg.ps("pO", [128, 512]), "pO")
    pZ, pZid = (pg.ps("pZ", [128, 512]), "pZ")
    rz = (pg.sb("rz", [128, 512]), "rz")
    ot = [(pg.sb("ot%d" % i, [128, 512]), "ot%d" % i) for i in range(2)]
    o2 = (pg.sb("o2", [128, 512]), "o2")
    osq = (pg.sb("osq", [128, 512], BF16), "osq")
    ors = (pg.sb("ors", [128, 512]), "ors")
    qtiles = [(0, CTX, 0, 2)] + [(CTX + 512 * i, 512, 0, NCH) for i in range(16)]

    def qrope(qi):
        q0, nq, _, _ = qtiles[qi]
        Q, Qid = QT[qi % 2]
        rope(0, q0, nq, lambda m: Q[:, m, 0:nq], Qid)

    qrope(0)
    cnt = {"s": 0, "p": 0, "it": 0}
    for qi, (q0, nq, k0, k1) in enumerate(qtiles):
        Q, Qid = QT[qi % 2]
        o_, oid = ot[qi % 2]
        its = [(m, kc) for m in range(2) for kc in range(k0, k1)]

        def emitS(m, kc, Q=Q, Qid=Qid, nq=nq):
            pS, pSid = p_s[cnt["s"] % 2]
            cnt["s"] += 1
            P, Pid = Pt[cnt["p"] % 3]
            cnt["p"] += 1
            pg.mm(pS[:, 0:nq], KT[:, m, kc * 128:(kc + 1) * 128], Q[:, m, 0:nq], True, True, r=["KT", Qid], w=[pSid])
            pg.act(P[:, 0:nq], pS[:, 0:nq], AF.Exp, r=[pSid], w=[Pid], scale=0.125)
            return P, Pid

        pend = emitS(*its[0])
        for idx, (m, kc) in enumerate(its):
            P, Pid = pend
            if idx + 1 < len(its):
                pend = emitS(*its[idx + 1])
            pg.mm(pO[:, 0:nq], V[:, kc, :], P[:, 0:nq], kc == k0, kc == k1 - 1, r=["V", Pid], w=[pOid])
            pg.mm(pZ[:, 0:nq], ones[:, :], P[:, 0:nq], kc == k0, kc == k1 - 1, r=["ones", Pid], w=[pZid])
            cnt["it"] += 1
            if cnt["it"] % 8 == 0:
                s5_tick()
            if m == 0 and kc == k0 + 8 and qi + 1 < len(qtiles):
                qrope(qi + 1)
            if kc == k1 - 1:
                pg.recip(rz[0][:, 0:nq], pZ[:, 0:nq], r=[pZid], w=[rz[1]])
                if m == 0:
                    pg.tt("dve", o_[:, 0:nq], pO[:, 0:nq], rz[0][:, 0:nq], ALU.mult, r=[pOid, rz[1]], w=[oid])
                else:
                    pg.tt("dve", o2[0][:, 0:nq], pO[:, 0:nq], rz[0][:, 0:nq], ALU.mult, r=[pOid, rz[1]], w=[o2[1]])
                    pg.stt(o_[:, 0:nq], o2[0][:, 0:nq], nlam[:, 0:1], o_[:, 0:nq], ALU.mult, ALU.add, r=[o2[1], oid, "nlam"], w=[oid])
        if qi == 0 and len(qtiles) > 1:
            qrope(1)
        pg.act(osq[0][:, 0:nq], o_[:, 0:nq], AF.Square, r=[oid], w=[osq[1]])
        pS, pSid = p_s[cnt["s"] % 2]
        cnt["s"] += 1
        pg.mm(pS[:, 0:nq], ones[:, :], osq[0][:, 0:nq], True, True, r=["ones", osq[1]], w=[pSid])
        pg.act(ors[0][:, 0:nq], pS[:, 0:nq], AF.Sqrt, r=[pSid], w=[ors[1]], bias=pg.eps_ap, scale=1.0 / 128)
        pg.recip(ors[0][:, 0:nq], ors[0][:, 0:nq], r=[ors[1]], w=[ors[1]])
        pg.tt("dve", o_[:, 0:nq], o_[:, 0:nq], ors[0][:, 0:nq], ALU.mult, r=[oid, ors[1]], w=[oid])
        pg.ts("dve", o_[:, 0:nq], o_[:, 0:nq], sgt[:, 0:1], ALU.mult, r=[oid, "sgt"], w=[oid])
        pg.dma("sp", oT[:, q0:q0 + nq], o_[:, 0:nq], r=[oid], final=True)
    while tick["k"] < NU + len(stages):
        s5_tick()
    return pg


def rope_tables():
    rows = SEQ // 64
    row = np.repeat(np.arange(rows, dtype=np.float32), 64)
    col = np.tile(np.arange(64, dtype=np.float32), rows)
    inv = (10000.0 ** (-np.arange(0, 32, 2, dtype=np.float32) / 32)).astype(np.float32)
    ang_r = row[:, None] * inv
    ang_c = col[:, None] * inv
    ang = np.concatenate([ang_r, ang_r, ang_c, ang_c], axis=-1)
    cos = np.cos(ang).astype(np.float32)
    sin = np.sin(ang).astype(np.float32)
    sgn = np.concatenate([-np.ones(16), np.ones(16), -np.ones(16), np.ones(16)]).astype(np.float32)
    cosT = np.concatenate([np.ones((64, CTX), np.float32), cos.T], axis=1)
    sinT = np.concatenate([np.zeros((64, CTX), np.float32), (sin * sgn[None, :]).T], axis=1)
    return np.ascontiguousarray(np.stack([cosT, sinT]))


ROPE_PERM = np.concatenate([np.arange(16, 32), np.arange(0, 16), np.arange(48, 64), np.arange(32, 48)])


def k2e_inputs(pT, j, P, cores=range(NCORES)):
    cs = rope_tables()
    c = np.ascontiguousarray
    maps = []
    for k in cores:
        q = np.stack([pT[m * 512 + k * 64:m * 512 + k * 64 + 64] for m in range(2)])
        kk = np.stack([pT[1024 + m * 512 + k * 64:1024 + m * 512 + k * 64 + 64] for m in range(2)])
        qk = np.stack([q, kk])
        qkp = qk[:, :, ROPE_PERM, :]
        v = pT[2048 + 128 * k:2048 + 128 * (k + 1)].T
        s = pT[3072 + 128 * k:3072 + 128 * (k + 1)]
        su = np.stack([s, flipseg(s, 1)])
        Bblk = np.zeros((128, 2, 4, 2, 128), np.float32)
        Cblk = np.zeros((128, 2, 4, 2, 128), np.float32)
        s5p = np.zeros((128, 2, 4, 3), np.float32)
        for d in range(2):
            for gp in range(4):
                for gi in range(2):
                    gl = 2 * gp + gi
                    g = 8 * k + gl
                    for comp, (bn, cn) in enumerate([("s5_b_re", "s5_c_re"), ("s5_b_im", "s5_c_im")]):
                        Bblk[gl * 16:(gl + 1) * 16, d, gp, comp, gi * 64:(gi + 1) * 64] = P[bn][j, d, g].T
                        Cblk[gi * 64:(gi + 1) * 64, d, gp, comp, gl * 16:(gl + 1) * 16] = P[cn][j, d, g].T
                    s5p[gi * 64:(gi + 1) * 64, d, gp, 0] = P["s5_lam_re"][j, d, g]
                    s5p[gi * 64:(gi + 1) * 64, d, gp, 1] = P["s5_lam_im"][j, d, g]
                    s5p[gi * 64:(gi + 1) * 64, d, gp, 2] = P["s5_log_dt"][j, d, g]
        dsk = P["s5_d"][j, 8 * k:8 * k + 8].reshape(128, 1)
        maps.append({"qk": c(qk), "qkp": c(qkp), "cs": cs, "v": c(v),
                     "lamb": c(P["diff_lam"][j].reshape(1, 256).repeat(128, 0)), "sg": c(P["diff_subln"][j].reshape(128, 1)),
                     "su": c(su), "Bblk": c(Bblk.reshape(128, -1)), "Cblk": c(Cblk.reshape(128, -1)),
                     "s5p": c(s5p.reshape(128, -1)), "dsk": c(dsk.astype(np.float32))})
    return maps


def run_k2e(pT, j, lam_init, P):
    res = _run(build_k2e(lam_init), k2e_inputs(pT, j, P))
    oT = np.concatenate([res[k]["oT"] for k in range(NCORES)], axis=0)
    sfT = np.concatenate([res[k]["sf"] for k in range(NCORES)], axis=0)
    srT = np.concatenate([flipseg(res[k]["sr"], 1) for k in range(NCORES)], axis=0)
    return oT, sfT, srT


def kernel(**inp):
    P = {k: np.asarray(v, dtype=np.float32) for k, v in inp.items()}
    mod = run_k0(P["c"][0], P["c_ctx"], P["w_mod"], P["b_mod"])
    hT = np.ascontiguousarray(np.concatenate([P["ctx"][0], P["x"][0]], axis=0).T)
    for i in range(4):
        j = i // 2
        g = P["norm_g"][i]
        ml, mc = mod[i, 0], mod[i, 1]
        if i % 2 == 0:
            lam_init = 0.8 - 0.6 * math.exp(-0.3 * i)
            pT = run_k1(hT, g[0], ml, mc, P["w_in_even"][j])
            oT, sfT, srT = run_k2e(pT, j, lam_init, P)
            bg = np.ascontiguousarray(P["s5_b_glu"][j].reshape(8, 128).T)
            h1T, u2T = run_k3a(True, hT, {"oT": oT, "sfT": sfT, "srT": srT}, g, ml, mc, P["w_out_even"][j],
                               {"wglu": P["s5_w_glu"][j], "bglu": bg})
        else:
            pT = run_k1(hT, g[0], ml, mc, P["w_in_odd"][j])
            zT = pT[0:4096]
            xbcaT = run_k2oa(pT[4096:4096 + 6144], P["conv_w"][j], P["conv_b"][j])
            yfT, yrT = run_k2ob(xbcaT, pT[10240:10368], P["ssd_dt_bias"][j], P["ssd_a_log"][j])
            dexp = np.repeat(P["ssd_d"][j], 64)
            nw = np.concatenate([P["ssd_norm_w"][j].reshape(32, 128).T, dexp.reshape(32, 128).T], axis=1)
            h1T, u2T = run_k3a(False, hT, {"yfT": yfT, "yrT": yrT, "xT": xbcaT[0:4096], "zT": zT}, g, ml, mc,
                               P["w_out_odd"][j], {"nw": np.ascontiguousarray(nw.astype(np.float32))})
        hT = run_k3b(u2T, h1T, g[3], ml, mc, P["w_up"][i], P["w_down"][i])
    return np.ascontiguousarray(hT[:, CTX:].T)[None].astype(np.float32)
```

```python
import math
import numpy as np
import concourse.bass as bass
import concourse.mybir as mybir
from concourse.bass_utils import run_bass_kernel_spmd

F32 = mybir.dt.float32
BF16 = mybir.dt.bfloat16
AF = mybir.ActivationFunctionType
ALU = mybir.AluOpType
AX = mybir.AxisListType

NCORES = 8
D = 2048
SEQ = 8192
CTX = 256
TOK = SEQ + CTX
TPC = 1056
LAT = 1024
TT = 352
NTT = 3
EPS = 1e-6


class _Op:
    __slots__ = ("eng", "fn", "deps", "needs_inc", "count", "sem", "dma", "idx")

    def __init__(self, eng, fn, dma):
        self.eng = eng
        self.fn = fn
        self.deps = []
        self.needs_inc = False
        self.count = 0
        self.sem = None
        self.dma = dma


ENGS = ("pe", "act", "dve", "pool", "sp")
N_DMA_SEMS = 24


class Prog:
    def __init__(self):
        self.nc = bass.Bass("TRN2", target_bir_lowering=False)
        self.ops = {e: [] for e in ENGS}
        self.lastw = {}
        self.readers = {}
        self.sems = {e: self.nc.alloc_semaphore("s_" + e) for e in ENGS if e != "sp"}
        self.dsems = [self.nc.alloc_semaphore("d%d" % i) for i in range(N_DMA_SEMS)]
        self.dcount = [0] * N_DMA_SEMS
        self.dlast = [None] * N_DMA_SEMS
        self.dnext = 0
        self.out_dmas = []
        self.uid = 0

    def din(self, name, shape, dt=F32):
        return self.nc.dram_tensor(name, list(shape), dt, kind="ExternalInput").ap()

    def dout(self, name, shape, dt=F32):
        return self.nc.dram_tensor(name, list(shape), dt, kind="ExternalOutput").ap()

    def sb(self, name, shape, dt=F32):
        return self.nc.alloc_sbuf_tensor(name, list(shape), dt)

    def ps(self, name, shape, dt=F32):
        return self.nc.alloc_psum_tensor(name, list(shape), dt)

    def add(self, eng, fn, r=(), w=(), dma=False, out=False):
        op = _Op(eng, fn, dma)
        deps = []
        seen = set()

        def push(d):
            if d is None or id(d) in seen or d is op:
                return
            seen.add(id(d))
            deps.append(d)

        for b in r:
            push(self.lastw.get(b))
        for b in w:
            push(self.lastw.get(b))
            for rd in self.readers.get(b, {}).values():
                push(rd)
        if dma:
            k = self.dnext
            self.dnext = (k + 1) % N_DMA_SEMS
            push(self.dlast[k])
            self.dlast[k] = op
            self.dcount[k] += 16
            op.sem = self.dsems[k]
            op.count = self.dcount[k]
            op.needs_inc = True
        final = []
        for d in deps:
            if (not d.dma) and d.eng == "pe" and eng == "pe" and not dma:
                continue
            d.needs_inc = True
            final.append(d)
        op.deps = final
        for b in r:
            key = ("dma", id(op)) if dma else eng
            self.readers.setdefault(b, {})[key] = op
        for b in w:
            self.lastw[b] = op
            self.readers[b] = {}
        self.ops[eng].append(op)
        if out:
            self.out_dmas.append(op)
        return op

    def finish(self):
        fin = _Op("sp", None, False)
        fin.deps = list(self.out_dmas)
        self.ops["sp"].append(fin)
        for e in ENGS:
            c = 0
            for op in self.ops[e]:
                if op.dma or op.fn is None:
                    continue
                if op.needs_inc:
                    c += 1
                    op.count = c
                    op.sem = self.sems[e]
        nc = self.nc
        progs = self.ops

        def emit(e, ops):
            waited = {}
            for op in ops:
                for d in op.deps:
                    key = id(d.sem)
                    if waited.get(key, 0) < d.count:
                        e.wait_ge(d.sem, d.count)
                        waited[key] = d.count
                if op.fn is None:
                    continue
                ins = op.fn(e)
                if op.dma:
                    ins.then_inc(op.sem, 16)
                elif op.needs_inc:
                    ins.then_inc(op.sem, 1)

        with nc.Block() as block:
            @block.tensor
            def _(e):
                emit(e, progs["pe"])

            @block.scalar
            def _(e):
                emit(e, progs["act"])

            @block.vector
            def _(e):
                emit(e, progs["dve"])

            @block.gpsimd
            def _(e):
                emit(e, progs["pool"])

            @block.sync
            def _(e):
                emit(e, progs["sp"])
        return nc

    def dma(self, q, out, in_, r=(), w=(), final=False):
        return self.add(q, lambda e, o=out, i=in_: e.dma_start(out=o, in_=i), r=r, w=w, dma=True, out=final)

    def mm(self, out, lhsT, rhs, start, stop, r=(), w=()):
        return self.add("pe", lambda e, o=out, l=lhsT, rr=rhs, s=start, t=stop: e.matmul(o, l, rr, start=s, stop=t), r=r, w=w)

    def act(self, out, in_, func, r=(), w=(), bias=None, scale=None, eng="act"):
        def fn(e, o=out, i=in_, f=func, b=bias, sc=scale):
            kw = {}
            if b is not None:
                kw["bias"] = b
            if sc is not None:
                kw["scale"] = sc
            return e.activation(out=o, in_=i, func=f, **kw)
        return self.add("act", fn, r=r, w=w)

    def tt(self, eng, out, in0, in1, op, r=(), w=()):
        return self.add(eng, lambda e, o=out, a=in0, b=in1, p=op: e.tensor_tensor(out=o, in0=a, in1=b, op=p), r=r, w=w)

    def ts(self, eng, out, in0, s1, op0, s2=None, op1=None, r=(), w=()):
        def fn(e, o=out, a=in0, x1=s1, x2=s2, p0=op0, p1=op1):
            if p1 is None:
                return e.tensor_scalar(out=o, in0=a, scalar1=x1, scalar2=None, op0=p0)
            return e.tensor_scalar(out=o, in0=a, scalar1=x1, scalar2=x2, op0=p0, op1=p1)
        return self.add(eng, fn, r=r, w=w)

    def stt(self, out, in0, scalar, in1, op0, op1, r=(), w=()):
        return self.add("dve", lambda e, o=out, a=in0, s=scalar, b=in1, p0=op0, p1=op1:
                        e.scalar_tensor_tensor(out=o, in0=a, scalar=s, in1=b, op0=p0, op1=p1), r=r, w=w)

    def copy(self, eng, out, in_, r=(), w=()):
        if eng == "act":
            return self.add("act", lambda e, o=out, i=in_: e.copy(out=o, in_=i), r=r, w=w)
        return self.add(eng, lambda e, o=out, i=in_: e.tensor_copy(out=o, in_=i), r=r, w=w)

    def memset(self, eng, ap, val, w=()):
        return self.add(eng, lambda e, a=ap, v=val: e.memset(a, v), w=w)

    def recip(self, out, in_, r=(), w=()):
        return self.add("dve", lambda e, o=out, i=in_: e.reciprocal(out=o, in_=i), r=r, w=w)


_TRACE = {"on": False, "log": []}


def _run(pg, in_maps):
    nc = pg.finish()
    if _TRACE["on"]:
        res = run_bass_kernel_spmd(nc, in_maps, core_ids=list(range(NCORES)), trace=True)
        _TRACE["log"].append(res.exec_time_ns)
        print("EXEC_NS", res.exec_time_ns, {e: len(pg.ops[e]) for e in ENGS}, flush=True)
    else:
        res = run_bass_kernel_spmd(nc, in_maps, core_ids=list(range(NCORES)))
    return res.results


def build_k0():
    pg = Prog()
    NJ = 12
    cc = pg.din("cc", [128, 32])
    wm = pg.din("wm", [4, 2048, 1536])
    bm = pg.din("bm", [128, 4 * NJ])
    o = pg.dout("modT", [128, 4 * NJ * 2])
    cct = pg.sb("cct", [128, 32])
    sc = pg.sb("sc", [128, 32])
    bmt = pg.sb("bmt", [128, 4 * NJ])
    ot = pg.sb("ot", [128, 4 * NJ * 2])
    wt = [pg.sb("wt%d" % i, [128, 16, 768]) for i in range(2)]
    pst = [pg.ps("ps%d" % i, [128, 512]) for i in range(2)]
    pg.dma("sp", cct[:], cc[:, :], w=["cct"])
    pg.dma("sp", bmt[:], bm[:, :], w=["bmt"])
    pg.act(sc[:], cct[:], AF.Silu, r=["cct"], w=["sc"])
    n = 0
    for i in range(4):
        for hf in range(2):
            b = n % 2
            wv = wm[i, :, hf * 768:(hf + 1) * 768].rearrange("(dc p) n -> p dc n", p=128)
            for q4 in range(4):
                pg.dma("sp", wt[b][:, 4 * q4:4 * q4 + 4, :], wv[:, 4 * q4:4 * q4 + 4, :], w=["wt%d_%d" % (b, q4)])
            for jj in range(6):
                j = hf * 6 + jj
                pb = (i * NJ + j) % 2
                for dc in range(16):
                    pg.mm(pst[pb][:, 0:2], wt[b][:, dc, jj * 128:(jj + 1) * 128], sc[:, 2 * dc:2 * dc + 2],
                          dc == 0, dc == 15, r=["wt%d_%d" % (b, dc // 4), "sc"], w=["ps%d" % pb])
                col = (i * NJ + j)
                pg.ts("dve", ot[:, 2 * col:2 * col + 2], pst[pb][:, 0:2], bmt[:, col:col + 1], ALU.add,
                      r=["ps%d" % pb, "bmt"], w=["ot"])
            n += 1
    pg.dma("sp", o[:, :], ot[:], r=["ot"], final=True)
    return pg


def run_k0(c, c_ctx, w_mod, b_mod):
    cc = np.stack([c.reshape(16, 128), c_ctx.reshape(16, 128)], axis=-1)
    cc = np.ascontiguousarray(cc.transpose(1, 0, 2).reshape(128, 32))
    in_maps = []
    for k in range(NCORES):
        wm = np.ascontiguousarray(w_mod[:, :, 1536 * k:1536 * (k + 1)])
        bm = b_mod[:, 1536 * k:1536 * (k + 1)].reshape(4, 12, 128).transpose(2, 0, 1).reshape(128, 48)
        in_maps.append({"cc": cc, "wm": wm, "bm": np.ascontiguousarray(bm)})
    res = _run(build_k0(), in_maps)
    mod = np.zeros((4, 2, 12288), np.float32)
    for k in range(NCORES):
        r = res[k]["modT"].reshape(128, 4, 12, 2)
        mod[:, :, 1536 * k:1536 * (k + 1)] = r.transpose(1, 3, 2, 0).reshape(4, 2, 1536)
    return mod


class WStream:
    def __init__(self, pg, name, KC, gw=512, nbuf=3):
        self.pg, self.name, self.KC, self.gw, self.nbuf = pg, name, KC, gw, nbuf
        self.bufs = [pg.sb("%s_w%d" % (name, i), [128, KC, gw], BF16) for i in range(nbuf)]
        self.n = 0

    def load(self, wap, c0, cols):
        b = self.n % self.nbuf
        self.n += 1
        wv = wap[:, c0:c0 + cols].rearrange("(kc p) n -> p kc n", p=128)
        step = 4
        for q in range(0, self.KC, step):
            self.pg.dma("pool", self.bufs[b][:, q:q + step, 0:cols], wv[:, q:q + step, :],
                        w=["%s_w%d_%d" % (self.name, b, q // step)])
        return b

    def rid(self, b, kc):
        return "%s_w%d_%d" % (self.name, b, kc // 4)


def linear_fm(pg, ws, wap, N, X, xid, KC, pss, evac, col_ranges=None):
    ng = (N + ws.gw - 1) // ws.gw
    cnt = 0
    for g in range(ng):
        cols = min(ws.gw, N - g * ws.gw)
        b = ws.load(wap, g * ws.gw, cols)
        for ocl in range(cols // 128):
            oc = g * (ws.gw // 128) + ocl
            for tt in range(NTT):
                pb = cnt % len(pss)
                cnt += 1
                ps, psid = pss[pb]
                for kc in range(KC):
                    pg.mm(ps[:, 0:TT], ws.bufs[b][:, kc, ocl * 128:(ocl + 1) * 128], X[:, kc, tt * TT:(tt + 1) * TT],
                          kc == 0, kc == KC - 1, r=[ws.rid(b, kc), xid(kc)], w=[psid])
                evac(oc, tt, ps[:, 0:TT], psid)


def sumsq_rstd(pg, name, src_fn, src_id, KC, dim, ones, sq_bufs, ss_ps, rstd, eng_sq="act"):
    for c in range(KC):
        sb_, sid = sq_bufs[c % len(sq_bufs)]
        pg.act(sb_[:, :], src_fn(c), AF.Square, r=[src_id(c)], w=[sid])
        for tt in range(NTT):
            ps, psid = ss_ps[tt]
            pg.mm(ps[:, 0:TT], ones[:, :], sb_[:, tt * TT:(tt + 1) * TT], c == 0, c == KC - 1, r=[sid, "ones"], w=[psid])
    for tt in range(NTT):
        ps, psid = ss_ps[tt]
        pg.act(rstd[:, tt * TT:(tt + 1) * TT], ps[:, 0:TT], AF.Sqrt, r=[psid], w=[name + "_rs%d" % tt],
               bias=pg.eps_ap, scale=1.0 / dim)
        pg.recip(rstd[:, tt * TT:(tt + 1) * TT], rstd[:, tt * TT:(tt + 1) * TT], r=[name + "_rs%d" % tt], w=[name + "_rs%d" % tt])
    return [name + "_rs%d" % tt for tt in range(NTT)]


def setup_consts(pg):
    ones = pg.sb("ones", [128, 128], BF16)
    pg.memset("dve", ones[:, :], 1.0, w=["ones"])
    epst = pg.sb("epst", [128, 1])
    pg.memset("dve", epst[:, :], EPS, w=["eps"])
    pg.eps_ap = epst[:, 0:1]
    return ones


def build_k1(N):
    pg = Prog()
    hT = pg.din("hT", [D, TPC])
    mp = pg.din("mp", [128, 16 * 5])
    w = pg.din("w", [D, N])
    pT = pg.dout("pT", [N, TPC])
    ones = setup_consts(pg)
    H = pg.sb("H", [128, 16, TPC])
    U = pg.sb("U", [128, 16, TPC], BF16)
    mpt = pg.sb("mpt", [128, 16, 5])
    AB = pg.sb("AB", [128, 16, 2])
    rstd = pg.sb("rstd", [128, TPC])
    sqb = [(pg.sb("sq%d" % i, [128, TPC], BF16), "sq%d" % i) for i in range(2)]
    tmpb = [(pg.sb("tmp%d" % i, [128, TPC]), "tmp%d" % i) for i in range(2)]
    Ob = [(pg.sb("O%d" % i, [128, TPC]), "O%d" % i) for i in range(3)]
    ssp = [(pg.ps("ss%d" % i, [128, 512]), "ss%d" % i) for i in range(3)]
    mmp = [(pg.ps("mm%d" % i, [128, 512]), "mm%d" % i) for i in range(4)]
    ws = WStream(pg, "win", 16)

    hv = hT.rearrange("(c p) t -> p c t", p=128)
    for q in range(4):
        pg.dma("sp", H[:, 4 * q:4 * q + 4, :], hv[:, 4 * q:4 * q + 4, :], w=["H%d" % q])
    pg.dma("sp", mpt[:, :, :], mp.rearrange("p (c f) -> p c f", f=5), w=["mpt"])
    rs_ids = sumsq_rstd(pg, "n0", lambda c: H[:, c, :], lambda c: "H%d" % (c // 4), 16, D, ones, sqb, ssp, rstd)
    pg.stt(AB[:, :, 0], mpt[:, :, 2], 1.0, mpt[:, :, 0], ALU.add, ALU.mult, r=["mpt"], w=["AB"])
    pg.stt(AB[:, :, 1], mpt[:, :, 4], 1.0, mpt[:, :, 0], ALU.add, ALU.mult, r=["mpt"], w=["AB"])
    for c in range(16):
        tb, tid = tmpb[c % 2]
        pg.tt("dve", tb[:, :], H[:, c, :], rstd[:, :], ALU.mult, r=["H%d" % (c // 4)] + rs_ids, w=[tid])
        pg.act(U[:, c, 0:LAT], tb[:, 0:LAT], AF.Identity, r=[tid, "AB", "mpt"], w=["U%d" % c],
               bias=mpt[:, c, 1:2], scale=AB[:, c, 0:1])
        pg.act(U[:, c, LAT:TPC], tb[:, LAT:TPC], AF.Identity, r=[tid, "AB", "mpt"], w=["U%d" % c],
               bias=mpt[:, c, 3:4], scale=AB[:, c, 1:2])
    state = {"n": 0}

    def evac(oc, tt, ps, psid):
        ob, oid = Ob[oc % 3]
        eng = "act" if (state["n"] % 2 == 0) else "dve"
        state["n"] += 1
        pg.copy(eng, ob[:, tt * TT:(tt + 1) * TT], ps, r=[psid], w=[oid + "_%d" % tt])
        if tt == NTT - 1:
            pg.dma("sp", pT[oc * 128:(oc + 1) * 128, :], ob[:, :], r=[oid + "_%d" % t for t in range(NTT)], final=True)

    linear_fm(pg, ws, w, N, U, lambda kc: "U%d" % kc, 16, mmp, evac)
    return pg


def tok_cols(k):
    return np.concatenate([CTX + np.arange(LAT * k, LAT * (k + 1)), np.arange(32 * k, 32 * (k + 1))])


def shard_T(aT):
    return [np.ascontiguousarray(aT[:, tok_cols(k)]) for k in range(NCORES)]


def unshard_T(parts):
    F = parts[0].shape[0]
    out = np.empty((F, TOK), parts[0].dtype)
    for k in range(NCORES):
        out[:, tok_cols(k)] = parts[k]
    return out


def pack_cols(vecs):
    a = np.stack([v.reshape(16, 128) for v in vecs], axis=-1)
    return np.ascontiguousarray(a.transpose(1, 0, 2).reshape(128, -1)).astype(np.float32)


def run_k1(hT, g0, mod_l, mod_c, w_in):
    N = w_in.shape[1]
    m = lambda v, i: v[i * D:(i + 1) * D]
    mp = pack_cols([g0, m(mod_l, 0), m(mod_l, 1), m(mod_c, 0), m(mod_c, 1)])
    hs = shard_T(hT)
    in_maps = [{"hT": hs[k], "mp": mp, "w": w_in} for k in range(NCORES)]
    res = _run(build_k1(N), in_maps)
    return unshard_T([res[k]["pT"] for k in range(NCORES)])


def residual_tail(pg, name, Y, yid, rstd, rs_ids, coef, coef_id, h_dram, out_dram, scr, inplace=False):
    hb = scr[0:2]
    tb = scr[2:4]
    ob = scr[4:6]
    for c in range(16):
        h_, hid = hb[c % 2]
        t_, tid = tb[c % 2]
        if inplace:
            oo, oid = Y[:, c, :], yid(c)
        else:
            o_, oid = ob[c % 2]
            oo = o_[:, :]
        pg.dma("sp", h_[:, :], h_dram[c * 128:(c + 1) * 128, :], w=[hid])
        pg.tt("dve", t_[:, :], Y[:, c, :], rstd[:, :], ALU.mult, r=[yid(c)] + rs_ids, w=[tid])
        pg.stt(oo[:, 0:LAT], t_[:, 0:LAT], coef[:, c, 0:1], h_[:, 0:LAT], ALU.mult, ALU.add, r=[tid, hid, coef_id], w=[oid])
        pg.stt(oo[:, LAT:TPC], t_[:, LAT:TPC], coef[:, c, 1:2], h_[:, LAT:TPC], ALU.mult, ALU.add, r=[tid, hid, coef_id], w=[oid])
        pg.dma("sp", out_dram[c * 128:(c + 1) * 128, :], oo, r=[oid], final=True)


def build_k3b():
    pg = Prog()
    uT = pg.din("uT", [D, TPC])
    hT = pg.din("hT", [D, TPC])
    mp = pg.din("mp", [128, 16 * 3])
    wu = pg.din("wu", [D, 4 * D])
    wd = pg.din("wd", [4 * D, D])
    oT = pg.dout("oT", [D, TPC])
    ones = setup_consts(pg)
    U = pg.sb("U", [128, 16, TPC], BF16)
    HQ = pg.sb("HQ", [128, 8, TPC], BF16)
    Y = pg.sb("Y", [128, 16, TPC])
    mpt = pg.sb("mpt", [128, 16, 3])
    coef = pg.sb("coef", [128, 16, 2])
    rstd = pg.sb("rstd", [128, TPC])
    rt = [(pg.sb("rt%d" % i, [128, TT]), "rt%d" % i) for i in range(3)]
    sqb = [(pg.sb("sq%d" % i, [128, TPC], BF16), "sq%d" % i) for i in range(2)]
    ssp = [(pg.ps("ss%d" % i, [128, 512]), "ss%d" % i) for i in range(3)]
    mmp = [(pg.ps("mm%d" % i, [128, 512]), "mm%d" % i) for i in range(4)]
    wsu = WStream(pg, "wu", 16, gw=256)
    wsd = WStream(pg, "wd", 8, gw=512)
    uv = uT.rearrange("(c p) t -> p c t", p=128)
    for q in range(4):
        pg.dma("pool", U[:, 4 * q:4 * q + 4, :], uv[:, 4 * q:4 * q + 4, :], w=["U%d" % c for c in range(4 * q, 4 * q + 4)])
    pg.dma("sp", mpt[:, :, :], mp.rearrange("p (c f) -> p c f", f=3), w=["mpt"])
    pg.tt("dve", coef[:, :, 0], mpt[:, :, 0], mpt[:, :, 1], ALU.mult, r=["mpt"], w=["coef"])
    pg.tt("dve", coef[:, :, 1], mpt[:, :, 0], mpt[:, :, 2], ALU.mult, r=["mpt"], w=["coef"])
    st = {"n": 0}
    for e in range(8):
        def evac_up(oc, tt, ps, psid):
            r_, rid = rt[st["n"] % 3]
            st["n"] += 1
            pg.act(r_[:, :], ps, AF.Relu, r=[psid], w=[rid])
            pg.tt("dve", HQ[:, oc, tt * TT:(tt + 1) * TT], r_[:, :], r_[:, :], ALU.mult, r=[rid], w=["HQ%d" % oc])

        linear_fm(pg, wsu, wu[:, e * 1024:(e + 1) * 1024], 1024, U, lambda kc: "U%d" % kc, 16, mmp, evac_up)

        def evac_dn(oc, tt, ps, psid, e=e):
            dst = Y[:, oc, tt * TT:(tt + 1) * TT]
            if e == 0:
                pg.copy("act", dst, ps, r=[psid], w=["Y%d" % oc])
            else:
                pg.tt("dve", dst, ps, dst, ALU.add, r=[psid, "Y%d" % oc], w=["Y%d" % oc])

        linear_fm(pg, wsd, wd[e * 1024:(e + 1) * 1024, :], D, HQ, lambda kc: "HQ%d" % kc, 8, mmp, evac_dn)
    rs_ids = sumsq_rstd(pg, "n3", lambda c: Y[:, c, :], lambda c: "Y%d" % c, 16, D, ones, sqb, ssp, rstd)
    scr = [(pg.sb("scr%d" % i, [128, TPC]), "scr%d" % i) for i in range(6)]
    residual_tail(pg, "rt", Y, lambda c: "Y%d" % c, rstd, rs_ids, coef, "coef", hT, oT, scr)
    return pg


def run_k3b(u2T, h1T, g3, mod_l, mod_c, w_up, w_down):
    m = lambda v, i: v[i * D:(i + 1) * D]
    mp = pack_cols([g3, m(mod_l, 5), m(mod_c, 5)])
    us, hs = shard_T(u2T), shard_T(h1T)
    in_maps = [{"uT": us[k], "hT": hs[k], "mp": mp, "wu": w_up, "wd": w_down} for k in range(NCORES)]
    res = _run(build_k3b(), in_maps)
    return unshard_T([res[k]["oT"] for k in range(NCORES)])


GELU_C = 0.044715
GELU_S = 2.0 * math.sqrt(2.0 / math.pi)


def build_k3a(even):
    pg = Prog()
    hT = pg.din("hT", [D, TPC])
    mp = pg.din("mp", [128, 16 * 8])
    KC = 16 if even else 32
    wout = pg.din("wout", [KC * 128, D])
    h1T = pg.dout("h1T", [D, TPC])
    u2T = pg.dout("u2T", [D, TPC])
    ones = setup_consts(pg)
    M = pg.sb("M", [128, KC, TPC], BF16)
    Y = pg.sb("Y", [128, 16, TPC])
    mpt = pg.sb("mpt", [128, 16, 8])
    coef = pg.sb("coef", [128, 16, 2])
    AB = pg.sb("AB", [128, 16, 2])
    rstd = pg.sb("rstd", [128, TPC])
    rstd2 = pg.sb("rstd2", [128, TPC])
    sqb = [(pg.sb("sq%d" % i, [128, TPC], BF16), "sq%d" % i) for i in range(2)]
    ssp = [(pg.ps("ss%d" % i, [128, 512]), "ss%d" % i) for i in range(3)]
    mmp = [(pg.ps("mm%d" % i, [128, 512]), "mm%d" % i) for i in range(4)]
    scr = [(pg.sb("scr%d" % i, [128, TPC]), "scr%d" % i) for i in range(4)]
    pg.dma("sp", mpt[:, :, :], mp.rearrange("p (c f) -> p c f", f=8), w=["mpt"])
    pg.tt("dve", coef[:, :, 0], mpt[:, :, 0], mpt[:, :, 1], ALU.mult, r=["mpt"], w=["coef"])
    pg.tt("dve", coef[:, :, 1], mpt[:, :, 0], mpt[:, :, 2], ALU.mult, r=["mpt"], w=["coef"])
    pg.stt(AB[:, :, 0], mpt[:, :, 5], 1.0, mpt[:, :, 3], ALU.add, ALU.mult, r=["mpt"], w=["AB"])
    pg.stt(AB[:, :, 1], mpt[:, :, 7], 1.0, mpt[:, :, 3], ALU.add, ALU.mult, r=["mpt"], w=["AB"])
    if even:
        ws = WStream(pg, "w", 16, gw=512, nbuf=3)
        oT = pg.din("oT", [1024, TPC])
        sfT = pg.din("sfT", [1024, TPC])
        srT = pg.din("srT", [1024, TPC])
        wglu = pg.din("wglu", [1024, 1024])
        bglu = pg.din("bglu", [128, 8])
        bgt = pg.sb("bgt", [128, 8])
        Gb = pg.sb("Gb", [128, 8, TPC], BF16)
        pg.dma("sp", bgt[:, :], bglu[:, :], w=["bgt"])
        ov = oT.rearrange("(c p) t -> p c t", p=128)
        for q in range(2):
            pg.dma("pool", M[:, 4 * q:4 * q + 4, :], ov[:, 4 * q:4 * q + 4, :], w=["M%d" % c for c in range(4 * q, 4 * q + 4)])
        fa = scr[0:2]
        fb = scr[2:4]
        for c in range(8):
            a_, aid = fa[c % 2]
            b_, bid = fb[c % 2]
            G = Y[:, 8 + c, :]
            gid = "Y%d" % (8 + c)
            pg.dma("sp", a_[:, :], sfT[c * 128:(c + 1) * 128, :], w=[aid])
            pg.dma("sp", b_[:, :], srT[c * 128:(c + 1) * 128, :], w=[bid])
            pg.tt("dve", a_[:, :], a_[:, :], b_[:, :], ALU.add, r=[aid, bid], w=[aid])
            pg.tt("pool", b_[:, :], a_[:, :], a_[:, :], ALU.mult, r=[aid], w=[bid])
            pg.ts("dve", b_[:, :], b_[:, :], GELU_C, ALU.mult, 1.0, ALU.add, r=[bid], w=[bid])
            pg.tt("pool", b_[:, :], b_[:, :], a_[:, :], ALU.mult, r=[aid, bid], w=[bid])
            pg.act(b_[:, :], b_[:, :], AF.Sigmoid, r=[bid], w=[bid], scale=GELU_S)
            pg.tt("dve", G, a_[:, :], b_[:, :], ALU.mult, r=[aid, bid], w=[gid])
            pg.copy("pool", Gb[:, c, :], G, r=[gid], w=["Gb%d" % c])
        st = {"n": 0}
        zt = [(pg.sb("zt%d" % i, [128, TT]), "zt%d" % i) for i in range(3)]

        def evac_glu(oc, tt, ps, psid):
            z_, zid = zt[st["n"] % 3]
            st["n"] += 1
            pg.act(z_[:, :], ps, AF.Sigmoid, r=[psid, "bgt"], w=[zid], bias=bgt[:, oc:oc + 1])
            pg.tt("dve", M[:, 8 + oc, tt * TT:(tt + 1) * TT], z_[:, :], Y[:, 8 + oc, tt * TT:(tt + 1) * TT], ALU.mult,
                  r=[zid, "Y%d" % (8 + oc)], w=["M%d" % (8 + oc)])

        ws.KC = 8
        linear_fm(pg, ws, wglu, 1024, Gb, lambda kc: "Gb%d" % kc, 8, mmp, evac_glu)
        ws.KC = 16

        def evac_out(oc, tt, ps, psid):
            eng = "act" if (tt % 2 == 0) else "dve"
            pg.copy(eng, Y[:, oc, tt * TT:(tt + 1) * TT], ps, r=[psid], w=["Y%d" % oc])
    else:
        ws = WStream(pg, "w", 32, gw=256, nbuf=2)
        yfT = pg.din("yfT", [4096, TPC])
        yrT = pg.din("yrT", [4096, TPC])
        xT = pg.din("xT", [4096, TPC])
        zT = pg.din("zT", [4096, TPC])
        nw = pg.din("nw", [128, 64])
        nwt = pg.sb("nwt", [128, 64])
        rstdg = pg.sb("rstdg", [128, TPC])
        pg.dma("sp", nwt[:, :], nw[:, :], w=["nwt"])
        HW = TPC // 2
        half = [(scr[i // 2][0][:, (i % 2) * HW:(i % 2 + 1) * HW], scr[i // 2][1] + "_h%d" % (i % 2)) for i in range(8)]
        for c in range(32):
            sb_, sid = sqb[c % 2]
            rows = slice(c * 128, (c + 1) * 128)
            for hf in range(2):
                cols = slice(hf * HW, (hf + 1) * HW)
                base = 4 * ((2 * c + hf) % 2)
                (a_, aid), (b_, bid), (x_, xid), (z_, zid) = half[base:base + 4]
                pg.dma("sp", a_, yfT[rows, cols], w=[aid])
                pg.dma("sp", b_, yrT[rows, cols], w=[bid])
                pg.dma("sp", x_, xT[rows, cols], w=[xid])
                pg.dma("sp", z_, zT[rows, cols], w=[zid])
                pg.tt("dve", a_, a_, b_, ALU.add, r=[aid, bid], w=[aid])
                pg.stt(a_, x_, nwt[:, 32 + c:33 + c], a_, ALU.mult, ALU.add, r=[aid, xid, "nwt"], w=[aid])
                pg.act(z_, z_, AF.Silu, r=[zid], w=[zid])
                pg.tt("dve", a_, a_, z_, ALU.mult, r=[aid, zid], w=[aid])
                pg.act(sb_[:, cols], a_, AF.Square, r=[aid], w=[sid + "_h%d" % hf])
                pg.act(M[:, c, cols], a_, AF.Identity, r=[aid, "nwt"], w=["M%d" % c], scale=nwt[:, c:c + 1])
            for tt in range(NTT):
                ps, psid = ssp[tt]
                pg.mm(ps[:, 0:TT], ones[:, :], sb_[:, tt * TT:(tt + 1) * TT], c == 0, c == 31,
                      r=[sid + "_h0", sid + "_h1", "ones"], w=[psid])
        for i in range(4):
            t_, tid = scr[i]
            pg.copy("dve", t_[:, 0:1], t_[:, 0:1], w=[tid + "_h0", tid + "_h1", tid])
        for i in range(2):
            t_, tid = sqb[i]
            pg.copy("dve", t_[:, 0:1], t_[:, 0:1], w=[tid + "_h0", tid + "_h1", tid])
        rg_ids = []
        for tt in range(NTT):
            ps, psid = ssp[tt]
            sl = rstdg[:, tt * TT:(tt + 1) * TT]
            pg.act(sl, ps[:, 0:TT], AF.Sqrt, r=[psid], w=["rg%d" % tt], bias=pg.eps_ap, scale=1.0 / 4096)
            pg.recip(sl, sl, r=["rg%d" % tt], w=["rg%d" % tt])
            rg_ids.append("rg%d" % tt)

        def evac_out(oc, tt, ps, psid):
            pg.tt("dve", Y[:, oc, tt * TT:(tt + 1) * TT], ps, rstdg[:, tt * TT:(tt + 1) * TT], ALU.mult,
                  r=[psid, "rg%d" % tt], w=["Y%d" % oc])

    linear_fm(pg, ws, wout, D, M, lambda kc: "M%d" % kc, KC, mmp, evac_out)
    rs_ids = sumsq_rstd(pg, "n1", lambda c: Y[:, c, :], lambda c: "Y%d" % c, 16, D, ones, sqb, ssp, rstd)
    residual_tail(pg, "r1", Y, lambda c: "Y%d" % c, rstd, rs_ids, coef, "coef", hT, h1T, scr, inplace=True)
    rs2 = sumsq_rstd(pg, "n2", lambda c: Y[:, c, :], lambda c: "Y%d" % c, 16, D, ones, sqb, ssp, rstd2)
    tb = scr[0:2]
    ub = scr[2:4]
    for c in range(16):
        t_, tid = tb[c % 2]
        u_, uid = ub[c % 2]
        pg.tt("dve", t_[:, :], Y[:, c, :], rstd2[:, :], ALU.mult, r=["Y%d" % c] + rs2, w=[tid])
        pg.act(u_[:, 0:LAT], t_[:, 0:LAT], AF.Identity, r=[tid, "AB", "mpt"], w=[uid], bias=mpt[:, c, 4:5], scale=AB[:, c, 0:1])
        pg.act(u_[:, LAT:TPC], t_[:, LAT:TPC], AF.Identity, r=[tid, "AB", "mpt"], w=[uid], bias=mpt[:, c, 6:7], scale=AB[:, c, 1:2])
        pg.dma("sp", u2T[c * 128:(c + 1) * 128, :], u_[:, :], r=[uid], final=True)
    return pg


def run_k3a(even, hT, mix, g, mod_l, mod_c, w_out, extra):
    m = lambda v, i: v[i * D:(i + 1) * D]
    mp = pack_cols([g[1], m(mod_l, 2), m(mod_c, 2), g[2], m(mod_l, 3), m(mod_l, 4), m(mod_c, 3), m(mod_c, 4)])
    hs = shard_T(hT)
    sh = {k: shard_T(v) for k, v in mix.items()}
    in_maps = []
    for k in range(NCORES):
        d = {"hT": hs[k], "mp": mp, "wout": w_out}
        for kk in sh:
            d[kk] = sh[kk][k]
        d.update(extra)
        in_maps.append(d)
    res = _run(build_k3a(even), in_maps)
    return unshard_T([res[k]["h1T"] for k in range(NCORES)]), unshard_T([res[k]["u2T"] for k in range(NCORES)])


NCH = TOK // 128


def build_k2ob(nch=NCH):
    pg = Prog()
    T = nch * 128
    xa = pg.din("xa", [2, T, 512])
    dtr = pg.din("dtr", [2, T, 8])
    Bm = pg.din("Bm", [2, T, 128])
    BT = pg.din("BT", [2, 128, T])
    CT = pg.din("CT", [2, 128, T])
    hp = pg.din("hp", [128, 32])
    msk = pg.din("msk", [128, 256])
    y = pg.dout("y", [2, T, 512])
    hpt = pg.sb("hpt", [128, 2, 2, 8])
    mk = pg.sb("mk", [128, 256])
    onesf = pg.sb("onesf", [128, 128])
    Aneg = pg.sb("Aneg", [128, 2, 8])
    pg.dma("sp", hpt[:, :, :, :], hp.rearrange("p (d f h) -> p d f h", d=2, f=2), w=["hpt"])
    pg.dma("sp", mk[:, :], msk[:, :], w=["mk"])
    pg.memset("dve", onesf[:, :], 1.0, w=["onesf"])
    UT = mk[:, 0:128]
    TRI = mk[:, 128:256]
    for d in range(2):
        pg.act(Aneg[:, d, :], hpt[:, d, 1, :], AF.Exp, r=["hpt"], w=["Aneg"])
    pg.ts("dve", Aneg[:, :, :], Aneg[:, :, :], -1.0, ALU.mult, r=["Aneg"], w=["Aneg"])
    NB = 4

    def tiles(nm, shape, dt=F32):
        return [[(pg.sb("%s_%d_%d" % (nm, d, i), shape, dt), "%s_%d_%d" % (nm, d, i)) for i in range(NB)] for d in range(2)]

    Xt = tiles("X", [128, 8, 64])
    Dt = tiles("dt", [128, 8])
    At = tiles("a", [128, 8])
    Bb = tiles("Bb", [128, 128], BF16)
    BTb = tiles("BTb", [128, 128], BF16)
    CTb = tiles("CTb", [128, 128], BF16)
    Rt = tiles("R", [128, 8, 128])
    Et = tiles("E", [128, 8, 128])
    CBm = tiles("CBm", [128, 128])
    MT = tiles("MT", [128, 8, 128], BF16)
    xdt = tiles("xdt", [128, 8, 64], BF16)
    xdec = tiles("xdec", [128, 8, 64], BF16)
    ev = tiles("ev", [128, 3, 8])
    yt = tiles("yt", [128, 8, 64])
    S = [(pg.sb("S%d" % d, [128, 8, 64]), "S%d" % d) for d in range(2)]
    Sb = [(pg.sb("Sb%d" % d, [128, 8, 64], BF16), "Sb%d" % d) for d in range(2)]
    p_seg = [(pg.ps("pseg%d" % i, [128, 512]), "pseg%d" % i) for i in range(2)]
    p_cbt = (pg.ps("pcbt", [128, 512]), "pcbt")
    p_yd = (pg.ps("pyd", [128, 512]), "pyd")
    p_yo = (pg.ps("pyo", [128, 512]), "pyo")
    p_ns = (pg.ps("pns", [128, 512]), "pns")
    p_vec = (pg.ps("pvec", [128, 512]), "pvec")
    for d in range(2):
        pg.memset("dve", S[d][0][:, :, :], 0.0, w=[S[d][1]])
        pg.memset("dve", Sb[d][0][:, :, :], 0.0, w=[Sb[d][1]])

    def bc_h(ap8, n):
        return ap8.unsqueeze(2).to_broadcast([128, 8, n])

    def T_(tl, n):
        ci, d = divmod(n, 2)
        return tl[d][ci % NB]

    def s0(n):
        ci, d = divmod(n, 2)
        t0 = ci * 128
        X, Xi = T_(Xt, n)
        dtt, dti = T_(Dt, n)
        pg.dma("sp", X[:, :, :], xa[d, t0:t0 + 128, :].rearrange("t (h p) -> t h p", h=8), w=[Xi])
        pg.dma("sp", dtt[:, :], dtr[d, t0:t0 + 128, :], w=[dti])
        pg.dma("pool", T_(Bb, n)[0][:, :], Bm[d, t0:t0 + 128, :], w=[T_(Bb, n)[1]])
        pg.dma("pool", T_(BTb, n)[0][:, :], BT[d, :, t0:t0 + 128], w=[T_(BTb, n)[1]])
        pg.dma("pool", T_(CTb, n)[0][:, :], CT[d, :, t0:t0 + 128], w=[T_(CTb, n)[1]])

    def s1(n):
        ci, d = divmod(n, 2)
        X, Xi = T_(Xt, n)
        dtt, dti = T_(Dt, n)
        a_, ai = T_(At, n)
        R_, Ri = T_(Rt, n)
        xd_, xdi = T_(xdt, n)
        pg.tt("dve", dtt[:, :], dtt[:, :], hpt[:, d, 0, :], ALU.add, r=[dti, "hpt"], w=[dti])
        pg.act(dtt[:, :], dtt[:, :], AF.Exp, r=[dti], w=[dti])
        pg.act(dtt[:, :], dtt[:, :], AF.Ln, r=[dti], w=[dti], bias=1.0)
        pg.tt("dve", a_[:, :], dtt[:, :], Aneg[:, d, :], ALU.mult, r=[dti, "Aneg"], w=[ai])
        pg.tt("pool", R_[:, :, :], TRI.unsqueeze(1).to_broadcast([128, 8, 128]), bc_h(a_[:, :], 128), ALU.mult,
              r=[ai, "mk"], w=[Ri])
        pg.tt("pool", xd_[:, :, :], X[:, :, :], bc_h(dtt[:, :], 64), ALU.mult, r=[Xi, dti], w=[xdi])

    def s2(n):
        a_, ai = T_(At, n)
        R_, Ri = T_(Rt, n)
        for hf in range(2):
            ps, pid = p_seg[hf]
            pg.mm(ps[:, :], UT, R_[:, 4 * hf:4 * hf + 4, :].rearrange("p h l -> p (h l)"), True, True, r=["mk", Ri], w=[pid])
        ps, pid = p_cbt
        pg.mm(ps[:, 0:128], T_(BTb, n)[0][:, :], T_(CTb, n)[0][:, :], True, True, r=[T_(BTb, n)[1], T_(CTb, n)[1]], w=[pid])
        pv, pvi = p_vec
        pg.mm(pv[:, 0:8], TRI, a_[:, :], True, True, r=["mk", ai], w=[pvi])
        pg.mm(pv[:, 8:16], UT, a_[:, :], True, True, r=["mk", ai], w=[pvi])
        pg.mm(pv[:, 16:24], onesf[:, :], a_[:, :], True, True, r=["onesf", ai], w=[pvi])

    def s3(n):
        E_, Ei = T_(Et, n)
        CB_, CBi = T_(CBm, n)
        ev_, evi = T_(ev, n)
        for hf in range(2):
            ps, pid = p_seg[hf]
            pg.act(E_[:, 4 * hf:4 * hf + 4, :].rearrange("p h l -> p (h l)"), ps[:, :], AF.Exp, r=[pid], w=[Ei + "_%d" % hf])
        ps, pid = p_cbt
        pg.tt("dve", CB_[:, :], ps[:, 0:128], TRI, ALU.mult, r=[pid, "mk"], w=[CBi])
        pv, pvi = p_vec
        pg.act(ev_[:, :, :].rearrange("p a h -> p (a h)"), pv[:, 0:24], AF.Exp, r=[pvi], w=[evi])

    def s4(n):
        E_, Ei = T_(Et, n)
        CB_, CBi = T_(CBm, n)
        M_, Mi = T_(MT, n)
        xd_, xdi = T_(xdt, n)
        xc_, xci = T_(xdec, n)
        ev_, evi = T_(ev, n)
        pg.tt("dve", M_[:, :, :], E_[:, :, :], CB_[:, :].unsqueeze(1).to_broadcast([128, 8, 128]), ALU.mult,
              r=[Ei + "_0", Ei + "_1", CBi], w=[Mi])
        pg.tt("dve", xc_[:, :, :], xd_[:, :, :], bc_h(ev_[:, 1, :], 64), ALU.mult, r=[xdi, evi], w=[xci])

    def s5(n):
        ci, d = divmod(n, 2)
        M_, Mi = T_(MT, n)
        xd_, xdi = T_(xdt, n)
        xc_, xci = T_(xdec, n)
        pyd, pydi = p_yd
        for h in range(8):
            pg.mm(pyd[:, 64 * h:64 * h + 64], M_[:, h, :], xd_[:, h, :], True, True, r=[Mi, xdi], w=[pydi])
        pyo, pyoi = p_yo
        pg.mm(pyo[:, :], T_(CTb, n)[0][:, :], Sb[d][0][:, :, :].rearrange("p h q -> p (h q)"), True, True,
              r=[T_(CTb, n)[1], Sb[d][1]], w=[pyoi])
        pns, pnsi = p_ns
        pg.mm(pns[:, :], T_(Bb, n)[0][:, :], xc_[:, :, :].rearrange("p h q -> p (h q)"), True, True, r=[T_(Bb, n)[1], xci], w=[pnsi])

    def s6(n):
        ci, d = divmod(n, 2)
        t0 = ci * 128
        ev_, evi = T_(ev, n)
        y_, yi = T_(yt, n)
        S_, Si = S[d]
        Sb_, Sbi = Sb[d]
        pyd, pydi = p_yd
        pyo, pyoi = p_yo
        pns, pnsi = p_ns
        pg.tt("dve", S_[:, :, :], S_[:, :, :], bc_h(ev_[:, 2, :], 64), ALU.mult, r=[Si, evi], w=[Si])
        pg.tt("dve", S_[:, :, :], pns[:, :].rearrange("p (h q) -> p h q", h=8), S_[:, :, :], ALU.add, r=[pnsi, Si], w=[Si])
        pg.copy("act", Sb_[:, :, :], S_[:, :, :], r=[Si], w=[Sbi])
        pg.tt("dve", y_[:, :, :], pyo[:, :].rearrange("p (h q) -> p h q", h=8), bc_h(ev_[:, 0, :], 64), ALU.mult,
              r=[pyoi, evi], w=[yi])
        pg.tt("dve", y_[:, :, :], pyd[:, :].rearrange("p (h q) -> p h q", h=8), y_[:, :, :], ALU.add, r=[pydi, yi], w=[yi])
        pg.dma("sp", y[d, t0:t0 + 128, :], y_[:, :, :].rearrange("p h q -> p (h q)"), r=[yi], final=True)

    stages = [s0, s1, s2, s3, s4, s5, s6]
    NS = nch * 2
    for k in range(NS + len(stages) - 1):
        for si in range(len(stages) - 1, -1, -1):
            n = k - si
            if 0 <= n < NS:
                stages[si](n)
    return pg


def ssd_masks():
    k = np.arange(128)
    UT = (k[:, None] > k[None, :]).astype(np.float32)
    TRI = (k[:, None] <= k[None, :]).astype(np.float32)
    return np.ascontiguousarray(np.concatenate([UT, TRI], axis=1))


def build_k2oa():
    pg = Prog()
    xin = pg.din("xin", [768, TOK])
    cw = pg.din("cw", [128, 36])
    xo = pg.dout("xo", [768, TOK])
    cwt = pg.sb("cwt", [128, 6, 6])
    pg.dma("sp", cwt[:, :, :], cw.rearrange("p (c f) -> p c f", f=6), w=["cwt"])
    Xb = [(pg.sb("X%d" % i, [128, TOK]), "X%d" % i) for i in range(2)]
    Ab = [(pg.sb("A%d" % i, [128, TOK]), "A%d" % i) for i in range(2)]
    segs = [(0, CTX), (CTX, TOK)]
    for c in range(6):
        X, Xi = Xb[c % 2]
        A, Ai = Ab[c % 2]
        for q in range(4):
            pg.dma("sp", X[:, q * 2112:(q + 1) * 2112], xin[c * 128:(c + 1) * 128, q * 2112:(q + 1) * 2112], w=[Xi])
        for (s, e) in segs:
            pg.ts("dve", A[:, s:e], X[:, s:e], cwt[:, c, 2:3], ALU.mult, r=[Xi, "cwt"], w=[Ai])
            for k in (0, 1, 3, 4):
                o = k - 2
                lo = s + max(0, -o)
                hi = e - max(0, o)
                pg.stt(A[:, lo:hi], X[:, lo + o:hi + o], cwt[:, c, k:k + 1], A[:, lo:hi], ALU.mult, ALU.add,
                       r=[Xi, Ai, "cwt"], w=[Ai])
        for q in range(4):
            sl = slice(q * 2112, (q + 1) * 2112)
            pg.act(A[:, sl], A[:, sl], AF.Silu, r=[Ai], w=[Ai], bias=cwt[:, c, 5:6])
        pg.dma("sp", xo[c * 128:(c + 1) * 128, :], A[:, :], r=[Ai], final=True)
    return pg


def run_k2oa(xbcT, conv_w, conv_b):
    in_maps = []
    for k in range(NCORES):
        ch = slice(768 * k, 768 * (k + 1))
        f = np.concatenate([conv_w[:, ch], conv_b[None, ch]], axis=0)
        cwp = f.reshape(6, 6, 128).transpose(2, 1, 0).reshape(128, 36)
        in_maps.append({"xin": np.ascontiguousarray(xbcT[ch]), "cw": np.ascontiguousarray(cwp)})
    res = _run(build_k2oa(), in_maps)
    return np.concatenate([res[k]["xo"] for k in range(NCORES)], axis=0)


def flipseg(a, axis):
    a = np.moveaxis(a, axis, 0)
    out = np.concatenate([a[:CTX][::-1], a[CTX:][::-1]], axis=0)
    return np.moveaxis(out, 0, axis)


def run_k2ob(xbcaT, dtrT, dt_bias, a_log):
    in_maps = []
    msk = ssd_masks()
    for g in range(NCORES):
        xs = xbcaT[512 * g:512 * (g + 1)]
        Bs = xbcaT[4096 + 128 * g:4096 + 128 * (g + 1)]
        Cs = xbcaT[5120 + 128 * g:5120 + 128 * (g + 1)]
        xa, dtr, Bm, BTt, CTt = [], [], [], [], []
        for d in range(2):
            f = (lambda a: flipseg(a, 1)) if d == 1 else (lambda a: a)
            xa.append(f(xs).T)
            dtr.append(f(dtrT[64 * d + 8 * g:64 * d + 8 * g + 8]).T)
            Bm.append(f(Bs).T)
            BTt.append(f(Bs))
            CTt.append(f(Cs))
        hp = np.stack([dt_bias[:, 8 * g:8 * g + 8], a_log[:, 8 * g:8 * g + 8]], axis=1).reshape(1, 32).repeat(128, 0)
        c = np.ascontiguousarray
        in_maps.append({"xa": c(np.stack(xa)), "dtr": c(np.stack(dtr)), "Bm": c(np.stack(Bm)), "BT": c(np.stack(BTt)),
                        "CT": c(np.stack(CTt)), "hp": c(hp.astype(np.float32)), "msk": msk})
    res = _run(build_k2ob(), in_maps)
    yf = np.concatenate([res[g]["y"][0].T for g in range(NCORES)], axis=0)
    yr = np.concatenate([flipseg(res[g]["y"][1], 0).T for g in range(NCORES)], axis=0)
    return yf, yr


S5T = 256
S5N = TOK // S5T
PI = math.pi


def build_k2e(lam_init):
    pg = Prog()
    qk = pg.din("qk", [2, 2, 64, TOK])
    qkp = pg.din("qkp", [2, 2, 64, TOK])
    cs = pg.din("cs", [2, 64, TOK])
    v = pg.din("v", [TOK, 128])
    lamb = pg.din("lamb", [128, 256])
    sg = pg.din("sg", [128, 1])
    su = pg.din("su", [2, 128, TOK])
    Bblk = pg.din("Bblk", [128, 2 * 4 * 2 * 128])
    Cblk = pg.din("Cblk", [128, 2 * 4 * 2 * 128])
    s5p = pg.din("s5p", [128, 2 * 4 * 3])
    dsk = pg.din("dsk", [128, 1])
    oT = pg.dout("oT", [128, TOK])
    sf = pg.dout("sf", [128, TOK])
    sr = pg.dout("sr", [128, TOK])
    ones = setup_consts(pg)

    lt = pg.sb("lt", [128, 4, 64])
    sgt = pg.sb("sgt", [128, 1])
    dskt = pg.sb("dskt", [128, 1])
    lam2 = pg.sb("lam2", [128, 2, 64])
    lame = pg.sb("lame", [128, 2])
    nlam = pg.sb("nlam", [128, 1])
    pg.dma("sp", lt[:, :, :], lamb.rearrange("p (a b) -> p a b", a=4), w=["lt"])
    pg.dma("sp", sgt[:, :], sg[:, :], w=["sgt"])
    pg.dma("sp", dskt[:, :], dsk[:, :], w=["dskt"])
    pg.tt("dve", lam2[:, 0, :], lt[:, 0, :], lt[:, 1, :], ALU.mult, r=["lt"], w=["lam2"])
    pg.tt("dve", lam2[:, 1, :], lt[:, 2, :], lt[:, 3, :], ALU.mult, r=["lt"], w=["lam2"])
    pg.add("dve", lambda e: e.tensor_reduce(out=lame[:, :], in_=lam2[:, :, :], axis=AX.X, op=ALU.add), r=["lam2"], w=["lame"])
    pg.act(lame[:, :], lame[:, :], AF.Exp, r=["lame"], w=["lame"])
    pg.tt("dve", nlam[:, :], lame[:, 1:2], lame[:, 0:1], ALU.subtract, r=["lame"], w=["nlam"])
    pg.ts("dve", nlam[:, :], nlam[:, :], -lam_init, ALU.add, r=["nlam"], w=["nlam"])
    pg.ts("dve", sgt[:, :], sgt[:, :], 1.0 - lam_init, ALU.mult, r=["sgt"], w=["sgt"])

    Bb = pg.sb("Bb", [128, 16, 128], BF16)
    Cb = pg.sb("Cb", [128, 16, 128], BF16)
    pg.dma("pool", Bb[:, :, :], Bblk.rearrange("p (a n) -> p a n", n=128), w=["Bb"])
    pg.dma("pool", Cb[:, :, :], Cblk.rearrange("p (a n) -> p a n", n=128), w=["Cb"])
    subt = [(pg.sb("sub%d" % i, [128, S5T], BF16), "sub%d" % i) for i in range(3)]
    pt = pg.sb("pt", [128, 2, 4, 3])
    pg.dma("sp", pt[:, :, :, :], s5p.rearrange("p (d g f) -> p d g f", d=2, g=4), w=["pt"])
    W8 = [128, 2, 4]
    names = ["dt", "th", "rho", "m", "sn", "cn", "thc", "nre", "nim", "den", "kre", "kim", "t1", "t2"]
    sm = {n: pg.sb("s5_" + n, W8) for n in names}
    a3 = lambda n: sm[n][:, :, :]
    lre, lim, ldt = pt[:, :, :, 0], pt[:, :, :, 1], pt[:, :, :, 2]
    pg.act(a3("dt"), ldt, AF.Exp, r=["pt"], w=["s_dt"])
    pg.tt("dve", a3("th"), lim, a3("dt"), ALU.mult, r=["pt", "s_dt"], w=["s_th"])
    pg.tt("dve", a3("rho"), lre, a3("dt"), ALU.mult, r=["pt", "s_dt"], w=["s_rho"])
    pg.act(a3("rho"), a3("rho"), AF.Exp, r=["s_rho"], w=["s_rho"])
    for _ in range(4):
        pg.ts("dve", a3("m"), a3("th"), PI, ALU.is_gt, r=["s_th"], w=["s_m"])
        pg.stt(a3("th"), a3("m"), -2.0 * PI, a3("th"), ALU.mult, ALU.add, r=["s_m", "s_th"], w=["s_th"])
    pg.act(a3("sn"), a3("th"), AF.Sin, r=["s_th"], w=["s_sn"])
    pg.ts("dve", a3("thc"), a3("th"), PI / 2, ALU.add, r=["s_th"], w=["s_thc"])
    pg.ts("dve", a3("m"), a3("thc"), PI, ALU.is_gt, r=["s_thc"], w=["s_m"])
    pg.stt(a3("thc"), a3("m"), -2.0 * PI, a3("thc"), ALU.mult, ALU.add, r=["s_m", "s_thc"], w=["s_thc"])
    pg.act(a3("cn"), a3("thc"), AF.Sin, r=["s_thc"], w=["s_cn"])
    pg.tt("dve", a3("nre"), a3("rho"), a3("cn"), ALU.mult, r=["s_rho", "s_cn"], w=["s_nre"])
    pg.ts("dve", a3("nre"), a3("nre"), -1.0, ALU.add, r=["s_nre"], w=["s_nre"])
    pg.tt("dve", a3("nim"), a3("rho"), a3("sn"), ALU.mult, r=["s_rho", "s_sn"], w=["s_nim"])
    pg.tt("dve", a3("den"), lre, lre, ALU.mult, r=["pt"], w=["s_den"])
    pg.tt("dve", a3("t1"), lim, lim, ALU.mult, r=["pt"], w=["s_t1"])
    pg.tt("dve", a3("den"), a3("den"), a3("t1"), ALU.add, r=["s_den", "s_t1"], w=["s_den"])
    pg.recip(a3("den"), a3("den"), r=["s_den"], w=["s_den"])
    pg.tt("dve", a3("t1"), a3("nre"), lre, ALU.mult, r=["s_nre", "pt"], w=["s_t1"])
    pg.tt("dve", a3("t2"), a3("nim"), lim, ALU.mult, r=["s_nim", "pt"], w=["s_t2"])
    pg.tt("dve", a3("kre"), a3("t1"), a3("t2"), ALU.add, r=["s_t1", "s_t2"], w=["s_kre"])
    pg.tt("dve", a3("kre"), a3("kre"), a3("den"), ALU.mult, r=["s_kre", "s_den"], w=["s_kre"])
    pg.tt("dve", a3("t1"), a3("nim"), lre, ALU.mult, r=["s_nim", "pt"], w=["s_t1"])
    pg.tt("dve", a3("t2"), a3("nre"), lim, ALU.mult, r=["s_nre", "pt"], w=["s_t2"])
    pg.tt("dve", a3("kim"), a3("t1"), a3("t2"), ALU.subtract, r=["s_t1", "s_t2"], w=["s_kim"])
    pg.tt("dve", a3("kim"), a3("kim"), a3("den"), ALU.mult, r=["s_kim", "s_den"], w=["s_kim"])
    TS = [128, 2, 4, S5T]
    Ec, Es, Fre, Fim, Rho, Tmp, Tmp2 = [pg.sb("tab%d" % i, TS) for i in range(7)]
    pg.copy("dve", Ec[:, :, :, 0], a3("cn"), r=["s_cn"], w=["Ec"])
    pg.copy("dve", Es[:, :, :, 0], a3("sn"), r=["s_sn"], w=["Es"])
    mlen = 1
    while mlen < S5T:
        bc = lambda T_: T_[:, :, :, mlen - 1:mlen].to_broadcast([128, 2, 4, mlen])
        lo = lambda T_: T_[:, :, :, 0:mlen]
        hi = lambda T_: T_[:, :, :, mlen:2 * mlen]
        pg.tt("dve", lo(Tmp), lo(Ec), bc(Ec), ALU.mult, r=["Ec"], w=["Tmp"])
        pg.tt("dve", lo(Tmp2), lo(Es), bc(Es), ALU.mult, r=["Es"], w=["Tmp2"])
        pg.tt("dve", hi(Tmp), lo(Ec), bc(Es), ALU.mult, r=["Ec", "Es"], w=["Tmp"])
        pg.tt("dve", hi(Tmp2), lo(Es), bc(Ec), ALU.mult, r=["Ec", "Es"], w=["Tmp2"])
        pg.tt("dve", hi(Ec), lo(Tmp), lo(Tmp2), ALU.subtract, r=["Tmp", "Tmp2"], w=["Ec"])
        pg.tt("dve", hi(Es), hi(Tmp), hi(Tmp2), ALU.add, r=["Tmp", "Tmp2"], w=["Es"])
        mlen *= 2
    bk = lambda n: sm[n][:, :, :].unsqueeze(3).to_broadcast(TS)
    A4 = lambda T_: T_[:, :, :, :]
    pg.tt("dve", A4(Tmp), A4(Ec), bk("kre"), ALU.mult, r=["Ec", "s_kre"], w=["Tmp"])
    pg.tt("dve", A4(Tmp2), A4(Es), bk("kim"), ALU.mult, r=["Es", "s_kim"], w=["Tmp2"])
    pg.tt("dve", A4(Fre), A4(Tmp), A4(Tmp2), ALU.add, r=["Tmp", "Tmp2"], w=["Fre"])
    pg.tt("dve", A4(Tmp), A4(Es), bk("kre"), ALU.mult, r=["Es", "s_kre"], w=["Tmp"])
    pg.tt("dve", A4(Tmp2), A4(Ec), bk("kim"), ALU.mult, r=["Ec", "s_kim"], w=["Tmp2"])
    pg.tt("dve", A4(Fim), A4(Tmp), A4(Tmp2), ALU.subtract, r=["Tmp", "Tmp2"], w=["Fim"])
    pg.copy("dve", A4(Rho), bk("rho"), r=["s_rho"], w=["Rho"])

    carry = pg.sb("carry", [128, 2, 4, 2])
    pg.memset("dve", carry[:, :, :, :], 0.0, w=["carry%d%d" % (d, g) for d in range(2) for g in range(4)])
    NB = 3
    s5t = [[(pg.sb("s5w%d_%d" % (j, i), [128, S5T]), "s5w%d_%d" % (j, i)) for j in range(10)] for i in range(NB)]
    s5c = [[(pg.sb("s5c%d_%d" % (j, i), [128, S5T], BF16), "s5c%d_%d" % (j, i)) for j in range(2)] for i in range(NB)]
    s5u = [(pg.sb("s5u%d" % i, [128, S5T]), "s5u%d" % i) for i in range(2)]
    s5o = [(pg.sb("s5o%d" % i, [128, S5T]), "s5o%d" % i) for i in range(2)]
    p_raw = [(pg.ps("praw%d" % i, [128, 512]), "praw%d" % i) for i in range(2)]
    p_y = [(pg.ps("pys5%d" % i, [128, 512]), "pys5%d" % i) for i in range(2)]
    NU = S5N * 8

    def unit(n):
        ci, r = divmod(n, 8)
        d, g = divmod(r, 4)
        return d, ci, g, ci * 2 + d

    def s5_prefetch(grp):
        if grp >= S5N * 2:
            return
        ci, d = divmod(grp, 2)
        sb_, sbid = subt[grp % 3]
        pg.dma("pool", sb_[:, :], su[d, :, ci * S5T:(ci + 1) * S5T], w=[sbid])

    def stA(n):
        d, ci, g, grp = unit(n)
        if g == 0:
            s5_prefetch(grp + 1)
        sb_, sbid = subt[grp % 3]
        praw, prid = p_raw[n % 2]
        for comp in range(2):
            pg.mm(praw[:, comp * S5T:(comp + 1) * S5T], Bb[:, (d * 4 + g) * 2 + comp, :], sb_[:, :], True, True,
                  r=["Bb", sbid], w=[prid])

    def stB(n):
        d, ci, g, grp = unit(n)
        praw, prid = p_raw[n % 2]
        W = s5t[n % NB]
        cid = "carry%d%d" % (d, g)
        rre, rim = praw[:, 0:S5T], praw[:, S5T:2 * S5T]
        fre, fim = Fre[:, d, g, :], Fim[:, d, g, :]
        (t1, i1), (t2, i2), (bre, ib), (bim, ibm), (gre, igr), (gim, igi) = W[0:6]
        pg.tt("dve", t1[:, :], rre, fre, ALU.mult, r=[prid, "Fre"], w=[i1])
        pg.tt("dve", t2[:, :], rim, fim, ALU.mult, r=[prid, "Fim"], w=[i2])
        pg.tt("dve", bre[:, :], t1[:, :], t2[:, :], ALU.add, r=[i1, i2], w=[ib])
        pg.tt("dve", t1[:, :], rre, fim, ALU.mult, r=[prid, "Fim"], w=[i1])
        pg.tt("dve", t2[:, :], rim, fre, ALU.mult, r=[prid, "Fre"], w=[i2])
        pg.tt("dve", bim[:, :], t1[:, :], t2[:, :], ALU.subtract, r=[i1, i2], w=[ibm])
        init_re = 0.0 if ci == 0 else carry[:, d, g, 0:1]
        init_im = 0.0 if ci == 0 else carry[:, d, g, 1:2]
        pg.add("dve", lambda e, o=gre[:, :], a=Rho[:, d, g, :], b=bre[:, :], ini=init_re:
               e.tensor_tensor_scan(out=o, data0=a, data1=b, initial=ini, op0=ALU.mult, op1=ALU.add), r=["Rho", ib, cid], w=[igr])
        pg.add("dve", lambda e, o=gim[:, :], a=Rho[:, d, g, :], b=bim[:, :], ini=init_im:
               e.tensor_tensor_scan(out=o, data0=a, data1=b, initial=ini, op0=ALU.mult, op1=ALU.add), r=["Rho", ibm, cid], w=[igi])

    def stC(n):
        d, ci, g, grp = unit(n)
        W = s5t[n % NB]
        cid = "carry%d%d" % (d, g)
        ec, es = Ec[:, d, g, :], Es[:, d, g, :]
        (gre, igr), (gim, igi), (u1, j1), (u2, j2), (cre, icr), (cim, ici) = W[4:10]
        pg.tt("pool", u1[:, :], gre[:, :], ec, ALU.mult, r=[igr, "Ec"], w=[j1])
        pg.tt("pool", u2[:, :], gim[:, :], es, ALU.mult, r=[igi, "Es"], w=[j2])
        pg.tt("pool", cre[:, :], u1[:, :], u2[:, :], ALU.add, r=[j1, j2], w=[icr])
        pg.tt("pool", u1[:, :], gim[:, :], ec, ALU.mult, r=[igi, "Ec"], w=[j1])
        pg.tt("pool", u2[:, :], gre[:, :], es, ALU.mult, r=[igr, "Es"], w=[j2])
        pg.tt("pool", cim[:, :], u1[:, :], u2[:, :], ALU.subtract, r=[j1, j2], w=[ici])
        pg.copy("pool", carry[:, d, g, 0:1], cre[:, S5T - 1:S5T], r=[icr], w=[cid])
        pg.copy("pool", carry[:, d, g, 1:2], cim[:, S5T - 1:S5T], r=[ici], w=[cid])

    def stD(n):
        W = s5t[n % NB]
        (cre, icr), (cim, ici) = W[8:10]
        (cbr, icbr), (cbi, icbi) = s5c[n % NB]
        pg.copy("act", cbr[:, :], cre[:, :], r=[icr], w=[icbr])
        pg.copy("act", cbi[:, :], cim[:, :], r=[ici], w=[icbi])

    def stE(n):
        d, ci, g, grp = unit(n)
        (cbr, icbr), (cbi, icbi) = s5c[n % NB]
        py, pyid = p_y[grp % 2]
        pg.mm(py[:, 0:S5T], Cb[:, (d * 4 + g) * 2 + 0, :], cbr[:, :], g == 0, False, r=["Cb", icbr], w=[pyid])
        pg.mm(py[:, 0:S5T], Cb[:, (d * 4 + g) * 2 + 1, :], cbi[:, :], False, g == 3, r=["Cb", icbi], w=[pyid])

    def stF(n):
        d, ci, g, grp = unit(n)
        if g != 3:
            return
        t0 = ci * S5T
        py, pyid = p_y[grp % 2]
        o_, oid = s5o[d]
        if d == 0:
            u_, uid = s5u[ci % 2]
            pg.dma("sp", u_[:, :], su[0, :, t0:t0 + S5T], w=[uid])
            pg.stt(o_[:, :], u_[:, :], dskt[:, 0:1], py[:, 0:S5T], ALU.mult, ALU.add, r=[uid, "dskt", pyid], w=[oid])
            pg.dma("sp", sf[:, t0:t0 + S5T], o_[:, :], r=[oid], final=True)
        else:
            pg.copy("dve", o_[:, :], py[:, 0:S5T], r=[pyid], w=[oid])
            pg.dma("sp", sr[:, t0:t0 + S5T], o_[:, :], r=[oid], final=True)

    stages = [stA, stB, stC, stD, stE, stF]
    tick = {"k": 0}

    def s5_tick():
        k = tick["k"]
        tick["k"] += 1
        for si, fn in enumerate(stages):
            n = k - si
            if 0 <= n < NU:
                fn(n)

    s5_prefetch(0)

    KT = pg.sb("KT", [64, 2, TOK], BF16)
    V = pg.sb("V", [128, NCH, 128], BF16)
    vv = v.rearrange("(c p) e -> p c e", p=128)
    for q in range(3):
        pg.dma("pool", V[:, 22 * q:22 * q + 22, :], vv[:, 22 * q:22 * q + 22, :], w=["V"])
    RW = 528
    rt_ = [[(pg.sb("rp%d_%d" % (j, i), [64, RW]), "rp%d_%d" % (j, i)) for j in range(4)] for i in range(2)]
    rc_ = [[(pg.sb("rc%d_%d" % (j, i), [64, RW]), "rc%d_%d" % (j, i)) for j in range(2)] for i in range(2)]
    rn = {"n": 0, "c": 0}

    def rope(which, t0, n, dst_fn, dst_id):
        ci = rn["c"] % 2
        rn["c"] += 1
        (cc, cci), (ss, ssi) = rc_[ci]
        pg.dma("sp", cc[:, 0:n], cs[0, :, t0:t0 + n], w=[cci])
        pg.dma("sp", ss[:, 0:n], cs[1, :, t0:t0 + n], w=[ssi])
        for m in range(2):
            i = rn["n"] % 2
            rn["n"] += 1
            (x, xi), (xp, xpi), (a, ai), (b, bi) = rt_[i]
            pg.dma("sp", x[:, 0:n], qk[which, m, :, t0:t0 + n], w=[xi])
            pg.dma("sp", xp[:, 0:n], qkp[which, m, :, t0:t0 + n], w=[xpi])
            pg.tt("dve", a[:, 0:n], x[:, 0:n], cc[:, 0:n], ALU.mult, r=[xi, cci], w=[ai])
            pg.tt("dve", b[:, 0:n], xp[:, 0:n], ss[:, 0:n], ALU.mult, r=[xpi, ssi], w=[bi])
            pg.tt("dve", dst_fn(m), a[:, 0:n], b[:, 0:n], ALU.add, r=[ai, bi], w=[dst_id])

    for t in range(TOK // RW):
        rope(1, t * RW, RW, lambda m, t=t: KT[:, m, t * RW:(t + 1) * RW], "KT")

    QT = [(pg.sb("QT%d" % i, [64, 2, 512], BF16), "QT%d" % i) for i in range(2)]
    Pt = [(pg.sb("P%d" % i, [128, 512], BF16), "P%d" % i) for i in range(3)]
    p_s = [(pg.ps("pS%d" % i, [128, 512]), "pS%d" % i) for i in range(2)]
    pO, pOid = (pg.ps("pO", [128, 512]), "pO")
    pZ, pZid = (pg.ps("pZ", [128, 512]), "pZ")
    rz = (pg.sb("rz", [128, 512]), "rz")
    ot = [(pg.sb("ot%d" % i, [128, 512]), "ot%d" % i) for i in range(2)]
    o2 = (pg.sb("o2", [128, 512]), "o2")
    osq = (pg.sb("osq", [128, 512], BF16), "osq")
    ors = (pg.sb("ors", [128, 512]), "ors")
    qtiles = [(0, CTX, 0, 2)] + [(CTX + 512 * i, 512, 0, NCH) for i in range(16)]

    def qrope(qi):
        q0, nq, _, _ = qtiles[qi]
        Q, Qid = QT[qi % 2]
        rope(0, q0, nq, lambda m: Q[:, m, 0:nq], Qid)

    qrope(0)
    cnt = {"s": 0, "p": 0, "it": 0}
    for qi, (q0, nq, k0, k1) in enumerate(qtiles):
        Q, Qid = QT[qi % 2]
        o_, oid = ot[qi % 2]
        its = [(m, kc) for m in range(2) for kc in range(k0, k1)]

        def emitS(m, kc, Q=Q, Qid=Qid, nq=nq):
            pS, pSid = p_s[cnt["s"] % 2]
            cnt["s"] += 1
            P, Pid = Pt[cnt["p"] % 3]
            cnt["p"] += 1
            pg.mm(pS[:, 0:nq], KT[:, m, kc * 128:(kc + 1) * 128], Q[:, m, 0:nq], True, True, r=["KT", Qid], w=[pSid])
            pg.act(P[:, 0:nq], pS[:, 0:nq], AF.Exp, r=[pSid], w=[Pid], scale=0.125)
            return P, Pid

        pend = emitS(*its[0])
        for idx, (m, kc) in enumerate(its):
            P, Pid = pend
            if idx + 1 < len(its):
                pend = emitS(*its[idx + 1])
            pg.mm(pO[:, 0:nq], V[:, kc, :], P[:, 0:nq], kc == k0, kc == k1 - 1, r=["V", Pid], w=[pOid])
            pg.mm(pZ[:, 0:nq], ones[:, :], P[:, 0:nq], kc == k0, kc == k1 - 1, r=["ones", Pid], w=[pZid])
            cnt["it"] += 1
            if cnt["it"] % 8 == 0:
                s5_tick()
            if m == 0 and kc == k0 + 8 and qi + 1 < len(qtiles):
                qrope(qi + 1)
            if kc == k1 - 1:
                pg.recip(rz[0][:, 0:nq], pZ[:, 0:nq], r=[pZid], w=[rz[1]])
                if m == 0:
                    pg.tt("dve", o_[:, 0:nq], pO[:, 0:nq], rz[0][:, 0:nq], ALU.mult, r=[pOid, rz[1]], w=[oid])
                else:
                    pg.tt("dve", o2[0][:, 0:nq], pO[:, 0:nq], rz[0][:, 0:nq], ALU.mult, r=[pOid, rz[1]], w=[o2[1]])
                    pg.stt(o_[:, 0:nq], o2[0][:, 0:nq], nlam[:, 0:1], o_[:, 0:nq], ALU.mult, ALU.add, r=[o2[1], oid, "nlam"], w=[oid])
        if qi == 0 and len(qtiles) > 1:
            qrope(1)
        pg.act(osq[0][:, 0:nq], o_[:, 0:nq], AF.Square, r=[oid], w=[osq[1]])
        pS, pSid = p_s[cnt["s"] % 2]
        cnt["s"] += 1
        pg.mm(pS[:, 0:nq], ones[:, :], osq[0][:, 0:nq], True, True, r=["ones", osq[1]], w=[pSid])
        pg.act(ors[0][:, 0:nq], pS[:, 0:nq], AF.Sqrt, r=[pSid], w=[ors[1]], bias=pg.eps_ap, scale=1.0 / 128)
        pg.recip(ors[0][:, 0:nq], ors[0][:, 0:nq], r=[ors[1]], w=[ors[1]])
        pg.tt("dve", o_[:, 0:nq], o_[:, 0:nq], ors[0][:, 0:nq], ALU.mult, r=[oid, ors[1]], w=[oid])
        pg.ts("dve", o_[:, 0:nq], o_[:, 0:nq], sgt[:, 0:1], ALU.mult, r=[oid, "sgt"], w=[oid])
        pg.dma("sp", oT[:, q0:q0 + nq], o_[:, 0:nq], r=[oid], final=True)
    while tick["k"] < NU + len(stages):
        s5_tick()
    return pg


def rope_tables():
    rows = SEQ // 64
    row = np.repeat(np.arange(rows, dtype=np.float32), 64)
    col = np.tile(np.arange(64, dtype=np.float32), rows)
    inv = (10000.0 ** (-np.arange(0, 32, 2, dtype=np.float32) / 32)).astype(np.float32)
    ang_r = row[:, None] * inv
    ang_c = col[:, None] * inv
    ang = np.concatenate([ang_r, ang_r, ang_c, ang_c], axis=-1)
    cos = np.cos(ang).astype(np.float32)
    sin = np.sin(ang).astype(np.float32)
    sgn = np.concatenate([-np.ones(16), np.ones(16), -np.ones(16), np.ones(16)]).astype(np.float32)
    cosT = np.concatenate([np.ones((64, CTX), np.float32), cos.T], axis=1)
    sinT = np.concatenate([np.zeros((64, CTX), np.float32), (sin * sgn[None, :]).T], axis=1)
    return np.ascontiguousarray(np.stack([cosT, sinT]))


ROPE_PERM = np.concatenate([np.arange(16, 32), np.arange(0, 16), np.arange(48, 64), np.arange(32, 48)])


def k2e_inputs(pT, j, P, cores=range(NCORES)):
    cs = rope_tables()
    c = np.ascontiguousarray
    maps = []
    for k in cores:
        q = np.stack([pT[m * 512 + k * 64:m * 512 + k * 64 + 64] for m in range(2)])
        kk = np.stack([pT[1024 + m * 512 + k * 64:1024 + m * 512 + k * 64 + 64] for m in range(2)])
        qk = np.stack([q, kk])
        qkp = qk[:, :, ROPE_PERM, :]
        v = pT[2048 + 128 * k:2048 + 128 * (k + 1)].T
        s = pT[3072 + 128 * k:3072 + 128 * (k + 1)]
        su = np.stack([s, flipseg(s, 1)])
        Bblk = np.zeros((128, 2, 4, 2, 128), np.float32)
        Cblk = np.zeros((128, 2, 4, 2, 128), np.float32)
        s5p = np.zeros((128, 2, 4, 3), np.float32)
        for d in range(2):
            for gp in range(4):
                for gi in range(2):
                    gl = 2 * gp + gi
                    g = 8 * k + gl
                    for comp, (bn, cn) in enumerate([("s5_b_re", "s5_c_re"), ("s5_b_im", "s5_c_im")]):
                        Bblk[gl * 16:(gl + 1) * 16, d, gp, comp, gi * 64:(gi + 1) * 64] = P[bn][j, d, g].T
                        Cblk[gi * 64:(gi + 1) * 64, d, gp, comp, gl * 16:(gl + 1) * 16] = P[cn][j, d, g].T
                    s5p[gi * 64:(gi + 1) * 64, d, gp, 0] = P["s5_lam_re"][j, d, g]
                    s5p[gi * 64:(gi + 1) * 64, d, gp, 1] = P["s5_lam_im"][j, d, g]
                    s5p[gi * 64:(gi + 1) * 64, d, gp, 2] = P["s5_log_dt"][j, d, g]
        dsk = P["s5_d"][j, 8 * k:8 * k + 8].reshape(128, 1)
        maps.append({"qk": c(qk), "qkp": c(qkp), "cs": cs, "v": c(v),
                     "lamb": c(P["diff_lam"][j].reshape(1, 256).repeat(128, 0)), "sg": c(P["diff_subln"][j].reshape(128, 1)),
                     "su": c(su), "Bblk": c(Bblk.reshape(128, -1)), "Cblk": c(Cblk.reshape(128, -1)),
                     "s5p": c(s5p.reshape(128, -1)), "dsk": c(dsk.astype(np.float32))})
    return maps


def run_k2e(pT, j, lam_init, P):
    res = _run(build_k2e(lam_init), k2e_inputs(pT, j, P))
    oT = np.concatenate([res[k]["oT"] for k in range(NCORES)], axis=0)
    sfT = np.concatenate([res[k]["sf"] for k in range(NCORES)], axis=0)
    srT = np.concatenate([flipseg(res[k]["sr"], 1) for k in range(NCORES)], axis=0)
    return oT, sfT, srT


def kernel(**inp):
    P = {k: np.asarray(v, dtype=np.float32) for k, v in inp.items()}
    mod = run_k0(P["c"][0], P["c_ctx"], P["w_mod"], P["b_mod"])
    hT = np.ascontiguousarray(np.concatenate([P["ctx"][0], P["x"][0]], axis=0).T)
    for i in range(4):
        j = i // 2
        g = P["norm_g"][i]
        ml, mc = mod[i, 0], mod[i, 1]
        if i % 2 == 0:
            lam_init = 0.8 - 0.6 * math.exp(-0.3 * i)
            pT = run_k1(hT, g[0], ml, mc, P["w_in_even"][j])
            oT, sfT, srT = run_k2e(pT, j, lam_init, P)
            bg = np.ascontiguousarray(P["s5_b_glu"][j].reshape(8, 128).T)
            h1T, u2T = run_k3a(True, hT, {"oT": oT, "sfT": sfT, "srT": srT}, g, ml, mc, P["w_out_even"][j],
                               {"wglu": P["s5_w_glu"][j], "bglu": bg})
        else:
            pT = run_k1(hT, g[0], ml, mc, P["w_in_odd"][j])
            zT = pT[0:4096]
            xbcaT = run_k2oa(pT[4096:4096 + 6144], P["conv_w"][j], P["conv_b"][j])
            yfT, yrT = run_k2ob(xbcaT, pT[10240:10368], P["ssd_dt_bias"][j], P["ssd_a_log"][j])
            dexp = np.repeat(P["ssd_d"][j], 64)
            nw = np.concatenate([P["ssd_norm_w"][j].reshape(32, 128).T, dexp.reshape(32, 128).T], axis=1)
            h1T, u2T = run_k3a(False, hT, {"yfT": yfT, "yrT": yrT, "xT": xbcaT[0:4096], "zT": zT}, g, ml, mc,
                               P["w_out_odd"][j], {"nw": np.ascontiguousarray(nw.astype(np.float32))})
        hT = run_k3b(u2T, h1T, g[3], ml, mc, P["w_up"][i], P["w_down"][i])
    return np.ascontiguousarray(hT[:, CTX:].T)[None].astype(np.float32)
```

```python
import math
import numpy as np
import concourse.bass as bass
import concourse.mybir as mybir
from concourse.bass_utils import run_bass_kernel_spmd

F32 = mybir.dt.float32
BF16 = mybir.dt.bfloat16
AF = mybir.ActivationFunctionType
ALU = mybir.AluOpType
AX = mybir.AxisListType

NCORES = 8
D = 2048
SEQ = 8192
CTX = 256
TOK = SEQ + CTX
TPC = 1056
LAT = 1024
TT = 352
NTT = 3
EPS = 1e-6


class _Op:
    __slots__ = ("eng", "fn", "deps", "needs_inc", "count", "sem", "dma", "idx")

    def __init__(self, eng, fn, dma):
        self.eng = eng
        self.fn = fn
        self.deps = []
        self.needs_inc = False
        self.count = 0
        self.sem = None
        self.dma = dma


ENGS = ("pe", "act", "dve", "pool", "sp")
N_DMA_SEMS = 24


class Prog:
    def __init__(self):
        self.nc = bass.Bass("TRN2", target_bir_lowering=False)
        self.ops = {e: [] for e in ENGS}
        self.lastw = {}
        self.readers = {}
        self.sems = {e: self.nc.alloc_semaphore("s_" + e) for e in ENGS if e != "sp"}
        self.dsems = [self.nc.alloc_semaphore("d%d" % i) for i in range(N_DMA_SEMS)]
        self.dcount = [0] * N_DMA_SEMS
        self.dlast = [None] * N_DMA_SEMS
        self.dnext = 0
        self.out_dmas = []
        self.uid = 0

    def din(self, name, shape, dt=F32):
        return self.nc.dram_tensor(name, list(shape), dt, kind="ExternalInput").ap()

    def dout(self, name, shape, dt=F32):
        return self.nc.dram_tensor(name, list(shape), dt, kind="ExternalOutput").ap()

    def sb(self, name, shape, dt=F32):
        return self.nc.alloc_sbuf_tensor(name, list(shape), dt)

    def ps(self, name, shape, dt=F32):
        return self.nc.alloc_psum_tensor(name, list(shape), dt)

    def add(self, eng, fn, r=(), w=(), dma=False, out=False):
        op = _Op(eng, fn, dma)
        deps = []
        seen = set()

        def push(d):
            if d is None or id(d) in seen or d is op:
                return
            seen.add(id(d))
            deps.append(d)

        for b in r:
            push(self.lastw.get(b))
        for b in w:
            push(self.lastw.get(b))
            for rd in self.readers.get(b, {}).values():
                push(rd)
        if dma:
            k = self.dnext
            self.dnext = (k + 1) % N_DMA_SEMS
            push(self.dlast[k])
            self.dlast[k] = op
            self.dcount[k] += 16
            op.sem = self.dsems[k]
            op.count = self.dcount[k]
            op.needs_inc = True
        final = []
        for d in deps:
            if (not d.dma) and d.eng == "pe" and eng == "pe" and not dma:
                continue
            d.needs_inc = True
            final.append(d)
        op.deps = final
        for b in r:
            key = ("dma", id(op)) if dma else eng
            self.readers.setdefault(b, {})[key] = op
        for b in w:
            self.lastw[b] = op
            self.readers[b] = {}
        self.ops[eng].append(op)
        if out:
            self.out_dmas.append(op)
        return op

    def finish(self):
        fin = _Op("sp", None, False)
        fin.deps = list(self.out_dmas)
        self.ops["sp"].append(fin)
        for e in ENGS:
            c = 0
            for op in self.ops[e]:
                if op.dma or op.fn is None:
                    continue
                if op.needs_inc:
                    c += 1
                    op.count = c
                    op.sem = self.sems[e]
        nc = self.nc
        progs = self.ops

        def emit(e, ops):
            waited = {}
            for op in ops:
                for d in op.deps:
                    key = id(d.sem)
                    if waited.get(key, 0) < d.count:
                        e.wait_ge(d.sem, d.count)
                        waited[key] = d.count
                if op.fn is None:
                    continue
                ins = op.fn(e)
                if op.dma:
                    ins.then_inc(op.sem, 16)
                elif op.needs_inc:
                    ins.then_inc(op.sem, 1)

        with nc.Block() as block:
            @block.tensor
            def _(e):
                emit(e, progs["pe"])

            @block.scalar
            def _(e):
                emit(e, progs["act"])

            @block.vector
            def _(e):
                emit(e, progs["dve"])

            @block.gpsimd
            def _(e):
                emit(e, progs["pool"])

            @block.sync
            def _(e):
                emit(e, progs["sp"])
        return nc

    def dma(self, q, out, in_, r=(), w=(), final=False):
        return self.add(q, lambda e, o=out, i=in_: e.dma_start(out=o, in_=i), r=r, w=w, dma=True, out=final)

    def mm(self, out, lhsT, rhs, start, stop, r=(), w=()):
        return self.add("pe", lambda e, o=out, l=lhsT, rr=rhs, s=start, t=stop: e.matmul(o, l, rr, start=s, stop=t), r=r, w=w)

    def act(self, out, in_, func, r=(), w=(), bias=None, scale=None, eng="act"):
        def fn(e, o=out, i=in_, f=func, b=bias, sc=scale):
            kw = {}
            if b is not None:
                kw["bias"] = b
            if sc is not None:
                kw["scale"] = sc
            return e.activation(out=o, in_=i, func=f, **kw)
        return self.add("act", fn, r=r, w=w)

    def tt(self, eng, out, in0, in1, op, r=(), w=()):
        return self.add(eng, lambda e, o=out, a=in0, b=in1, p=op: e.tensor_tensor(out=o, in0=a, in1=b, op=p), r=r, w=w)

    def ts(self, eng, out, in0, s1, op0, s2=None, op1=None, r=(), w=()):
        def fn(e, o=out, a=in0, x1=s1, x2=s2, p0=op0, p1=op1):
            if p1 is None:
                return e.tensor_scalar(out=o, in0=a, scalar1=x1, scalar2=None, op0=p0)
            return e.tensor_scalar(out=o, in0=a, scalar1=x1, scalar2=x2, op0=p0, op1=p1)
        return self.add(eng, fn, r=r, w=w)

    def stt(self, out, in0, scalar, in1, op0, op1, r=(), w=()):
        return self.add("dve", lambda e, o=out, a=in0, s=scalar, b=in1, p0=op0, p1=op1:
                        e.scalar_tensor_tensor(out=o, in0=a, scalar=s, in1=b, op0=p0, op1=p1), r=r, w=w)

    def copy(self, eng, out, in_, r=(), w=()):
        if eng == "act":
            return self.add("act", lambda e, o=out, i=in_: e.copy(out=o, in_=i), r=r, w=w)
        return self.add(eng, lambda e, o=out, i=in_: e.tensor_copy(out=o, in_=i), r=r, w=w)

    def memset(self, eng, ap, val, w=()):
        return self.add(eng, lambda e, a=ap, v=val: e.memset(a, v), w=w)

    def recip(self, out, in_, r=(), w=()):
        return self.add("dve", lambda e, o=out, i=in_: e.reciprocal(out=o, in_=i), r=r, w=w)


_TRACE = {"on": False, "log": []}


def _run(pg, in_maps):
    nc = pg.finish()
    if _TRACE["on"]:
        res = run_bass_kernel_spmd(nc, in_maps, core_ids=list(range(NCORES)), trace=True)
        _TRACE["log"].append(res.exec_time_ns)
        print("EXEC_NS", res.exec_time_ns, {e: len(pg.ops[e]) for e in ENGS}, flush=True)
    else:
        res = run_bass_kernel_spmd(nc, in_maps, core_ids=list(range(NCORES)))
    return res.results


def build_k0():
    pg = Prog()
    NJ = 12
    cc = pg.din("cc", [128, 32])
    wm = pg.din("wm", [4, 2048, 1536])
    bm = pg.din("bm", [128, 4 * NJ])
    o = pg.dout("modT", [128, 4 * NJ * 2])
    cct = pg.sb("cct", [128, 32])
    sc = pg.sb("sc", [128, 32])
    bmt = pg.sb("bmt", [128, 4 * NJ])
    ot = pg.sb("ot", [128, 4 * NJ * 2])
    wt = [pg.sb("wt%d" % i, [128, 16, 768]) for i in range(2)]
    pst = [pg.ps("ps%d" % i, [128, 512]) for i in range(2)]
    pg.dma("sp", cct[:], cc[:, :], w=["cct"])
    pg.dma("sp", bmt[:], bm[:, :], w=["bmt"])
    pg.act(sc[:], cct[:], AF.Silu, r=["cct"], w=["sc"])
    n = 0
    for i in range(4):
        for hf in range(2):
            b = n % 2
            wv = wm[i, :, hf * 768:(hf + 1) * 768].rearrange("(dc p) n -> p dc n", p=128)
            for q4 in range(4):
                pg.dma("sp", wt[b][:, 4 * q4:4 * q4 + 4, :], wv[:, 4 * q4:4 * q4 + 4, :], w=["wt%d_%d" % (b, q4)])
            for jj in range(6):
                j = hf * 6 + jj
                pb = (i * NJ + j) % 2
                for dc in range(16):
                    pg.mm(pst[pb][:, 0:2], wt[b][:, dc, jj * 128:(jj + 1) * 128], sc[:, 2 * dc:2 * dc + 2],
                          dc == 0, dc == 15, r=["wt%d_%d" % (b, dc // 4), "sc"], w=["ps%d" % pb])
                col = (i * NJ + j)
                pg.ts("dve", ot[:, 2 * col:2 * col + 2], pst[pb][:, 0:2], bmt[:, col:col + 1], ALU.add,
                      r=["ps%d" % pb, "bmt"], w=["ot"])
            n += 1
    pg.dma("sp", o[:, :], ot[:], r=["ot"], final=True)
    return pg


def run_k0(c, c_ctx, w_mod, b_mod):
    cc = np.stack([c.reshape(16, 128), c_ctx.reshape(16, 128)], axis=-1)
    cc = np.ascontiguousarray(cc.transpose(1, 0, 2).reshape(128, 32))
    in_maps = []
    for k in range(NCORES):
        wm = np.ascontiguousarray(w_mod[:, :, 1536 * k:1536 * (k + 1)])
        bm = b_mod[:, 1536 * k:1536 * (k + 1)].reshape(4, 12, 128).transpose(2, 0, 1).reshape(128, 48)
        in_maps.append({"cc": cc, "wm": wm, "bm": np.ascontiguousarray(bm)})
    res = _run(build_k0(), in_maps)
    mod = np.zeros((4, 2, 12288), np.float32)
    for k in range(NCORES):
        r = res[k]["modT"].reshape(128, 4, 12, 2)
        mod[:, :, 1536 * k:1536 * (k + 1)] = r.transpose(1, 3, 2, 0).reshape(4, 2, 1536)
    return mod


class WStream:
    def __init__(self, pg, name, KC, gw=512, nbuf=3):
        self.pg, self.name, self.KC, self.gw, self.nbuf = pg, name, KC, gw, nbuf
        self.bufs = [pg.sb("%s_w%d" % (name, i), [128, KC, gw], BF16) for i in range(nbuf)]
        self.n = 0

    def load(self, wap, c0, cols):
        b = self.n % self.nbuf
        self.n += 1
        wv = wap[:, c0:c0 + cols].rearrange("(kc p) n -> p kc n", p=128)
        step = 4
        for q in range(0, self.KC, step):
            self.pg.dma("pool", self.bufs[b][:, q:q + step, 0:cols], wv[:, q:q + step, :],
                        w=["%s_w%d_%d" % (self.name, b, q // step)])
        return b

    def rid(self, b, kc):
        return "%s_w%d_%d" % (self.name, b, kc // 4)


def linear_fm(pg, ws, wap, N, X, xid, KC, pss, evac, col_ranges=None):
    ng = (N + ws.gw - 1) // ws.gw
    cnt = 0
    for g in range(ng):
        cols = min(ws.gw, N - g * ws.gw)
        b = ws.load(wap, g * ws.gw, cols)
        for ocl in range(cols // 128):
            oc = g * (ws.gw // 128) + ocl
            for tt in range(NTT):
                pb = cnt % len(pss)
                cnt += 1
                ps, psid = pss[pb]
                for kc in range(KC):
                    pg.mm(ps[:, 0:TT], ws.bufs[b][:, kc, ocl * 128:(ocl + 1) * 128], X[:, kc, tt * TT:(tt + 1) * TT],
                          kc == 0, kc == KC - 1, r=[ws.rid(b, kc), xid(kc)], w=[psid])
                evac(oc, tt, ps[:, 0:TT], psid)


def sumsq_rstd(pg, name, src_fn, src_id, KC, dim, ones, sq_bufs, ss_ps, rstd, eng_sq="act"):
    for c in range(KC):
        sb_, sid = sq_bufs[c % len(sq_bufs)]
        pg.act(sb_[:, :], src_fn(c), AF.Square, r=[src_id(c)], w=[sid])
        for tt in range(NTT):
            ps, psid = ss_ps[tt]
            pg.mm(ps[:, 0:TT], ones[:, :], sb_[:, tt * TT:(tt + 1) * TT], c == 0, c == KC - 1, r=[sid, "ones"], w=[psid])
    for tt in range(NTT):
        ps, psid = ss_ps[tt]
        pg.act(rstd[:, tt * TT:(tt + 1) * TT], ps[:, 0:TT], AF.Sqrt, r=[psid], w=[name + "_rs%d" % tt],
               bias=pg.eps_ap, scale=1.0 / dim)
        pg.recip(rstd[:, tt * TT:(tt + 1) * TT], rstd[:, tt * TT:(tt + 1) * TT], r=[name + "_rs%d" % tt], w=[name + "_rs%d" % tt])
    return [name + "_rs%d" % tt for tt in range(NTT)]


def setup_consts(pg):
    ones = pg.sb("ones", [128, 128], BF16)
    pg.memset("dve", ones[:, :], 1.0, w=["ones"])
    epst = pg.sb("epst", [128, 1])
    pg.memset("dve", epst[:, :], EPS, w=["eps"])
    pg.eps_ap = epst[:, 0:1]
    return ones


def build_k1(N):
    pg = Prog()
    hT = pg.din("hT", [D, TPC])
    mp = pg.din("mp", [128, 16 * 5])
    w = pg.din("w", [D, N])
    pT = pg.dout("pT", [N, TPC])
    ones = setup_consts(pg)
    H = pg.sb("H", [128, 16, TPC])
    U = pg.sb("U", [128, 16, TPC], BF16)
    mpt = pg.sb("mpt", [128, 16, 5])
    AB = pg.sb("AB", [128, 16, 2])
    rstd = pg.sb("rstd", [128, TPC])
    sqb = [(pg.sb("sq%d" % i, [128, TPC], BF16), "sq%d" % i) for i in range(2)]
    tmpb = [(pg.sb("tmp%d" % i, [128, TPC]), "tmp%d" % i) for i in range(2)]
    Ob = [(pg.sb("O%d" % i, [128, TPC]), "O%d" % i) for i in range(3)]
    ssp = [(pg.ps("ss%d" % i, [128, 512]), "ss%d" % i) for i in range(3)]
    mmp = [(pg.ps("mm%d" % i, [128, 512]), "mm%d" % i) for i in range(4)]
    ws = WStream(pg, "win", 16)

    hv = hT.rearrange("(c p) t -> p c t", p=128)
    for q in range(4):
        pg.dma("sp", H[:, 4 * q:4 * q + 4, :], hv[:, 4 * q:4 * q + 4, :], w=["H%d" % q])
    pg.dma("sp", mpt[:, :, :], mp.rearrange("p (c f) -> p c f", f=5), w=["mpt"])
    rs_ids = sumsq_rstd(pg, "n0", lambda c: H[:, c, :], lambda c: "H%d" % (c // 4), 16, D, ones, sqb, ssp, rstd)
    pg.stt(AB[:, :, 0], mpt[:, :, 2], 1.0, mpt[:, :, 0], ALU.add, ALU.mult, r=["mpt"], w=["AB"])
    pg.stt(AB[:, :, 1], mpt[:, :, 4], 1.0, mpt[:, :, 0], ALU.add, ALU.mult, r=["mpt"], w=["AB"])
    for c in range(16):
        tb, tid = tmpb[c % 2]
        pg.tt("dve", tb[:, :], H[:, c, :], rstd[:, :], ALU.mult, r=["H%d" % (c // 4)] + rs_ids, w=[tid])
        pg.act(U[:, c, 0:LAT], tb[:, 0:LAT], AF.Identity, r=[tid, "AB", "mpt"], w=["U%d" % c],
               bias=mpt[:, c, 1:2], scale=AB[:, c, 0:1])
        pg.act(U[:, c, LAT:TPC], tb[:, LAT:TPC], AF.Identity, r=[tid, "AB", "mpt"], w=["U%d" % c],
               bias=mpt[:, c, 3:4], scale=AB[:, c, 1:2])
    state = {"n": 0}

    def evac(oc, tt, ps, psid):
        ob, oid = Ob[oc % 3]
        eng = "act" if (state["n"] % 2 == 0) else "dve"
        state["n"] += 1
        pg.copy(eng, ob[:, tt * TT:(tt + 1) * TT], ps, r=[psid], w=[oid + "_%d" % tt])
        if tt == NTT - 1:
            pg.dma("sp", pT[oc * 128:(oc + 1) * 128, :], ob[:, :], r=[oid + "_%d" % t for t in range(NTT)], final=True)

    linear_fm(pg, ws, w, N, U, lambda kc: "U%d" % kc, 16, mmp, evac)
    return pg


def tok_cols(k):
    return np.concatenate([CTX + np.arange(LAT * k, LAT * (k + 1)), np.arange(32 * k, 32 * (k + 1))])


def shard_T(aT):
    return [np.ascontiguousarray(aT[:, tok_cols(k)]) for k in range(NCORES)]


def unshard_T(parts):
    F = parts[0].shape[0]
    out = np.empty((F, TOK), parts[0].dtype)
    for k in range(NCORES):
        out[:, tok_cols(k)] = parts[k]
    return out


def pack_cols(vecs):
    a = np.stack([v.reshape(16, 128) for v in vecs], axis=-1)
    return np.ascontiguousarray(a.transpose(1, 0, 2).reshape(128, -1)).astype(np.float32)


def run_k1(hT, g0, mod_l, mod_c, w_in):
    N = w_in.shape[1]
    m = lambda v, i: v[i * D:(i + 1) * D]
    mp = pack_cols([g0, m(mod_l, 0), m(mod_l, 1), m(mod_c, 0), m(mod_c, 1)])
    hs = shard_T(hT)
    in_maps = [{"hT": hs[k], "mp": mp, "w": w_in} for k in range(NCORES)]
    res = _run(build_k1(N), in_maps)
    return unshard_T([res[k]["pT"] for k in range(NCORES)])


def residual_tail(pg, name, Y, yid, rstd, rs_ids, coef, coef_id, h_dram, out_dram, scr, inplace=False):
    hb = scr[0:2]
    tb = scr[2:4]
    ob = scr[4:6]
    for c in range(16):
        h_, hid = hb[c % 2]
        t_, tid = tb[c % 2]
        if inplace:
            oo, oid = Y[:, c, :], yid(c)
        else:
            o_, oid = ob[c % 2]
            oo = o_[:, :]
        pg.dma("sp", h_[:, :], h_dram[c * 128:(c + 1) * 128, :], w=[hid])
        pg.tt("dve", t_[:, :], Y[:, c, :], rstd[:, :], ALU.mult, r=[yid(c)] + rs_ids, w=[tid])
        pg.stt(oo[:, 0:LAT], t_[:, 0:LAT], coef[:, c, 0:1], h_[:, 0:LAT], ALU.mult, ALU.add, r=[tid, hid, coef_id], w=[oid])
        pg.stt(oo[:, LAT:TPC], t_[:, LAT:TPC], coef[:, c, 1:2], h_[:, LAT:TPC], ALU.mult, ALU.add, r=[tid, hid, coef_id], w=[oid])
        pg.dma("sp", out_dram[c * 128:(c + 1) * 128, :], oo, r=[oid], final=True)


def build_k3b():
    pg = Prog()
    uT = pg.din("uT", [D, TPC])
    hT = pg.din("hT", [D, TPC])
    mp = pg.din("mp", [128, 16 * 3])
    wu = pg.din("wu", [D, 4 * D])
    wd = pg.din("wd", [4 * D, D])
    oT = pg.dout("oT", [D, TPC])
    ones = setup_consts(pg)
    U = pg.sb("U", [128, 16, TPC], BF16)
    HQ = pg.sb("HQ", [128, 8, TPC], BF16)
    Y = pg.sb("Y", [128, 16, TPC])
    mpt = pg.sb("mpt", [128, 16, 3])
    coef = pg.sb("coef", [128, 16, 2])
    rstd = pg.sb("rstd", [128, TPC])
    rt = [(pg.sb("rt%d" % i, [128, TT]), "rt%d" % i) for i in range(3)]
    sqb = [(pg.sb("sq%d" % i, [128, TPC], BF16), "sq%d" % i) for i in range(2)]
    ssp = [(pg.ps("ss%d" % i, [128, 512]), "ss%d" % i) for i in range(3)]
    mmp = [(pg.ps("mm%d" % i, [128, 512]), "mm%d" % i) for i in range(4)]
    wsu = WStream(pg, "wu", 16, gw=256)
    wsd = WStream(pg, "wd", 8, gw=512)
    uv = uT.rearrange("(c p) t -> p c t", p=128)
    for q in range(4):
        pg.dma("pool", U[:, 4 * q:4 * q + 4, :], uv[:, 4 * q:4 * q + 4, :], w=["U%d" % c for c in range(4 * q, 4 * q + 4)])
    pg.dma("sp", mpt[:, :, :], mp.rearrange("p (c f) -> p c f", f=3), w=["mpt"])
    pg.tt("dve", coef[:, :, 0], mpt[:, :, 0], mpt[:, :, 1], ALU.mult, r=["mpt"], w=["coef"])
    pg.tt("dve", coef[:, :, 1], mpt[:, :, 0], mpt[:, :, 2], ALU.mult, r=["mpt"], w=["coef"])
    st = {"n": 0}
    for e in range(8):
        def evac_up(oc, tt, ps, psid):
            r_, rid = rt[st["n"] % 3]
            st["n"] += 1
            pg.act(r_[:, :], ps, AF.Relu, r=[psid], w=[rid])
            pg.tt("dve", HQ[:, oc, tt * TT:(tt + 1) * TT], r_[:, :], r_[:, :], ALU.mult, r=[rid], w=["HQ%d" % oc])

        linear_fm(pg, wsu, wu[:, e * 1024:(e + 1) * 1024], 1024, U, lambda kc: "U%d" % kc, 16, mmp, evac_up)

        def evac_dn(oc, tt, ps, psid, e=e):
            dst = Y[:, oc, tt * TT:(tt + 1) * TT]
            if e == 0:
                pg.copy("act", dst, ps, r=[psid], w=["Y%d" % oc])
            else:
                pg.tt("dve", dst, ps, dst, ALU.add, r=[psid, "Y%d" % oc], w=["Y%d" % oc])

        linear_fm(pg, wsd, wd[e * 1024:(e + 1) * 1024, :], D, HQ, lambda kc: "HQ%d" % kc, 8, mmp, evac_dn)
    rs_ids = sumsq_rstd(pg, "n3", lambda c: Y[:, c, :], lambda c: "Y%d" % c, 16, D, ones, sqb, ssp, rstd)
    scr = [(pg.sb("scr%d" % i, [128, TPC]), "scr%d" % i) for i in range(6)]
    residual_tail(pg, "rt", Y, lambda c: "Y%d" % c, rstd, rs_ids, coef, "coef", hT, oT, scr)
    return pg


def run_k3b(u2T, h1T, g3, mod_l, mod_c, w_up, w_down):
    m = lambda v, i: v[i * D:(i + 1) * D]
    mp = pack_cols([g3, m(mod_l, 5), m(mod_c, 5)])
    us, hs = shard_T(u2T), shard_T(h1T)
    in_maps = [{"uT": us[k], "hT": hs[k], "mp": mp, "wu": w_up, "wd": w_down} for k in range(NCORES)]
    res = _run(build_k3b(), in_maps)
    return unshard_T([res[k]["oT"] for k in range(NCORES)])


GELU_C = 0.044715
GELU_S = 2.0 * math.sqrt(2.0 / math.pi)


def build_k3a(even):
    pg = Prog()
    hT = pg.din("hT", [D, TPC])
    mp = pg.din("mp", [128, 16 * 8])
    KC = 16 if even else 32
    wout = pg.din("wout", [KC * 128, D])
    h1T = pg.dout("h1T", [D, TPC])
    u2T = pg.dout("u2T", [D, TPC])
    ones = setup_consts(pg)
    M = pg.sb("M", [128, KC, TPC], BF16)
    Y = pg.sb("Y", [128, 16, TPC])
    mpt = pg.sb("mpt", [128, 16, 8])
    coef = pg.sb("coef", [128, 16, 2])
    AB = pg.sb("AB", [128, 16, 2])
    rstd = pg.sb("rstd", [128, TPC])
    rstd2 = pg.sb("rstd2", [128, TPC])
    sqb = [(pg.sb("sq%d" % i, [128, TPC], BF16), "sq%d" % i) for i in range(2)]
    ssp = [(pg.ps("ss%d" % i, [128, 512]), "ss%d" % i) for i in range(3)]
    mmp = [(pg.ps("mm%d" % i, [128, 512]), "mm%d" % i) for i in range(4)]
    scr = [(pg.sb("scr%d" % i, [128, TPC]), "scr%d" % i) for i in range(4)]
    pg.dma("sp", mpt[:, :, :], mp.rearrange("p (c f) -> p c f", f=8), w=["mpt"])
    pg.tt("dve", coef[:, :, 0], mpt[:, :, 0], mpt[:, :, 1], ALU.mult, r=["mpt"], w=["coef"])
    pg.tt("dve", coef[:, :, 1], mpt[:, :, 0], mpt[:, :, 2], ALU.mult, r=["mpt"], w=["coef"])
    pg.stt(AB[:, :, 0], mpt[:, :, 5], 1.0, mpt[:, :, 3], ALU.add, ALU.mult, r=["mpt"], w=["AB"])
    pg.stt(AB[:, :, 1], mpt[:, :, 7], 1.0, mpt[:, :, 3], ALU.add, ALU.mult, r=["mpt"], w=["AB"])
    if even:
        ws = WStream(pg, "w", 16, gw=512, nbuf=3)
        oT = pg.din("oT", [1024, TPC])
        sfT = pg.din("sfT", [1024, TPC])
        srT = pg.din("srT", [1024, TPC])
        wglu = pg.din("wglu", [1024, 1024])
        bglu = pg.din("bglu", [128, 8])
        bgt = pg.sb("bgt", [128, 8])
        Gb = pg.sb("Gb", [128, 8, TPC], BF16)
        pg.dma("sp", bgt[:, :], bglu[:, :], w=["bgt"])
        ov = oT.rearrange("(c p) t -> p c t", p=128)
        for q in range(2):
            pg.dma("pool", M[:, 4 * q:4 * q + 4, :], ov[:, 4 * q:4 * q + 4, :], w=["M%d" % c for c in range(4 * q, 4 * q + 4)])
        fa = scr[0:2]
        fb = scr[2:4]
        for c in range(8):
            a_, aid = fa[c % 2]
            b_, bid = fb[c % 2]
            G = Y[:, 8 + c, :]
            gid = "Y%d" % (8 + c)
            pg.dma("sp", a_[:, :], sfT[c * 128:(c + 1) * 128, :], w=[aid])
            pg.dma("sp", b_[:, :], srT[c * 128:(c + 1) * 128, :], w=[bid])
            pg.tt("dve", a_[:, :], a_[:, :], b_[:, :], ALU.add, r=[aid, bid], w=[aid])
            pg.tt("pool", b_[:, :], a_[:, :], a_[:, :], ALU.mult, r=[aid], w=[bid])
            pg.ts("dve", b_[:, :], b_[:, :], GELU_C, ALU.mult, 1.0, ALU.add, r=[bid], w=[bid])
            pg.tt("pool", b_[:, :], b_[:, :], a_[:, :], ALU.mult, r=[aid, bid], w=[bid])
            pg.act(b_[:, :], b_[:, :], AF.Sigmoid, r=[bid], w=[bid], scale=GELU_S)
            pg.tt("dve", G, a_[:, :], b_[:, :], ALU.mult, r=[aid, bid], w=[gid])
            pg.copy("pool", Gb[:, c, :], G, r=[gid], w=["Gb%d" % c])
        st = {"n": 0}
        zt = [(pg.sb("zt%d" % i, [128, TT]), "zt%d" % i) for i in range(3)]

        def evac_glu(oc, tt, ps, psid):
            z_, zid = zt[st["n"] % 3]
            st["n"] += 1
            pg.act(z_[:, :], ps, AF.Sigmoid, r=[psid, "bgt"], w=[zid], bias=bgt[:, oc:oc + 1])
            pg.tt("dve", M[:, 8 + oc, tt * TT:(tt + 1) * TT], z_[:, :], Y[:, 8 + oc, tt * TT:(tt + 1) * TT], ALU.mult,
                  r=[zid, "Y%d" % (8 + oc)], w=["M%d" % (8 + oc)])

        ws.KC = 8
        linear_fm(pg, ws, wglu, 1024, Gb, lambda kc: "Gb%d" % kc, 8, mmp, evac_glu)
        ws.KC = 16

        def evac_out(oc, tt, ps, psid):
            eng = "act" if (tt % 2 == 0) else "dve"
            pg.copy(eng, Y[:, oc, tt * TT:(tt + 1) * TT], ps, r=[psid], w=["Y%d" % oc])
    else:
        ws = WStream(pg, "w", 32, gw=256, nbuf=2)
        yfT = pg.din("yfT", [4096, TPC])
        yrT = pg.din("yrT", [4096, TPC])
        xT = pg.din("xT", [4096, TPC])
        zT = pg.din("zT", [4096, TPC])
        nw = pg.din("nw", [128, 64])
        nwt = pg.sb("nwt", [128, 64])
        rstdg = pg.sb("rstdg", [128, TPC])
        pg.dma("sp", nwt[:, :], nw[:, :], w=["nwt"])
        HW = TPC // 2
        half = [(scr[i // 2][0][:, (i % 2) * HW:(i % 2 + 1) * HW], scr[i // 2][1] + "_h%d" % (i % 2)) for i in range(8)]
        for c in range(32):
            sb_, sid = sqb[c % 2]
            rows = slice(c * 128, (c + 1) * 128)
            for hf in range(2):
                cols = slice(hf * HW, (hf + 1) * HW)
                base = 4 * ((2 * c + hf) % 2)
                (a_, aid), (b_, bid), (x_, xid), (z_, zid) = half[base:base + 4]
                pg.dma("sp", a_, yfT[rows, cols], w=[aid])
                pg.dma("sp", b_, yrT[rows, cols], w=[bid])
                pg.dma("sp", x_, xT[rows, cols], w=[xid])
                pg.dma("sp", z_, zT[rows, cols], w=[zid])
                pg.tt("dve", a_, a_, b_, ALU.add, r=[aid, bid], w=[aid])
                pg.stt(a_, x_, nwt[:, 32 + c:33 + c], a_, ALU.mult, ALU.add, r=[aid, xid, "nwt"], w=[aid])
                pg.act(z_, z_, AF.Silu, r=[zid], w=[zid])
                pg.tt("dve", a_, a_, z_, ALU.mult, r=[aid, zid], w=[aid])
                pg.act(sb_[:, cols], a_, AF.Square, r=[aid], w=[sid + "_h%d" % hf])
                pg.act(M[:, c, cols], a_, AF.Identity, r=[aid, "nwt"], w=["M%d" % c], scale=nwt[:, c:c + 1])
            for tt in range(NTT):
                ps, psid = ssp[tt]
                pg.mm(ps[:, 0:TT], ones[:, :], sb_[:, tt * TT:(tt + 1) * TT], c == 0, c == 31,
                      r=[sid + "_h0", sid + "_h1", "ones"], w=[psid])
        for i in range(4):
            t_, tid = scr[i]
            pg.copy("dve", t_[:, 0:1], t_[:, 0:1], w=[tid + "_h0", tid + "_h1", tid])
        for i in range(2):
            t_, tid = sqb[i]
            pg.copy("dve", t_[:, 0:1], t_[:, 0:1], w=[tid + "_h0", tid + "_h1", tid])
        rg_ids = []
        for tt in range(NTT):
            ps, psid = ssp[tt]
            sl = rstdg[:, tt * TT:(tt + 1) * TT]
            pg.act(sl, ps[:, 0:TT], AF.Sqrt, r=[psid], w=["rg%d" % tt], bias=pg.eps_ap, scale=1.0 / 4096)
            pg.recip(sl, sl, r=["rg%d" % tt], w=["rg%d" % tt])
            rg_ids.append("rg%d" % tt)

        def evac_out(oc, tt, ps, psid):
            pg.tt("dve", Y[:, oc, tt * TT:(tt + 1) * TT], ps, rstdg[:, tt * TT:(tt + 1) * TT], ALU.mult,
                  r=[psid, "rg%d" % tt], w=["Y%d" % oc])

    linear_fm(pg, ws, wout, D, M, lambda kc: "M%d" % kc, KC, mmp, evac_out)
    rs_ids = sumsq_rstd(pg, "n1", lambda c: Y[:, c, :], lambda c: "Y%d" % c, 16, D, ones, sqb, ssp, rstd)
    residual_tail(pg, "r1", Y, lambda c: "Y%d" % c, rstd, rs_ids, coef, "coef", hT, h1T, scr, inplace=True)
    rs2 = sumsq_rstd(pg, "n2", lambda c: Y[:, c, :], lambda c: "Y%d" % c, 16, D, ones, sqb, ssp, rstd2)
    tb = scr[0:2]
    ub = scr[2:4]
    for c in range(16):
        t_, tid = tb[c % 2]
        u_, uid = ub[c % 2]
        pg.tt("dve", t_[:, :], Y[:, c, :], rstd2[:, :], ALU.mult, r=["Y%d" % c] + rs2, w=[tid])
        pg.act(u_[:, 0:LAT], t_[:, 0:LAT], AF.Identity, r=[tid, "AB", "mpt"], w=[uid], bias=mpt[:, c, 4:5], scale=AB[:, c, 0:1])
        pg.act(u_[:, LAT:TPC], t_[:, LAT:TPC], AF.Identity, r=[tid, "AB", "mpt"], w=[uid], bias=mpt[:, c, 6:7], scale=AB[:, c, 1:2])
        pg.dma("sp", u2T[c * 128:(c + 1) * 128, :], u_[:, :], r=[uid], final=True)
    return pg


def run_k3a(even, hT, mix, g, mod_l, mod_c, w_out, extra):
    m = lambda v, i: v[i * D:(i + 1) * D]
    mp = pack_cols([g[1], m(mod_l, 2), m(mod_c, 2), g[2], m(mod_l, 3), m(mod_l, 4), m(mod_c, 3), m(mod_c, 4)])
    hs = shard_T(hT)
    sh = {k: shard_T(v) for k, v in mix.items()}
    in_maps = []
    for k in range(NCORES):
        d = {"hT": hs[k], "mp": mp, "wout": w_out}
        for kk in sh:
            d[kk] = sh[kk][k]
        d.update(extra)
        in_maps.append(d)
    res = _run(build_k3a(even), in_maps)
    return unshard_T([res[k]["h1T"] for k in range(NCORES)]), unshard_T([res[k]["u2T"] for k in range(NCORES)])


NCH = TOK // 128


def build_k2ob(nch=NCH):
    pg = Prog()
    T = nch * 128
    xa = pg.din("xa", [2, T, 512])
    dtr = pg.din("dtr", [2, T, 8])
    Bm = pg.din("Bm", [2, T, 128])
    BT = pg.din("BT", [2, 128, T])
    CT = pg.din("CT", [2, 128, T])
    hp = pg.din("hp", [128, 32])
    msk = pg.din("msk", [128, 256])
    y = pg.dout("y", [2, T, 512])
    hpt = pg.sb("hpt", [128, 2, 2, 8])
    mk = pg.sb("mk", [128, 256])
    onesf = pg.sb("onesf", [128, 128])
    Aneg = pg.sb("Aneg", [128, 2, 8])
    pg.dma("sp", hpt[:, :, :, :], hp.rearrange("p (d f h) -> p d f h", d=2, f=2), w=["hpt"])
    pg.dma("sp", mk[:, :], msk[:, :], w=["mk"])
    pg.memset("dve", onesf[:, :], 1.0, w=["onesf"])
    UT = mk[:, 0:128]
    TRI = mk[:, 128:256]
    for d in range(2):
        pg.act(Aneg[:, d, :], hpt[:, d, 1, :], AF.Exp, r=["hpt"], w=["Aneg"])
    pg.ts("dve", Aneg[:, :, :], Aneg[:, :, :], -1.0, ALU.mult, r=["Aneg"], w=["Aneg"])
    NB = 4

    def tiles(nm, shape, dt=F32):
        return [[(pg.sb("%s_%d_%d" % (nm, d, i), shape, dt), "%s_%d_%d" % (nm, d, i)) for i in range(NB)] for d in range(2)]

    Xt = tiles("X", [128, 8, 64])
    Dt = tiles("dt", [128, 8])
    At = tiles("a", [128, 8])
    Bb = tiles("Bb", [128, 128], BF16)
    BTb = tiles("BTb", [128, 128], BF16)
    CTb = tiles("CTb", [128, 128], BF16)
    Rt = tiles("R", [128, 8, 128])
    Et = tiles("E", [128, 8, 128])
    CBm = tiles("CBm", [128, 128])
    MT = tiles("MT", [128, 8, 128], BF16)
    xdt = tiles("xdt", [128, 8, 64], BF16)
    xdec = tiles("xdec", [128, 8, 64], BF16)
    ev = tiles("ev", [128, 3, 8])
    yt = tiles("yt", [128, 8, 64])
    S = [(pg.sb("S%d" % d, [128, 8, 64]), "S%d" % d) for d in range(2)]
    Sb = [(pg.sb("Sb%d" % d, [128, 8, 64], BF16), "Sb%d" % d) for d in range(2)]
    p_seg = [(pg.ps("pseg%d" % i, [128, 512]), "pseg%d" % i) for i in range(2)]
    p_cbt = (pg.ps("pcbt", [128, 512]), "pcbt")
    p_yd = (pg.ps("pyd", [128, 512]), "pyd")
    p_yo = (pg.ps("pyo", [128, 512]), "pyo")
    p_ns = (pg.ps("pns", [128, 512]), "pns")
    p_vec = (pg.ps("pvec", [128, 512]), "pvec")
    for d in range(2):
        pg.memset("dve", S[d][0][:, :, :], 0.0, w=[S[d][1]])
        pg.memset("dve", Sb[d][0][:, :, :], 0.0, w=[Sb[d][1]])

    def bc_h(ap8, n):
        return ap8.unsqueeze(2).to_broadcast([128, 8, n])

    def T_(tl, n):
        ci, d = divmod(n, 2)
        return tl[d][ci % NB]

    def s0(n):
        ci, d = divmod(n, 2)
        t0 = ci * 128
        X, Xi = T_(Xt, n)
        dtt, dti = T_(Dt, n)
        pg.dma("sp", X[:, :, :], xa[d, t0:t0 + 128, :].rearrange("t (h p) -> t h p", h=8), w=[Xi])
        pg.dma("sp", dtt[:, :], dtr[d, t0:t0 + 128, :], w=[dti])
        pg.dma("pool", T_(Bb, n)[0][:, :], Bm[d, t0:t0 + 128, :], w=[T_(Bb, n)[1]])
        pg.dma("pool", T_(BTb, n)[0][:, :], BT[d, :, t0:t0 + 128], w=[T_(BTb, n)[1]])
        pg.dma("pool", T_(CTb, n)[0][:, :], CT[d, :, t0:t0 + 128], w=[T_(CTb, n)[1]])

    def s1(n):
        ci, d = divmod(n, 2)
        X, Xi = T_(Xt, n)
        dtt, dti = T_(Dt, n)
        a_, ai = T_(At, n)
        R_, Ri = T_(Rt, n)
        xd_, xdi = T_(xdt, n)
        pg.tt("dve", dtt[:, :], dtt[:, :], hpt[:, d, 0, :], ALU.add, r=[dti, "hpt"], w=[dti])
        pg.act(dtt[:, :], dtt[:, :], AF.Exp, r=[dti], w=[dti])
        pg.act(dtt[:, :], dtt[:, :], AF.Ln, r=[dti], w=[dti], bias=1.0)
        pg.tt("dve", a_[:, :], dtt[:, :], Aneg[:, d, :], ALU.mult, r=[dti, "Aneg"], w=[ai])
        pg.tt("pool", R_[:, :, :], TRI.unsqueeze(1).to_broadcast([128, 8, 128]), bc_h(a_[:, :], 128), ALU.mult,
              r=[ai, "mk"], w=[Ri])
        pg.tt("pool", xd_[:, :, :], X[:, :, :], bc_h(dtt[:, :], 64), ALU.mult, r=[Xi, dti], w=[xdi])

    def s2(n):
        a_, ai = T_(At, n)
        R_, Ri = T_(Rt, n)
        for hf in range(2):
            ps, pid = p_seg[hf]
            pg.mm(ps[:, :], UT, R_[:, 4 * hf:4 * hf + 4, :].rearrange("p h l -> p (h l)"), True, True, r=["mk", Ri], w=[pid])
        ps, pid = p_cbt
        pg.mm(ps[:, 0:128], T_(BTb, n)[0][:, :], T_(CTb, n)[0][:, :], True, True, r=[T_(BTb, n)[1], T_(CTb, n)[1]], w=[pid])
        pv, pvi = p_vec
        pg.mm(pv[:, 0:8], TRI, a_[:, :], True, True, r=["mk", ai], w=[pvi])
        pg.mm(pv[:, 8:16], UT, a_[:, :], True, True, r=["mk", ai], w=[pvi])
        pg.mm(pv[:, 16:24], onesf[:, :], a_[:, :], True, True, r=["onesf", ai], w=[pvi])

    def s3(n):
        E_, Ei = T_(Et, n)
        CB_, CBi = T_(CBm, n)
        ev_, evi = T_(ev, n)
        for hf in range(2):
            ps, pid = p_seg[hf]
            pg.act(E_[:, 4 * hf:4 * hf + 4, :].rearrange("p h l -> p (h l)"), ps[:, :], AF.Exp, r=[pid], w=[Ei + "_%d" % hf])
        ps, pid = p_cbt
        pg.tt("dve", CB_[:, :], ps[:, 0:128], TRI, ALU.mult, r=[pid, "mk"], w=[CBi])
        pv, pvi = p_vec
        pg.act(ev_[:, :, :].rearrange("p a h -> p (a h)"), pv[:, 0:24], AF.Exp, r=[pvi], w=[evi])

    def s4(n):
        E_, Ei = T_(Et, n)
        CB_, CBi = T_(CBm, n)
        M_, Mi = T_(MT, n)
        xd_, xdi = T_(xdt, n)
        xc_, xci = T_(xdec, n)
        ev_, evi = T_(ev, n)
        pg.tt("dve", M_[:, :, :], E_[:, :, :], CB_[:, :].unsqueeze(1).to_broadcast([128, 8, 128]), ALU.mult,
              r=[Ei + "_0", Ei + "_1", CBi], w=[Mi])
        pg.tt("dve", xc_[:, :, :], xd_[:, :, :], bc_h(ev_[:, 1, :], 64), ALU.mult, r=[xdi, evi], w=[xci])

    def s5(n):
        ci, d = divmod(n, 2)
        M_, Mi = T_(MT, n)
        xd_, xdi = T_(xdt, n)
        xc_, xci = T_(xdec, n)
        pyd, pydi = p_yd
        for h in range(8):
            pg.mm(pyd[:, 64 * h:64 * h + 64], M_[:, h, :], xd_[:, h, :], True, True, r=[Mi, xdi], w=[pydi])
        pyo, pyoi = p_yo
        pg.mm(pyo[:, :], T_(CTb, n)[0][:, :], Sb[d][0][:, :, :].rearrange("p h q -> p (h q)"), True, True,
              r=[T_(CTb, n)[1], Sb[d][1]], w=[pyoi])
        pns, pnsi = p_ns
        pg.mm(pns[:, :], T_(Bb, n)[0][:, :], xc_[:, :, :].rearrange("p h q -> p (h q)"), True, True, r=[T_(Bb, n)[1], xci], w=[pnsi])

    def s6(n):
        ci, d = divmod(n, 2)
        t0 = ci * 128
        ev_, evi = T_(ev, n)
        y_, yi = T_(yt, n)
        S_, Si = S[d]
        Sb_, Sbi = Sb[d]
        pyd, pydi = p_yd
        pyo, pyoi = p_yo
        pns, pnsi = p_ns
        pg.tt("dve", S_[:, :, :], S_[:, :, :], bc_h(ev_[:, 2, :], 64), ALU.mult, r=[Si, evi], w=[Si])
        pg.tt("dve", S_[:, :, :], pns[:, :].rearrange("p (h q) -> p h q", h=8), S_[:, :, :], ALU.add, r=[pnsi, Si], w=[Si])
        pg.copy("act", Sb_[:, :, :], S_[:, :, :], r=[Si], w=[Sbi])
        pg.tt("dve", y_[:, :, :], pyo[:, :].rearrange("p (h q) -> p h q", h=8), bc_h(ev_[:, 0, :], 64), ALU.mult,
              r=[pyoi, evi], w=[yi])
        pg.tt("dve", y_[:, :, :], pyd[:, :].rearrange("p (h q) -> p h q", h=8), y_[:, :, :], ALU.add, r=[pydi, yi], w=[yi])
        pg.dma("sp", y[d, t0:t0 + 128, :], y_[:, :, :].rearrange("p h q -> p (h q)"), r=[yi], final=True)

    stages = [s0, s1, s2, s3, s4, s5, s6]
    NS = nch * 2
    for k in range(NS + len(stages) - 1):
        for si in range(len(stages) - 1, -1, -1):
            n = k - si
            if 0 <= n < NS:
                stages[si](n)
    return pg


def ssd_masks():
    k = np.arange(128)
    UT = (k[:, None] > k[None, :]).astype(np.float32)
    TRI = (k[:, None] <= k[None, :]).astype(np.float32)
    return np.ascontiguousarray(np.concatenate([UT, TRI], axis=1))


def build_k2oa():
    pg = Prog()
    xin = pg.din("xin", [768, TOK])
    cw = pg.din("cw", [128, 36])
    xo = pg.dout("xo", [768, TOK])
    cwt = pg.sb("cwt", [128, 6, 6])
    pg.dma("sp", cwt[:, :, :], cw.rearrange("p (c f) -> p c f", f=6), w=["cwt"])
    Xb = [(pg.sb("X%d" % i, [128, TOK]), "X%d" % i) for i in range(2)]
    Ab = [(pg.sb("A%d" % i, [128, TOK]), "A%d" % i) for i in range(2)]
    segs = [(0, CTX), (CTX, TOK)]
    for c in range(6):
        X, Xi = Xb[c % 2]
        A, Ai = Ab[c % 2]
        for q in range(4):
            pg.dma("sp", X[:, q * 2112:(q + 1) * 2112], xin[c * 128:(c + 1) * 128, q * 2112:(q + 1) * 2112], w=[Xi])
        for (s, e) in segs:
            pg.ts("dve", A[:, s:e], X[:, s:e], cwt[:, c, 2:3], ALU.mult, r=[Xi, "cwt"], w=[Ai])
            for k in (0, 1, 3, 4):
                o = k - 2
                lo = s + max(0, -o)
                hi = e - max(0, o)
                pg.stt(A[:, lo:hi], X[:, lo + o:hi + o], cwt[:, c, k:k + 1], A[:, lo:hi], ALU.mult, ALU.add,
                       r=[Xi, Ai, "cwt"], w=[Ai])
        for q in range(4):
            sl = slice(q * 2112, (q + 1) * 2112)
            pg.act(A[:, sl], A[:, sl], AF.Silu, r=[Ai], w=[Ai], bias=cwt[:, c, 5:6])
        pg.dma("sp", xo[c * 128:(c + 1) * 128, :], A[:, :], r=[Ai], final=True)
    return pg


def run_k2oa(xbcT, conv_w, conv_b):
    in_maps = []
    for k in range(NCORES):
        ch = slice(768 * k, 768 * (k + 1))
        f = np.concatenate([conv_w[:, ch], conv_b[None, ch]], axis=0)
        cwp = f.reshape(6, 6, 128).transpose(2, 1, 0).reshape(128, 36)
        in_maps.append({"xin": np.ascontiguousarray(xbcT[ch]), "cw": np.ascontiguousarray(cwp)})
    res = _run(build_k2oa(), in_maps)
    return np.concatenate([res[k]["xo"] for k in range(NCORES)], axis=0)


def flipseg(a, axis):
    a = np.moveaxis(a, axis, 0)
    out = np.concatenate([a[:CTX][::-1], a[CTX:][::-1]], axis=0)
    return np.moveaxis(out, 0, axis)


def run_k2ob(xbcaT, dtrT, dt_bias, a_log):
    in_maps = []
    msk = ssd_masks()
    for g in range(NCORES):
        xs = xbcaT[512 * g:512 * (g + 1)]
        Bs = xbcaT[4096 + 128 * g:4096 + 128 * (g + 1)]
        Cs = xbcaT[5120 + 128 * g:5120 + 128 * (g + 1)]
        xa, dtr, Bm, BTt, CTt = [], [], [], [], []
        for d in range(2):
            f = (lambda a: flipseg(a, 1)) if d == 1 else (lambda a: a)
            xa.append(f(xs).T)
            dtr.append(f(dtrT[64 * d + 8 * g:64 * d + 8 * g + 8]).T)
            Bm.append(f(Bs).T)
            BTt.append(f(Bs))
            CTt.append(f(Cs))
        hp = np.stack([dt_bias[:, 8 * g:8 * g + 8], a_log[:, 8 * g:8 * g + 8]], axis=1).reshape(1, 32).repeat(128, 0)
        c = np.ascontiguousarray
        in_maps.append({"xa": c(np.stack(xa)), "dtr": c(np.stack(dtr)), "Bm": c(np.stack(Bm)), "BT": c(np.stack(BTt)),
                        "CT": c(np.stack(CTt)), "hp": c(hp.astype(np.float32)), "msk": msk})
    res = _run(build_k2ob(), in_maps)
    yf = np.concatenate([res[g]["y"][0].T for g in range(NCORES)], axis=0)
    yr = np.concatenate([flipseg(res[g]["y"][1], 0).T for g in range(NCORES)], axis=0)
    return yf, yr


S5T = 256
S5N = TOK // S5T
PI = math.pi


def build_k2e(lam_init):
    pg = Prog()
    qk = pg.din("qk", [2, 2, 64, TOK])
    qkp = pg.din("qkp", [2, 2, 64, TOK])
    cs = pg.din("cs", [2, 64, TOK])
    v = pg.din("v", [TOK, 128])
    lamb = pg.din("lamb", [128, 256])
    sg = pg.din("sg", [128, 1])
    su = pg.din("su", [2, 128, TOK])
    Bblk = pg.din("Bblk", [128, 2 * 4 * 2 * 128])
    Cblk = pg.din("Cblk", [128, 2 * 4 * 2 * 128])
    s5p = pg.din("s5p", [128, 2 * 4 * 3])
    dsk = pg.din("dsk", [128, 1])
    oT = pg.dout("oT", [128, TOK])
    sf = pg.dout("sf", [128, TOK])
    sr = pg.dout("sr", [128, TOK])
    ones = setup_consts(pg)

    lt = pg.sb("lt", [128, 4, 64])
    sgt = pg.sb("sgt", [128, 1])
    dskt = pg.sb("dskt", [128, 1])
    lam2 = pg.sb("lam2", [128, 2, 64])
    lame = pg.sb("lame", [128, 2])
    nlam = pg.sb("nlam", [128, 1])
    pg.dma("sp", lt[:, :, :], lamb.rearrange("p (a b) -> p a b", a=4), w=["lt"])
    pg.dma("sp", sgt[:, :], sg[:, :], w=["sgt"])
    pg.dma("sp", dskt[:, :], dsk[:, :], w=["dskt"])
    pg.tt("dve", lam2[:, 0, :], lt[:, 0, :], lt[:, 1, :], ALU.mult, r=["lt"], w=["lam2"])
    pg.tt("dve", lam2[:, 1, :], lt[:, 2, :], lt[:, 3, :], ALU.mult, r=["lt"], w=["lam2"])
    pg.add("dve", lambda e: e.tensor_reduce(out=lame[:, :], in_=lam2[:, :, :], axis=AX.X, op=ALU.add), r=["lam2"], w=["lame"])
    pg.act(lame[:, :], lame[:, :], AF.Exp, r=["lame"], w=["lame"])
    pg.tt("dve", nlam[:, :], lame[:, 1:2], lame[:, 0:1], ALU.subtract, r=["lame"], w=["nlam"])
    pg.ts("dve", nlam[:, :], nlam[:, :], -lam_init, ALU.add, r=["nlam"], w=["nlam"])
    pg.ts("dve", sgt[:, :], sgt[:, :], 1.0 - lam_init, ALU.mult, r=["sgt"], w=["sgt"])

    Bb = pg.sb("Bb", [128, 16, 128], BF16)
    Cb = pg.sb("Cb", [128, 16, 128], BF16)
    pg.dma("pool", Bb[:, :, :], Bblk.rearrange("p (a n) -> p a n", n=128), w=["Bb"])
    pg.dma("pool", Cb[:, :, :], Cblk.rearrange("p (a n) -> p a n", n=128), w=["Cb"])
    subt = [(pg.sb("sub%d" % i, [128, S5T], BF16), "sub%d" % i) for i in range(3)]
    pt = pg.sb("pt", [128, 2, 4, 3])
    pg.dma("sp", pt[:, :, :, :], s5p.rearrange("p (d g f) -> p d g f", d=2, g=4), w=["pt"])
    W8 = [128, 2, 4]
    names = ["dt", "th", "rho", "m", "sn", "cn", "thc", "nre", "nim", "den", "kre", "kim", "t1", "t2"]
    sm = {n: pg.sb("s5_" + n, W8) for n in names}
    a3 = lambda n: sm[n][:, :, :]
    lre, lim, ldt = pt[:, :, :, 0], pt[:, :, :, 1], pt[:, :, :, 2]
    pg.act(a3("dt"), ldt, AF.Exp, r=["pt"], w=["s_dt"])
    pg.tt("dve", a3("th"), lim, a3("dt"), ALU.mult, r=["pt", "s_dt"], w=["s_th"])
    pg.tt("dve", a3("rho"), lre, a3("dt"), ALU.mult, r=["pt", "s_dt"], w=["s_rho"])
    pg.act(a3("rho"), a3("rho"), AF.Exp, r=["s_rho"], w=["s_rho"])
    for _ in range(4):
        pg.ts("dve", a3("m"), a3("th"), PI, ALU.is_gt, r=["s_th"], w=["s_m"])
        pg.stt(a3("th"), a3("m"), -2.0 * PI, a3("th"), ALU.mult, ALU.add, r=["s_m", "s_th"], w=["s_th"])
    pg.act(a3("sn"), a3("th"), AF.Sin, r=["s_th"], w=["s_sn"])
    pg.ts("dve", a3("thc"), a3("th"), PI / 2, ALU.add, r=["s_th"], w=["s_thc"])
    pg.ts("dve", a3("m"), a3("thc"), PI, ALU.is_gt, r=["s_thc"], w=["s_m"])
    pg.stt(a3("thc"), a3("m"), -2.0 * PI, a3("thc"), ALU.mult, ALU.add, r=["s_m", "s_thc"], w=["s_thc"])
    pg.act(a3("cn"), a3("thc"), AF.Sin, r=["s_thc"], w=["s_cn"])
    pg.tt("dve", a3("nre"), a3("rho"), a3("cn"), ALU.mult, r=["s_rho", "s_cn"], w=["s_nre"])
    pg.ts("dve", a3("nre"), a3("nre"), -1.0, ALU.add, r=["s_nre"], w=["s_nre"])
    pg.tt("dve", a3("nim"), a3("rho"), a3("sn"), ALU.mult, r=["s_rho", "s_sn"], w=["s_nim"])
    pg.tt("dve", a3("den"), lre, lre, ALU.mult, r=["pt"], w=["s_den"])
    pg.tt("dve", a3("t1"), lim, lim, ALU.mult, r=["pt"], w=["s_t1"])
    pg.tt("dve", a3("den"), a3("den"), a3("t1"), ALU.add, r=["s_den", "s_t1"], w=["s_den"])
    pg.recip(a3("den"), a3("den"), r=["s_den"], w=["s_den"])
    pg.tt("dve", a3("t1"), a3("nre"), lre, ALU.mult, r=["s_nre", "pt"], w=["s_t1"])
    pg.tt("dve", a3("t2"), a3("nim"), lim, ALU.mult, r=["s_nim", "pt"], w=["s_t2"])
    pg.tt("dve", a3("kre"), a3("t1"), a3("t2"), ALU.add, r=["s_t1", "s_t2"], w=["s_kre"])
    pg.tt("dve", a3("kre"), a3("kre"), a3("den"), ALU.mult, r=["s_kre", "s_den"], w=["s_kre"])
    pg.tt("dve", a3("t1"), a3("nim"), lre, ALU.mult, r=["s_nim", "pt"], w=["s_t1"])
    pg.tt("dve", a3("t2"), a3("nre"), lim, ALU.mult, r=["s_nre", "pt"], w=["s_t2"])
    pg.tt("dve", a3("kim"), a3("t1"), a3("t2"), ALU.subtract, r=["s_t1", "s_t2"], w=["s_kim"])
    pg.tt("dve", a3("kim"), a3("kim"), a3("den"), ALU.mult, r=["s_kim", "s_den"], w=["s_kim"])
    TS = [128, 2, 4, S5T]
    Ec, Es, Fre, Fim, Rho, Tmp, Tmp2 = [pg.sb("tab%d" % i, TS) for i in range(7)]
    pg.copy("dve", Ec[:, :, :, 0], a3("cn"), r=["s_cn"], w=["Ec"])
    pg.copy("dve", Es[:, :, :, 0], a3("sn"), r=["s_sn"], w=["Es"])
    mlen = 1
    while mlen < S5T:
        bc = lambda T_: T_[:, :, :, mlen - 1:mlen].to_broadcast([128, 2, 4, mlen])
        lo = lambda T_: T_[:, :, :, 0:mlen]
        hi = lambda T_: T_[:, :, :, mlen:2 * mlen]
        pg.tt("dve", lo(Tmp), lo(Ec), bc(Ec), ALU.mult, r=["Ec"], w=["Tmp"])
        pg.tt("dve", lo(Tmp2), lo(Es), bc(Es), ALU.mult, r=["Es"], w=["Tmp2"])
        pg.tt("dve", hi(Tmp), lo(Ec), bc(Es), ALU.mult, r=["Ec", "Es"], w=["Tmp"])
        pg.tt("dve", hi(Tmp2), lo(Es), bc(Ec), ALU.mult, r=["Ec", "Es"], w=["Tmp2"])
        pg.tt("dve", hi(Ec), lo(Tmp), lo(Tmp2), ALU.subtract, r=["Tmp", "Tmp2"], w=["Ec"])
        pg.tt("dve", hi(Es), hi(Tmp), hi(Tmp2), ALU.add, r=["Tmp", "Tmp2"], w=["Es"])
        mlen *= 2
    bk = lambda n: sm[n][:, :, :].unsqueeze(3).to_broadcast(TS)
    A4 = lambda T_: T_[:, :, :, :]
    pg.tt("dve", A4(Tmp), A4(Ec), bk("kre"), ALU.mult, r=["Ec", "s_kre"], w=["Tmp"])
    pg.tt("dve", A4(Tmp2), A4(Es), bk("kim"), ALU.mult, r=["Es", "s_kim"], w=["Tmp2"])
    pg.tt("dve", A4(Fre), A4(Tmp), A4(Tmp2), ALU.add, r=["Tmp", "Tmp2"], w=["Fre"])
    pg.tt("dve", A4(Tmp), A4(Es), bk("kre"), ALU.mult, r=["Es", "s_kre"], w=["Tmp"])
    pg.tt("dve", A4(Tmp2), A4(Ec), bk("kim"), ALU.mult, r=["Ec", "s_kim"], w=["Tmp2"])
    pg.tt("dve", A4(Fim), A4(Tmp), A4(Tmp2), ALU.subtract, r=["Tmp", "Tmp2"], w=["Fim"])
    pg.copy("dve", A4(Rho), bk("rho"), r=["s_rho"], w=["Rho"])

    carry = pg.sb("carry", [128, 2, 4, 2])
    pg.memset("dve", carry[:, :, :, :], 0.0, w=["carry%d%d" % (d, g) for d in range(2) for g in range(4)])
    NB = 3
    s5t = [[(pg.sb("s5w%d_%d" % (j, i), [128, S5T]), "s5w%d_%d" % (j, i)) for j in range(10)] for i in range(NB)]
    s5c = [[(pg.sb("s5c%d_%d" % (j, i), [128, S5T], BF16), "s5c%d_%d" % (j, i)) for j in range(2)] for i in range(NB)]
    s5u = [(pg.sb("s5u%d" % i, [128, S5T]), "s5u%d" % i) for i in range(2)]
    s5o = [(pg.sb("s5o%d" % i, [128, S5T]), "s5o%d" % i) for i in range(2)]
    p_raw = [(pg.ps("praw%d" % i, [128, 512]), "praw%d" % i) for i in range(2)]
    p_y = [(pg.ps("pys5%d" % i, [128, 512]), "pys5%d" % i) for i in range(1)]
    NU = S5N * 8

    def unit(n):
        ci, r = divmod(n, 8)
        d, g = divmod(r, 4)
        return d, ci, g, ci * 2 + d

    def s5_prefetch(grp):
        if grp >= S5N * 2:
            return
        ci, d = divmod(grp, 2)
        sb_, sbid = subt[grp % 3]
        pg.dma("pool", sb_[:, :], su[d, :, ci * S5T:(ci + 1) * S5T], w=[sbid])

    def stA(n):
        d, ci, g, grp = unit(n)
        if g == 0:
            s5_prefetch(grp + 1)
        sb_, sbid = subt[grp % 3]
        praw, prid = p_raw[n % 2]
        for comp in range(2):
            pg.mm(praw[:, comp * S5T:(comp + 1) * S5T], Bb[:, (d * 4 + g) * 2 + comp, :], sb_[:, :], True, True,
                  r=["Bb", sbid], w=[prid])

    def stB(n):
        d, ci, g, grp = unit(n)
        praw, prid = p_raw[n % 2]
        W = s5t[n % NB]
        cid = "carry%d%d" % (d, g)
        rre, rim = praw[:, 0:S5T], praw[:, S5T:2 * S5T]
        fre, fim = Fre[:, d, g, :], Fim[:, d, g, :]
        (t1, i1), (t2, i2), (bre, ib), (bim, ibm), (gre, igr), (gim, igi) = W[0:6]
        pg.tt("dve", t1[:, :], rre, fre, ALU.mult, r=[prid, "Fre"], w=[i1])
        pg.tt("dve", t2[:, :], rim, fim, ALU.mult, r=[prid, "Fim"], w=[i2])
        pg.tt("dve", bre[:, :], t1[:, :], t2[:, :], ALU.add, r=[i1, i2], w=[ib])
        pg.tt("dve", t1[:, :], rre, fim, ALU.mult, r=[prid, "Fim"], w=[i1])
        pg.tt("dve", t2[:, :], rim, fre, ALU.mult, r=[prid, "Fre"], w=[i2])
        pg.tt("dve", bim[:, :], t1[:, :], t2[:, :], ALU.subtract, r=[i1, i2], w=[ibm])
        init_re = 0.0 if ci == 0 else carry[:, d, g, 0:1]
        init_im = 0.0 if ci == 0 else carry[:, d, g, 1:2]
        pg.add("dve", lambda e, o=gre[:, :], a=Rho[:, d, g, :], b=bre[:, :], ini=init_re:
               e.tensor_tensor_scan(out=o, data0=a, data1=b, initial=ini, op0=ALU.mult, op1=ALU.add), r=["Rho", ib, cid], w=[igr])
        pg.add("dve", lambda e, o=gim[:, :], a=Rho[:, d, g, :], b=bim[:, :], ini=init_im:
               e.tensor_tensor_scan(out=o, data0=a, data1=b, initial=ini, op0=ALU.mult, op1=ALU.add), r=["Rho", ibm, cid], w=[igi])

    def stC(n):
        d, ci, g, grp = unit(n)
        W = s5t[n % NB]
        cid = "carry%d%d" % (d, g)
        ec, es = Ec[:, d, g, :], Es[:, d, g, :]
        (gre, igr), (gim, igi), (u1, j1), (u2, j2), (cre, icr), (cim, ici) = W[4:10]
        pg.tt("pool", u1[:, :], gre[:, :], ec, ALU.mult, r=[igr, "Ec"], w=[j1])
        pg.tt("pool", u2[:, :], gim[:, :], es, ALU.mult, r=[igi, "Es"], w=[j2])
        pg.tt("pool", cre[:, :], u1[:, :], u2[:, :], ALU.add, r=[j1, j2], w=[icr])
        pg.tt("pool", u1[:, :], gim[:, :], ec, ALU.mult, r=[igi, "Ec"], w=[j1])
        pg.tt("pool", u2[:, :], gre[:, :], es, ALU.mult, r=[igr, "Es"], w=[j2])
        pg.tt("pool", cim[:, :], u1[:, :], u2[:, :], ALU.subtract, r=[j1, j2], w=[ici])
        pg.copy("pool", carry[:, d, g, 0:1], cre[:, S5T - 1:S5T], r=[icr], w=[cid])
        pg.copy("pool", carry[:, d, g, 1:2], cim[:, S5T - 1:S5T], r=[ici], w=[cid])

    def stD(n):
        W = s5t[n % NB]
        (cre, icr), (cim, ici) = W[8:10]
        (cbr, icbr), (cbi, icbi) = s5c[n % NB]
        pg.copy("act", cbr[:, :], cre[:, :], r=[icr], w=[icbr])
        pg.copy("act", cbi[:, :], cim[:, :], r=[ici], w=[icbi])

    def stE(n):
        d, ci, g, grp = unit(n)
        (cbr, icbr), (cbi, icbi) = s5c[n % NB]
        py, pyid = p_y[0]
        pg.mm(py[:, 0:S5T], Cb[:, (d * 4 + g) * 2 + 0, :], cbr[:, :], g == 0, False, r=["Cb", icbr], w=[pyid])
        pg.mm(py[:, 0:S5T], Cb[:, (d * 4 + g) * 2 + 1, :], cbi[:, :], False, g == 3, r=["Cb", icbi], w=[pyid])

    def stF(n):
        d, ci, g, grp = unit(n)
        if g != 3:
            return
        t0 = ci * S5T
        py, pyid = p_y[0]
        o_, oid = s5o[d]
        if d == 0:
            u_, uid = s5u[ci % 2]
            pg.dma("sp", u_[:, :], su[0, :, t0:t0 + S5T], w=[uid])
            pg.stt(o_[:, :], u_[:, :], dskt[:, 0:1], py[:, 0:S5T], ALU.mult, ALU.add, r=[uid, "dskt", pyid], w=[oid])
            pg.dma("sp", sf[:, t0:t0 + S5T], o_[:, :], r=[oid], final=True)
        else:
            pg.copy("dve", o_[:, :], py[:, 0:S5T], r=[pyid], w=[oid])
            pg.dma("sp", sr[:, t0:t0 + S5T], o_[:, :], r=[oid], final=True)

    stages = [stA, stB, stC, stD, stE, stF]
    tick = {"k": 0}

    def s5_tick():
        k = tick["k"]
        tick["k"] += 1
        for si in (5, 0, 1, 2, 3, 4):
            n = k - si
            if 0 <= n < NU:
                stages[si](n)

    s5_prefetch(0)

    KT = pg.sb("KT", [64, 2, TOK], BF16)
    V = pg.sb("V", [128, NCH, 128], BF16)
    vv = v.rearrange("(c p) e -> p c e", p=128)
    for q in range(3):
        pg.dma("pool", V[:, 22 * q:22 * q + 22, :], vv[:, 22 * q:22 * q + 22, :], w=["V"])
    RW = 528
    rt_ = [[(pg.sb("rp%d_%d" % (j, i), [64, RW]), "rp%d_%d" % (j, i)) for j in range(4)] for i in range(2)]
    rc_ = [[(pg.sb("rc%d_%d" % (j, i), [64, RW]), "rc%d_%d" % (j, i)) for j in range(2)] for i in range(2)]
    rn = {"n": 0, "c": 0}

    def rope(which, t0, n, dst_fn, dst_id):
        ci = rn["c"] % 2
        rn["c"] += 1
        (cc, cci), (ss, ssi) = rc_[ci]
        pg.dma("sp", cc[:, 0:n], cs[0, :, t0:t0 + n], w=[cci])
        pg.dma("sp", ss[:, 0:n], cs[1, :, t0:t0 + n], w=[ssi])
        for m in range(2):
            i = rn["n"] % 2
            rn["n"] += 1
            (x, xi), (xp, xpi), (a, ai), (b, bi) = rt_[i]
            pg.dma("sp", x[:, 0:n], qk[which, m, :, t0:t0 + n], w=[xi])
            pg.dma("sp", xp[:, 0:n], qkp[which, m, :, t0:t0 + n], w=[xpi])
            pg.tt("dve", a[:, 0:n], x[:, 0:n], cc[:, 0:n], ALU.mult, r=[xi, cci], w=[ai])
            pg.tt("dve", b[:, 0:n], xp[:, 0:n], ss[:, 0:n], ALU.mult, r=[xpi, ssi], w=[bi])
            pg.tt("dve", dst_fn(m), a[:, 0:n], b[:, 0:n], ALU.add, r=[ai, bi], w=[dst_id])

    for t in range(TOK // RW):
        rope(1, t * RW, RW, lambda m, t=t: KT[:, m, t * RW:(t + 1) * RW], "KT")

    QT = [(pg.sb("QT%d" % i, [64, 2, 512], BF16), "QT%d" % i) for i in range(2)]
    Pt = [(pg.sb("P%d" % i, [128, 512], BF16), "P%d" % i) for i in range(4)]
    p_s = [(pg.ps("pS%d" % i, [128, 512]), "pS%d" % i) for i in range(3)]
    pO, pOid = (pg.ps("pO", [128, 512]), "pO")
    pZ, pZid = (pg.ps("pZ", [128, 512]), "pZ")
    rz = (pg.sb("rz", [128, 512]), "rz")
    zc = (pg.sb("zc", [128, 512]), "zc")
    ot = [(pg.sb("ot%d" % i, [128, 512]), "ot%d" % i) for i in range(2)]
    o2 = (pg.sb("o2", [128, 512]), "o2")
    osq = (pg.sb("osq", [128, 512], BF16), "osq")
    ors = (pg.sb("ors", [128, 512]), "ors")
    qtiles = [(0, CTX, 0, 2)] + [(CTX + 512 * i, 512, 0, NCH) for i in range(16)]

    def qrope(qi):
        q0, nq, _, _ = qtiles[qi]
        Q, Qid = QT[qi % 2]
        rope(0, q0, nq, lambda m: Q[:, m, 0:nq], Qid)

    qrope(0)
    cnt = {"s": 0, "p": 0, "it": 0}
    for qi, (q0, nq, k0, k1) in enumerate(qtiles):
        Q, Qid = QT[qi % 2]
        o_, oid = ot[qi % 2]
        its = [(m, kc) for m in range(2) for kc in range(k0, k1)]

        def emitS(m, kc, Q=Q, Qid=Qid, nq=nq):
            pS, pSid = p_s[cnt["s"] % 3]
            cnt["s"] += 1
            P, Pid = Pt[cnt["p"] % 4]
            cnt["p"] += 1
            pg.mm(pS[:, 0:nq], KT[:, m, kc * 128:(kc + 1) * 128], Q[:, m, 0:nq], True, True, r=["KT", Qid], w=[pSid])
            pg.act(P[:, 0:nq], pS[:, 0:nq], AF.Exp, r=[pSid], w=[Pid], scale=0.125)
            return P, Pid

        pend = [emitS(*its[0]), emitS(*its[1])]
        for idx, (m, kc) in enumerate(its):
            P, Pid = pend.pop(0)
            if idx + 2 < len(its):
                pend.append(emitS(*its[idx + 2]))
            pg.mm(pO[:, 0:nq], V[:, kc, :], P[:, 0:nq], kc == k0, kc == k1 - 1, r=["V", Pid], w=[pOid])
            pg.mm(pZ[:, 0:nq], ones[:, :], P[:, 0:nq], kc == k0, kc == k1 - 1, r=["ones", Pid], w=[pZid])
            cnt["it"] += 1
            if cnt["it"] % 8 == 0:
                s5_tick()
            if m == 0 and kc == k0 + 8 and qi + 1 < len(qtiles):
                qrope(qi + 1)
            if kc == k1 - 1:
                pg.copy("act", zc[0][:, 0:nq], pZ[:, 0:nq], r=[pZid], w=[zc[1]])
                dst, dstid = (o_, oid) if m == 0 else o2
                pg.copy("dve", dst[:, 0:nq], pO[:, 0:nq], r=[pOid], w=[dstid])
                pg.recip(rz[0][:, 0:nq], zc[0][:, 0:nq], r=[zc[1]], w=[rz[1]])
                pg.tt("dve", dst[:, 0:nq], dst[:, 0:nq], rz[0][:, 0:nq], ALU.mult, r=[dstid, rz[1]], w=[dstid])
                if m == 1:
                    pg.stt(o_[:, 0:nq], o2[0][:, 0:nq], nlam[:, 0:1], o_[:, 0:nq], ALU.mult, ALU.add, r=[o2[1], oid, "nlam"], w=[oid])
        if qi == 0 and len(qtiles) > 1:
            qrope(1)
        pg.act(osq[0][:, 0:nq], o_[:, 0:nq], AF.Square, r=[oid], w=[osq[1]])
        pS, pSid = p_s[cnt["s"] % 3]
        cnt["s"] += 1
        pg.mm(pS[:, 0:nq], ones[:, :], osq[0][:, 0:nq], True, True, r=["ones", osq[1]], w=[pSid])
        pg.act(ors[0][:, 0:nq], pS[:, 0:nq], AF.Sqrt, r=[pSid], w=[ors[1]], bias=pg.eps_ap, scale=1.0 / 128)
        pg.recip(ors[0][:, 0:nq], ors[0][:, 0:nq], r=[ors[1]], w=[ors[1]])
        pg.tt("dve", o_[:, 0:nq], o_[:, 0:nq], ors[0][:, 0:nq], ALU.mult, r=[oid, ors[1]], w=[oid])
        pg.ts("dve", o_[:, 0:nq], o_[:, 0:nq], sgt[:, 0:1], ALU.mult, r=[oid, "sgt"], w=[oid])
        pg.dma("sp", oT[:, q0:q0 + nq], o_[:, 0:nq], r=[oid], final=True)
    while tick["k"] < NU + len(stages):
        s5_tick()
    return pg


def rope_tables():
    rows = SEQ // 64
    row = np.repeat(np.arange(rows, dtype=np.float32), 64)
    col = np.tile(np.arange(64, dtype=np.float32), rows)
    inv = (10000.0 ** (-np.arange(0, 32, 2, dtype=np.float32) / 32)).astype(np.float32)
    ang_r = row[:, None] * inv
    ang_c = col[:, None] * inv
    ang = np.concatenate([ang_r, ang_r, ang_c, ang_c], axis=-1)
    cos = np.cos(ang).astype(np.float32)
    sin = np.sin(ang).astype(np.float32)
    sgn = np.concatenate([-np.ones(16), np.ones(16), -np.ones(16), np.ones(16)]).astype(np.float32)
    cosT = np.concatenate([np.ones((64, CTX), np.float32), cos.T], axis=1)
    sinT = np.concatenate([np.zeros((64, CTX), np.float32), (sin * sgn[None, :]).T], axis=1)
    return np.ascontiguousarray(np.stack([cosT, sinT]))


ROPE_PERM = np.concatenate([np.arange(16, 32), np.arange(0, 16), np.arange(48, 64), np.arange(32, 48)])


def k2e_inputs(pT, j, P, cores=range(NCORES)):
    cs = rope_tables()
    c = np.ascontiguousarray
    maps = []
    for k in cores:
        q = np.stack([pT[m * 512 + k * 64:m * 512 + k * 64 + 64] for m in range(2)])
        kk = np.stack([pT[1024 + m * 512 + k * 64:1024 + m * 512 + k * 64 + 64] for m in range(2)])
        qk = np.stack([q, kk])
        qkp = qk[:, :, ROPE_PERM, :]
        v = pT[2048 + 128 * k:2048 + 128 * (k + 1)].T
        s = pT[3072 + 128 * k:3072 + 128 * (k + 1)]
        su = np.stack([s, flipseg(s, 1)])
        Bblk = np.zeros((128, 2, 4, 2, 128), np.float32)
        Cblk = np.zeros((128, 2, 4, 2, 128), np.float32)
        s5p = np.zeros((128, 2, 4, 3), np.float32)
        for d in range(2):
            for gp in range(4):
                for gi in range(2):
                    gl = 2 * gp + gi
                    g = 8 * k + gl
                    for comp, (bn, cn) in enumerate([("s5_b_re", "s5_c_re"), ("s5_b_im", "s5_c_im")]):
                        Bblk[gl * 16:(gl + 1) * 16, d, gp, comp, gi * 64:(gi + 1) * 64] = P[bn][j, d, g].T
                        Cblk[gi * 64:(gi + 1) * 64, d, gp, comp, gl * 16:(gl + 1) * 16] = P[cn][j, d, g].T
                    s5p[gi * 64:(gi + 1) * 64, d, gp, 0] = P["s5_lam_re"][j, d, g]
                    s5p[gi * 64:(gi + 1) * 64, d, gp, 1] = P["s5_lam_im"][j, d, g]
                    s5p[gi * 64:(gi + 1) * 64, d, gp, 2] = P["s5_log_dt"][j, d, g]
        dsk = P["s5_d"][j, 8 * k:8 * k + 8].reshape(128, 1)
        maps.append({"qk": c(qk), "qkp": c(qkp), "cs": cs, "v": c(v),
                     "lamb": c(P["diff_lam"][j].reshape(1, 256).repeat(128, 0)), "sg": c(P["diff_subln"][j].reshape(128, 1)),
                     "su": c(su), "Bblk": c(Bblk.reshape(128, -1)), "Cblk": c(Cblk.reshape(128, -1)),
                     "s5p": c(s5p.reshape(128, -1)), "dsk": c(dsk.astype(np.float32))})
    return maps


def run_k2e(pT, j, lam_init, P):
    res = _run(build_k2e(lam_init), k2e_inputs(pT, j, P))
    oT = np.concatenate([res[k]["oT"] for k in range(NCORES)], axis=0)
    sfT = np.concatenate([res[k]["sf"] for k in range(NCORES)], axis=0)
    srT = np.concatenate([flipseg(res[k]["sr"], 1) for k in range(NCORES)], axis=0)
    return oT, sfT, srT


def kernel(**inp):
    P = {k: np.asarray(v, dtype=np.float32) for k, v in inp.items()}
    mod = run_k0(P["c"][0], P["c_ctx"], P["w_mod"], P["b_mod"])
    hT = np.ascontiguousarray(np.concatenate([P["ctx"][0], P["x"][0]], axis=0).T)
    for i in range(4):
        j = i // 2
        g = P["norm_g"][i]
        ml, mc = mod[i, 0], mod[i, 1]
        if i % 2 == 0:
            lam_init = 0.8 - 0.6 * math.exp(-0.3 * i)
            pT = run_k1(hT, g[0], ml, mc, P["w_in_even"][j])
            oT, sfT, srT = run_k2e(pT, j, lam_init, P)
            bg = np.ascontiguousarray(P["s5_b_glu"][j].reshape(8, 128).T)
            h1T, u2T = run_k3a(True, hT, {"oT": oT, "sfT": sfT, "srT": srT}, g, ml, mc, P["w_out_even"][j],
                               {"wglu": P["s5_w_glu"][j], "bglu": bg})
        else:
            pT = run_k1(hT, g[0], ml, mc, P["w_in_odd"][j])
            zT = pT[0:4096]
            xbcaT = run_k2oa(pT[4096:4096 + 6144], P["conv_w"][j], P["conv_b"][j])
            yfT, yrT = run_k2ob(xbcaT, pT[10240:10368], P["ssd_dt_bias"][j], P["ssd_a_log"][j])
            dexp = np.repeat(P["ssd_d"][j], 64)
            nw = np.concatenate([P["ssd_norm_w"][j].reshape(32, 128).T, dexp.reshape(32, 128).T], axis=1)
            h1T, u2T = run_k3a(False, hT, {"yfT": yfT, "yrT": yrT, "xT": xbcaT[0:4096], "zT": zT}, g, ml, mc,
                               P["w_out_odd"][j], {"nw": np.ascontiguousarray(nw.astype(np.float32))})
        hT = run_k3b(u2T, h1T, g[3], ml, mc, P["w_up"][i], P["w_down"][i])
    return np.ascontiguousarray(hT[:, CTX:].T)[None].astype(np.float32)
```
